# Optimizing a Trainium2 kernel written in Bass

```python
import jax, jax.numpy as jnp
from jax import lax
import numpy as np

D_MODEL = 1024
BATCH = 2
SEQ = 8192
DEPTH = 1

POOL_WIDTH = D_MODEL
POOL_WINDOWS = (2, 4, 8, 16)
N_POOL_GROUPS = len(POOL_WINDOWS)
POOL_GROUP_DIM = POOL_WIDTH // N_POOL_GROUPS
MLSTM_WIDTH = D_MODEL
N_HEADS = 4
HEAD_DIM = MLSTM_WIDTH // N_HEADS
QKV_BLOCK = 4
N_QKV_BLOCKS = MLSTM_WIDTH // QKV_BLOCK
CONV_K = 4
CHUNK = 128
MIX_WIDTH = POOL_WIDTH + MLSTM_WIDTH
IN_WIDTH = 2 * POOL_WIDTH + 3 * MLSTM_WIDTH
EPS = 1e-6

kernel_name = "hybrid_pool_mlstm_parallel_heads"


def rmsnorm(x, w):
    xf = x.astype(jnp.float32)
    y = xf * lax.rsqrt(jnp.mean(xf * xf, axis=-1, keepdims=True) + EPS)
    return (y * w.astype(jnp.float32)).astype(x.dtype)


def pool_mixer(u, pool_w, pool_scale):
    B, S, _ = u.shape
    uf = u.astype(jnp.float32)
    cs = jnp.cumsum(uf, axis=1)
    pos = jnp.arange(1, S + 1, dtype=jnp.float32)
    outs = []
    for g, win in enumerate(POOL_WINDOWS):
        sl = slice(g * POOL_GROUP_DIM, (g + 1) * POOL_GROUP_DIM)
        seg = cs[..., sl]
        prev = jnp.pad(seg, ((0, 0), (win, 0), (0, 0)))[:, :S]
        cnt = jnp.minimum(pos, float(win))[None, :, None]
        outs.append((seg - prev) / cnt - uf[..., sl])
    d = jnp.stack(outs, axis=2).astype(u.dtype)
    y = jnp.einsum('bsgc,gcd->bsgd', d, pool_w).reshape(B, S, POOL_WIDTH)
    return y * pool_scale


def causal_conv(u, w, b):
    S = u.shape[1]
    up = jnp.pad(u, ((0, 0), (CONV_K - 1, 0), (0, 0)))
    y = sum(up[:, j:j + S] * w[j] for j in range(CONV_K))
    return y + b


def headwise(u, w):
    B, S, C = u.shape
    return jnp.einsum('bsnc,ncd->bsnd', u.reshape(B, S, N_QKV_BLOCKS, QKV_BLOCK), w).reshape(B, S, C)


def mlstm_chunkwise(q, k, v, i_pre, f_pre):
    B, H, S, DH = q.shape
    NC = S // CHUNK
    q = q.reshape(B, H, NC, CHUNK, DH)
    k = k.reshape(B, H, NC, CHUNK, DH) * (DH ** -0.5)
    v = v.reshape(B, H, NC, CHUNK, DH)
    log_f = jax.nn.log_sigmoid(f_pre).reshape(B, H, NC, CHUNK)
    ig = i_pre.reshape(B, H, NC, CHUNK)
    b = jnp.cumsum(log_f, axis=-1)
    g = b[..., -1]
    a = g[..., None] - b + ig
    a_max = jnp.max(a, axis=-1)

    def step(carry, xs):
        C, n, m = carry
        k_c, v_c, a_c, g_c, amax_c = xs
        m_new = jnp.maximum(g_c + m, amax_c)
        decay = jnp.exp(g_c + m - m_new)
        w = jnp.exp(a_c - m_new[..., None])
        C_new = decay[..., None, None] * C + jnp.einsum('bhl,bhle,bhld->bhed', w, v_c, k_c)
        n_new = decay[..., None] * n + jnp.einsum('bhl,bhld->bhd', w, k_c)
        return (C_new, n_new, m_new), (C, n, m)

    init = (jnp.zeros((B, H, DH, DH), jnp.float32),
            jnp.zeros((B, H, DH), jnp.float32),
            jnp.zeros((B, H), jnp.float32))
    xs = (jnp.moveaxis(k, 2, 0), jnp.moveaxis(v, 2, 0), jnp.moveaxis(a, 2, 0),
          jnp.moveaxis(g, 2, 0), jnp.moveaxis(a_max, 2, 0))
    _, (C_s, n_s, m_s) = lax.scan(step, init, xs)
    C_s = jnp.moveaxis(C_s, 0, 2)
    n_s = jnp.moveaxis(n_s, 0, 2)
    m_s = jnp.moveaxis(m_s, 0, 2)

    mask = jnp.tril(jnp.ones((CHUNK, CHUNK), dtype=bool))
    d_log = b[..., :, None] - b[..., None, :] + ig[..., None, :]
    d_log = jnp.where(mask, d_log, -jnp.inf)
    m_inter = b + m_s[..., None]
    m_t = jnp.maximum(jnp.max(d_log, axis=-1), m_inter)
    W = jnp.exp(d_log - m_t[..., None]) * jnp.einsum('bhcld,bhcsd->bhcls', q, k)
    inter = jnp.exp(m_inter - m_t)
    num = jnp.einsum('bhcls,bhcse->bhcle', W, v) + inter[..., None] * jnp.einsum('bhced,bhcld->bhcle', C_s, q)
    den = jnp.sum(W, axis=-1) + inter * jnp.einsum('bhcd,bhcld->bhcl', n_s, q)
    h = num / jnp.maximum(jnp.abs(den), jnp.exp(-m_t))[..., None]
    return h.reshape(B, H, S, DH)


def mlstm_branch(mx, mz, mo, conv_w, conv_b, w_q, w_k, w_v, w_if, b_if, mh_norm_w, m_skip):
    B, S, _ = mx.shape
    xc = jax.nn.silu(causal_conv(mx, conv_w, conv_b))
    q = headwise(xc, w_q)
    k = headwise(xc, w_k)
    v = headwise(mx, w_v)
    gates = (jnp.concatenate([q, k, v], axis=-1) @ w_if + b_if).astype(jnp.float32)
    i_pre = jnp.transpose(gates[..., :N_HEADS], (0, 2, 1))
    f_pre = jnp.transpose(gates[..., N_HEADS:], (0, 2, 1))

    def to_heads(t):
        return jnp.transpose(t.astype(jnp.float32).reshape(B, S, N_HEADS, HEAD_DIM), (0, 2, 1, 3))

    h = mlstm_chunkwise(to_heads(q), to_heads(k), to_heads(v), i_pre, f_pre)
    h = jnp.transpose(h, (0, 2, 1, 3))
    h = jax.nn.sigmoid(mo.astype(jnp.float32)).reshape(B, S, N_HEADS, HEAD_DIM) * h
    mu = jnp.mean(h, axis=-1, keepdims=True)
    var = jnp.mean(jnp.square(h - mu), axis=-1, keepdims=True)
    hn = ((h - mu) * lax.rsqrt(var + EPS)).reshape(B, S, MLSTM_WIDTH) * mh_norm_w.astype(jnp.float32)
    out = (hn.astype(mx.dtype) + m_skip * xc) * jax.nn.silu(mz)
    return out


def setup_inputs(seed: int = 0) -> dict:
    key = jax.random.key(seed)
    ks = jax.random.split(key, 20)
    f32 = jnp.float32
    nrm = lambda k, shape, s: jax.random.normal(k, shape, f32) * s
    x = jax.random.normal(ks[0], (BATCH, SEQ, D_MODEL), f32)
    norm_w = 1.0 + nrm(ks[1], (DEPTH, D_MODEL), 0.02)
    w_in = nrm(ks[2], (DEPTH, D_MODEL, IN_WIDTH), D_MODEL ** -0.5)
    pool_w = nrm(ks[3], (DEPTH, N_POOL_GROUPS, POOL_GROUP_DIM, POOL_GROUP_DIM), POOL_GROUP_DIM ** -0.5)
    pool_scale = 1.0 + nrm(ks[4], (DEPTH, POOL_WIDTH), 0.02)
    conv_w = nrm(ks[5], (DEPTH, CONV_K, MLSTM_WIDTH), CONV_K ** -0.5)
    conv_b = nrm(ks[6], (DEPTH, MLSTM_WIDTH), 0.02)
    w_q = nrm(ks[7], (DEPTH, N_QKV_BLOCKS, QKV_BLOCK, QKV_BLOCK), QKV_BLOCK ** -0.5)
    w_k = nrm(ks[8], (DEPTH, N_QKV_BLOCKS, QKV_BLOCK, QKV_BLOCK), QKV_BLOCK ** -0.5)
    w_v = nrm(ks[9], (DEPTH, N_QKV_BLOCKS, QKV_BLOCK, QKV_BLOCK), QKV_BLOCK ** -0.5)
    w_if = nrm(ks[10], (DEPTH, 3 * MLSTM_WIDTH, 2 * N_HEADS), 0.1 * (3 * MLSTM_WIDTH) ** -0.5)
    i_bias = nrm(ks[11], (DEPTH, N_HEADS), 0.1)
    f_bias = jnp.broadcast_to(jnp.linspace(3.0, 6.0, N_HEADS, dtype=f32), (DEPTH, N_HEADS)) + nrm(ks[12], (DEPTH, N_HEADS), 0.1)
    b_if = jnp.concatenate([i_bias, f_bias], axis=-1)
    mh_norm_w = 1.0 + nrm(ks[13], (DEPTH, MLSTM_WIDTH), 0.02)
    m_skip = 1.0 + nrm(ks[14], (DEPTH, MLSTM_WIDTH), 0.02)
    w_out = nrm(ks[15], (DEPTH, MIX_WIDTH, D_MODEL), MIX_WIDTH ** -0.5)
    final_norm_w = 1.0 + nrm(ks[16], (D_MODEL,), 0.02)
    return {"x": x, "norm_w": norm_w, "w_in": w_in, "pool_w": pool_w, "pool_scale": pool_scale,
            "conv_w": conv_w, "conv_b": conv_b, "w_q": w_q, "w_k": w_k, "w_v": w_v,
            "w_if": w_if, "b_if": b_if, "mh_norm_w": mh_norm_w, "m_skip": m_skip,
            "w_out": w_out, "final_norm_w": final_norm_w}


def reference(x, norm_w, w_in, pool_w, pool_scale, conv_w, conv_b, w_q, w_k, w_v,
              w_if, b_if, mh_norm_w, m_skip, w_out, final_norm_w):
    P, M = POOL_WIDTH, MLSTM_WIDTH
    for l in range(DEPTH):
        u = rmsnorm(x, norm_w[l])
        proj = u @ w_in[l]
        pool_x = proj[..., :P]
        pool_z = proj[..., P:2 * P]
        m_x = proj[..., 2 * P:2 * P + M]
        m_z = proj[..., 2 * P + M:2 * P + 2 * M]
        m_o = proj[..., 2 * P + 2 * M:]
        y_pool = pool_mixer(pool_x, pool_w[l], pool_scale[l]) * jax.nn.silu(pool_z)
        y_mlstm = mlstm_branch(m_x, m_z, m_o, conv_w[l], conv_b[l], w_q[l], w_k[l], w_v[l],
                               w_if[l], b_if[l], mh_norm_w[l], m_skip[l])
        mixed = jnp.concatenate([y_pool, y_mlstm], axis=-1)
        x = x + mixed @ w_out[l]
    return rmsnorm(x, final_norm_w)
```

```python
from contextlib import ExitStack
import numpy as np
import ml_dtypes
import concourse.bass as bass
import concourse.mybir as mybir
from concourse.bass_utils import run_bass_kernel_spmd

F32 = mybir.dt.float32
BF16 = mybir.dt.bfloat16
AF = mybir.ActivationFunctionType
ALU = mybir.AluOpType
AX = mybir.AxisListType

NCORES = 8
T = 2048
HALO = 16
TT = T + HALO
D = 1024
EPS = 1e-6
ENGS = ["pe", "act", "dve", "pool", "sp"]
STRICT_ENGS = ("dve", "act")


class Tk:
    __slots__ = ("w", "r", "excl")

    def __init__(self, excl=False):
        self.w = {}
        self.r = {}
        self.excl = excl


class Ins:
    __slots__ = ("eng", "stream", "fn", "deps", "needed", "seq", "gid", "isdma", "inc")


class Sched:
    def __init__(self):
        self.order = {e: [] for e in ENGS}
        self.dmacnt = {}
        self.gid = 0

    def _new(self, eng, stream, fn, isdma):
        i = Ins()
        i.eng, i.stream, i.fn, i.isdma = eng, stream, fn, isdma
        i.deps = set()
        i.needed = False
        i.seq = 0
        i.inc = 16
        self.gid += 1
        i.gid = self.gid
        return i

    def _link(self, ins, reads, writes):
        xr = [t for t in reads if t.excl and t not in writes]
        if xr:
            writes = list(writes) + xr
        raw, oth = set(), set()
        for t in reads:
            raw.update(t.w.values())
        for t in writes:
            oth.update(t.w.values())
            oth.update(t.r.values())
        deps = set()
        for d in raw:
            if d.stream == ins.stream and ins.stream == "pe":
                continue
            deps.add(d)
        for d in oth:
            if d.stream == ins.stream and not ins.isdma and ins.stream not in STRICT_ENGS:
                continue
            deps.add(d)
        deps.discard(ins)
        ins.deps = deps
        for d in deps:
            d.needed = True
        for t in reads:
            t.r[ins.stream] = ins
        for t in writes:
            t.w = {ins.stream: ins}
            t.r = {}
        self.order[ins.eng].append(ins)
        return ins

    def op(self, eng, fn, reads=(), writes=()):
        return self._link(self._new(eng, eng, fn, False), reads, writes)

    def dma(self, qeng, key, fn, reads=(), writes=(), inc=16):
        ins = self._new(qeng, ("d", key), fn, True)
        ins.inc = inc
        n = self.dmacnt.get(key, 0) + 1
        self.dmacnt[key] = n
        ins.seq = n
        return self._link(ins, reads, writes)

    def join(self, eng, tiles):
        return self._link(self._new(eng, eng, None, False), (), tiles)

    def emit(self, nc):
        for e in ENGS:
            n = 0
            for ins in self.order[e]:
                if ins.isdma:
                    continue
                if ins.needed and ins.fn is not None:
                    n += 1
                ins.seq = n
        with ExitStack() as st:
            sems = {}
            for e in ENGS:
                sems[e] = st.enter_context(nc.semaphore("s_" + e))
            for i, k in enumerate(self.dmacnt):
                sems[("d", k)] = st.enter_context(nc.semaphore("d%d" % i))
            block = st.enter_context(nc.Block())

            def run(eng_name, e):
                known = {}
                for ins in self.order[eng_name]:
                    for d in sorted(ins.deps, key=lambda z: z.gid):
                        val = d.seq * d.inc if d.isdma else d.seq
                        if val <= 0 or known.get(d.stream, 0) >= val:
                            continue
                        e.wait_ge(sems[d.stream], val)
                        known[d.stream] = val
                    if ins.fn is None:
                        continue
                    r = ins.fn(e)
                    if ins.isdma:
                        r.then_inc(sems[ins.stream], ins.inc)
                    elif ins.needed:
                        r.then_inc(sems[ins.stream], 1)

            @block.tensor
            def _(e):
                run("pe", e)

            @block.scalar
            def _(e):
                run("act", e)

            @block.vector
            def _(e):
                run("dve", e)

            @block.gpsimd
            def _(e):
                run("pool", e)

            @block.sync
            def _(e):
                run("sp", e)


DEBUG = False


def build_nc():
    nc = bass.Bass("TRN2", target_bir_lowering=False)
    S = Sched()

    def din(name, shape):
        return nc.dram_tensor(name, shape, F32, kind="ExternalInput").ap()

    x_d = din("x", [TT, D])
    win_d = din("w_in", [D, 5120])
    wout_d = din("w_out", [2048, D])
    poolw_d = din("pool_w", [1024, 256])
    wbd_d = din("wbd", [128, 3 * 8 * 128])
    wbdT_d = din("wbdT", [128, 3 * 8 * 128])
    wif_d = din("w_if", [128, 24 * 8])
    bifb_d = din("bifb", [128, 32])
    chv_d = din("chv", [128, 64])
    nw_d = din("normw_b", [128, D])
    fnw_d = din("fnw_b", [128, D])
    cst_d = din("cst", [128, 512])
    identb_d = nc.dram_tensor("identb_in", [128, 128], BF16, kind="ExternalInput").ap()
    invc_d = din("invcnt", [128, 64])
    brb_d = din("brb", [128, 16])
    cmask_d = din("cmask", [128, 4])
    out_d = nc.dram_tensor("out", [T, D], F32, kind="ExternalOutput").ap()
    acc_d = nc.dram_tensor("acc_scr", [T, D], F32).ap()
    cci_t = [nc.dram_tensor("cci%d" % h, [128, 516], F32) for h in range(4)]
    cco_t = [nc.dram_tensor("cco%d" % h, [512, 516], F32) for h in range(4)]

    with ExitStack() as st:
        def sb(name, shape, dt):
            return st.enter_context(nc.sbuf_tensor(name, shape, dt))

        def ps(name, shape):
            return st.enter_context(nc.psum_tensor(name, shape, F32))

        uT = sb("uT", [128, 8, TT], BF16)
        xcT = sb("xcT", [128, 8, T], BF16)
        vext = sb("vext", [128, 16, 4, 258], BF16)
        WS = [sb("ws%d" % i, [128, 8, 1024], BF16) for i in range(3)]
        poolw = sb("poolw", [128, 4, 2, 256], BF16)
        wbd = sb("wbd_sb", [128, 3, 8, 128], BF16)
        wif = sb("wif_sb", [128, 24, 8], BF16)
        bifb = sb("bifb_sb", [128, 32], F32)
        wg = sb("wg", [128, 16, 8], BF16)
        chv = sb("chv_sb", [128, 8, 8], F32)
        nwb = sb("nwb", [128, D], F32)
        identb = sb("identb", [128, 128], BF16)
        cstf = sb("cstf", [128, 4, 128], F32)
        invc = sb("invc", [128, 4, 16], F32)
        brb = sb("brb_sb", [128, 4, 4], F32)
        cmask = sb("cmask_sb", [128, 4], F32)
        Ax = sb("Ax", [128, 16, 4], F32)
        A2x = sb("A2x", [128, 16, 4], F32)
        EG = sb("EG", [128, 16, 4], F32)
        ENB = sb("ENB", [128, 16, 4], F32)
        Cst = sb("Cst", [128, 4, 2, 258], F32)
        Cbf = sb("Cbf", [128, 4, 2, 258], BF16)
        mxhalo = sb("mxhalo", [128, 8, 4], F32)
        pxhalo = sb("pxhalo", [128, 8, 16], F32)
        ngt = sb("ngt", [128, 4], F32)
        small = sb("small", [128, 16], F32)
        WN = 8170
        work = sb("work", [128, WN], F32)

        identf = cstf[:, 0, :]
        tri = cstf[:, 1, :]
        ones = cstf[:, 2, :]
        mask01 = cstf[:, 3, :]

        PA = ps("PA", [128, 1024])
        PB = ps("PB", [128, 1024])
        P4 = ps("P4", [128, 512])
        P5 = ps("P5", [128, 512])
        P67 = ps("P67", [128, 1024])
        P6 = P67[:, 0:512]
        P7 = P67[:, 512:1024]
        BK = [Tk(True) for _ in range(8)]

        UT = [Tk() for _ in range(17)]
        XC = [[Tk() for _ in range(8)] for _ in range(4)]
        VX = [Tk() for _ in range(16)]
        WST = [[Tk() for _ in range(8)] for _ in range(3)]
        CONST = Tk()
        NWB = Tk()
        GATE = [Tk() for _ in range(4)]
        CS = [Tk() for _ in range(4)]
        CB = [Tk() for _ in range(4)]
        MXH = [Tk() for _ in range(8)]
        PXH = [Tk() for _ in range(8)]
        NGT = Tk()
        OUT = Tk()
        ACC = [Tk() for _ in range(16)]
        CCI = [Tk() for _ in range(4)]
        CCO = [Tk() for _ in range(4)]

        wk = [0]

        def take(n):
            a = work[:, wk[0]:wk[0] + n]
            wk[0] += n
            assert wk[0] <= WN, wk[0]
            return a

        xt = [take(1024) for _ in range(2)]
        sqj = take(512).bitcast(BF16)
        ub = [take(512).bitcast(BF16) for _ in range(2)]
        XT = [Tk(), Tk()]
        SQ = Tk()
        UB = [Tk(), Tk()]
        SS = [Tk(), Tk()]
        def s0_load(k, ti):
            rows = 16 if ti < 0 else 128
            r0 = 0 if ti < 0 else HALO + 128 * ti
            sl = k % 2
            S.dma("sp", "x%d" % sl, lambda e: e.dma_start(out=xt[sl][:rows, :], in_=x_d[r0:r0 + rows, :]), writes=[XT[sl]])

        s0_load(0, -1)
        s0_load(1, 0)
        def ld(key, q, out, in_, w):
            ins = S.dma(q, key, lambda e: e.dma_start(out=out, in_=in_), writes=[] if w and w[0] is CONST else w)
            if w and w[0] is CONST:
                CONST.w[ins.stream] = ins

        ld("c0", "sp", nwb[:, :], nw_d, [NWB])
        ld("c1", "sp", identb[:, :], identb_d, [CONST])
        ld("c2", "sp", cstf[:, :, :], cst_d.rearrange("p (a b) -> p a b", a=4), [CONST])
        ld("c3", "sp", chv[:, :, :], chv_d.rearrange("p (a b) -> p a b", a=8), [CONST])
        ld("c4", "sp", bifb[:, :], bifb_d, [CONST])
        ld("c5", "sp", invc[:, :, :], invc_d.rearrange("p (a b) -> p a b", a=4), [CONST])
        ld("c6", "sp", brb[:, :, :], brb_d.rearrange("p (a b) -> p a b", a=4), [CONST])
        ld("c7", "sp", cmask[:, :], cmask_d, [CONST])
        def load_w(slot, src2d, after=()):
            for kt in range(8):
                S.dma("pool", "w%d_%d" % (slot, kt),
                      lambda e, kt=kt: e.dma_start(out=WS[slot][:, kt, :], in_=src2d[kt * 128:(kt + 1) * 128, :]),
                      reads=list(after), writes=[WST[slot][kt]])

        load_w(0, win_d[:, 2048:3072])
        WBD = Tk()
        WIF = Tk()
        POOLW = Tk()
        ld("c8", "pool", wbd[:, :, :, :], wbd_d.rearrange("p (w c n) -> p w c n", w=3, c=8), [WBD])
        ld("c9", "pool", wif[:, :, :], wif_d.rearrange("p (k n) -> p k n", k=24), [WIF])

        wbdT = work[:, 5808:5808 + 1536].bitcast(BF16).rearrange("p (w c n) -> p w c n", w=3, c=8)
        dg = work[:, 3760:3760 + 2048].bitcast(BF16).rearrange("p (c j n) -> p c j n", c=8, j=4)
        WBT = Tk()
        DGT = Tk()
        WG = Tk()
        ld("c11", "pool", wbdT, wbdT_d.rearrange("p (w c n) -> p w c n", w=3, c=8), [WBT])
        ld("c10", "pool", poolw[:, :, :, :], poolw_d.rearrange("(g c p) n -> p g c n", g=4, c=2), [POOLW])
        S.op("pool", lambda e: e.memset(Cst[:, :, :, :], 0.0), writes=CS)
        S.op("pool", lambda e: e.memset(vext[:, :, :, 256:258], 1.0), writes=VX)
        S.op("pool", lambda e: e.memset(ngt[:, :], 0.0), writes=[NGT])

        PTbs = [P6.bitcast(BF16), P7.bitcast(BF16)]

        def s0_front(k, ti):
            rows = 16 if ti < 0 else 128
            r0 = 0 if ti < 0 else HALO + 128 * ti
            sl = k % 2
            PTb = PTbs[sl]
            ss = small[:, sl * 2:sl * 2 + 1]
            rs = small[:, sl * 2 + 1:sl * 2 + 2]
            if k >= 2:
                s0_load(k, ti)
            S.op("act", lambda e: e.activation(out=sqj[:rows, :], in_=xt[sl][:rows, :], func=AF.Square, accum_out=ss[:rows, :]),
                 reads=[XT[sl]], writes=[SQ, SS[sl]])
            S.op("act", lambda e: e.activation(out=rs[:rows, :], in_=ss[:rows, :], func=AF.Sqrt, scale=1.0 / D, bias=EPS),
                 reads=[SS[sl]], writes=[SS[sl]])
            S.op("dve", lambda e: e.reciprocal(out=rs[:rows, :], in_=rs[:rows, :]), reads=[SS[sl]], writes=[SS[sl]])
            S.op("dve", lambda e: e.scalar_tensor_tensor(
                out=ub[sl][:rows, :], in0=xt[sl][:rows, :], scalar=rs[:rows, :], in1=nwb[:rows, :],
                op0=ALU.mult, op1=ALU.mult), reads=[XT[sl], SS[sl], NWB], writes=[UB[sl]])
            for kt in range(8):
                S.op("pe", lambda e, kt=kt: e.transpose(
                    out=PTb[:, kt * 128:kt * 128 + rows], in_=ub[sl][:rows, kt * 128:(kt + 1) * 128],
                    identity=identb[:rows, :rows]), reads=[UB[sl], CONST], writes=[BK[6 + sl]])

        def s0_back(k, ti):
            rows = 16 if ti < 0 else 128
            r0 = 0 if ti < 0 else HALO + 128 * ti
            sl = k % 2
            PTb = PTbs[sl]
            S.op("act", lambda e: e.activation(
                out=uT[:, :, r0:r0 + rows], in_=PTb.rearrange("p (k t) -> p k t", k=8)[:, :, 0:rows], func=AF.Copy),
                reads=[BK[6 + sl]], writes=[UT[k]])

        tiles = list(enumerate(range(-1, 16)))
        s0_front(*tiles[0])
        for i in range(len(tiles)):
            if i + 1 < len(tiles):
                s0_front(*tiles[i + 1])
            s0_back(*tiles[i])

        for nt in range(8):
            S.op("pe", lambda e, nt=nt: e.matmul(P6[:, nt * 8:(nt + 1) * 8], lhsT=wbdT[:, 0, nt, :], rhs=wif[:, nt, :],
                                                 start=True, stop=False), reads=[WBT, WIF], writes=[BK[6]])
            S.op("pe", lambda e, nt=nt: e.matmul(P6[:, nt * 8:(nt + 1) * 8], lhsT=wbdT[:, 1, nt, :], rhs=wif[:, 8 + nt, :],
                                                 start=False, stop=True), reads=[WBT, WIF], writes=[BK[6]])
            S.op("pe", lambda e, nt=nt: e.matmul(P6[:, 64 + nt * 8:64 + (nt + 1) * 8], lhsT=wbdT[:, 2, nt, :], rhs=wif[:, 16 + nt, :],
                                                 start=True, stop=True), reads=[WBT, WIF], writes=[BK[6]])
        S.op("act", lambda e: e.activation(out=wg[:, :, :], in_=P6[:, 0:128].rearrange("p (k n) -> p k n", k=16), func=AF.Copy),
             reads=[BK[6]], writes=[WG])
        for nt in range(8):
            for j in range(4):
                S.op("dve", lambda e, nt=nt, j=j: e.tensor_scalar(out=dg[:, nt, j, :], in0=identb[:, :], scalar1=chv[:, nt, j:j + 1],
                                                                  scalar2=None, op0=ALU.mult), reads=[CONST], writes=[DGT])
        load_w(1, win_d[:, 0:1024], after=[UT[16]])
        load_w(2, win_d[:, 1024:2048], after=[UT[16]])
        ld("c0", "sp", nwb[:, :], fnw_d, [NWB])

        STG0 = XT + [SQ] + UB
        wk[0] = 0
        mxb = take(2064).bitcast(BF16).rearrange("p (k t) -> p k t", k=8)
        k2 = [work[:, 5808 + 512 * i_:5808 + 512 * (i_ + 1)].bitcast(BF16) for i_ in range(2)]
        gtm = take(32)
        ef = take(16)
        lfn = take(16)
        nbg = take(32)
        t1 = take(16)
        t2 = take(16)
        assert wk[0] <= 3760
        MXB = [Tk() for _ in range(8)]
        K2 = [[Tk(), Tk()], [Tk(), Tk()]]
        GTM = Tk()
        STGA = MXB + [GTM]
        for t in STGA:
            for o in STG0:
                for s_, i_ in list(o.w.items()) + list(o.r.items()):
                    if s_ not in t.w or t.w[s_].gid < i_.gid:
                        t.w[s_] = i_

        STGA = STGA + [DGT, WBT]
        for t in K2[0] + K2[1]:
            for s_, i_ in list(WBT.w.items()) + list(WBT.r.items()):
                if s_ not in t.w or t.w[s_].gid < i_.gid:
                    t.w[s_] = i_
        PAh = [PA[:, 0:512], PA[:, 512:1024]]
        PBh = [PB[:, 0:512], PB[:, 512:1024]]

        def proj_fm(slot, nt, c0, n, hb, bank_base, pview):
            uts = [0] if c0 < HALO else list(range((c0 - HALO) // 128 + 1, (c0 + n - 1 - HALO) // 128 + 2))
            utk = [UT[i] for i in uts]
            for kt in range(8):
                S.op("pe", lambda e, kt=kt: e.matmul(
                    pview[:, 0:n], lhsT=WS[slot][:, kt, nt * 128:(nt + 1) * 128], rhs=uT[:, kt, c0:c0 + n],
                    start=(kt == 0), stop=(kt == 7)), reads=[WST[slot][kt]] + utk, writes=[BK[bank_base + hb]])

        def A_proj(b, nt, part=None):
            hb = nt % 2
            if b < 0:
                proj_fm(0, nt, 0, HALO, hb, 0, PAh[hb])
                S.op("act", lambda e: e.activation(out=mxb[:, nt, 1:4], in_=PAh[hb][:, 13:16], func=AF.Copy),
                     reads=[BK[hb]], writes=[MXB[nt]])
                return
            c0 = HALO + 512 * b
            if part in (None, "p"):
                proj_fm(0, nt, c0, 512, hb, 0, PAh[hb])
                if b > 0:
                    S.op("dve", lambda e: e.tensor_copy(out=mxb[:, nt, 1:4], in_=mxb[:, nt, 513:516]),
                         reads=[MXB[nt]], writes=[MXB[nt]])
                S.op("act", lambda e: e.activation(out=mxb[:, nt, 4:516], in_=PAh[hb][:, 0:512], func=AF.Copy),
                     reads=[BK[hb]], writes=[MXB[nt]])
            if part in (None, "c"):
                for j in range(4):
                    S.op("pe", lambda e, j=j: e.matmul(PBh[hb][:, 0:512], lhsT=dg[:, nt, j, :], rhs=mxb[:, nt, 1 + j:1 + j + 512],
                                                       start=(j == 0), stop=(j == 3)), reads=[MXB[nt], DGT], writes=[BK[2 + hb]])
                S.op("act", lambda e: e.activation(out=xcT[:, nt, b * 512:(b + 1) * 512], in_=PBh[hb][:, 0:512], func=AF.Silu,
                                                   bias=chv[:, nt, 4:5]), reads=[BK[2 + hb], CONST], writes=[XC[b][nt]])

        def A_vg(b):
            for j in range(4):
                c = 4 * b + j
                tc0 = b * 512 + j * 128
                pvv, bk0 = (PB, 2) if j % 2 == 0 else (PA, 0)
                for ct in range(8):
                    S.op("pe", lambda e, ct=ct, j=j, pvv=pvv: e.matmul(
                        pvv[:, ct * 128:(ct + 1) * 128], lhsT=mxb[:, ct, 4 + j * 128:4 + (j + 1) * 128], rhs=wbd[:, 2, ct, :],
                        start=True, stop=True), reads=[MXB[ct], WBD], writes=[BK[bk0 + ct // 4]])
                S.op("act", lambda e, c=c, pvv=pvv: e.activation(out=vext[:, c, :, 0:256],
                                                               in_=pvv[:, :].rearrange("p (h e) -> p h e", h=4), func=AF.Copy),
                     reads=[BK[bk0], BK[bk0 + 1]], writes=[VX[c]])
                for k in range(16):
                    if k < 8:
                        lh, rd = xcT[:, k, tc0:tc0 + 128], XC[b][k]
                    else:
                        lh, rd = mxb[:, k - 8, 4 + j * 128:4 + (j + 1) * 128], MXB[k - 8]
                    S.op("pe", lambda e, k=k, lh=lh, j=j: e.matmul(P4[:, j * 8:(j + 1) * 8], lhsT=lh, rhs=wg[:, k, :],
                                                                   start=(k == 0), stop=(k == 15)), reads=[rd, WG], writes=[BK[4]])

        def A_gmath(b):
            S.op("dve", lambda e: e.tensor_tensor(out=gtm[:, :], in0=P4[:, 0:32], in1=bifb[:, :], op=ALU.add),
                 reads=[BK[4], CONST], writes=[GTM])
            g3 = gtm.rearrange("p (a b) -> p a b", a=4)
            ef3 = ef.rearrange("p (a b) -> p a b", a=4)
            lf3 = lfn.rearrange("p (a b) -> p a b", a=4)
            t13 = t1.rearrange("p (a b) -> p a b", a=4)
            t23 = t2.rearrange("p (a b) -> p a b", a=4)
            nb3 = nbg[:, 0:16].rearrange("p (a b) -> p a b", a=4)
            ngg3 = nbg[:, 16:32].rearrange("p (a b) -> p a b", a=4)
            S.op("act", lambda e: e.activation(out=ef3, in_=g3[:, :, 4:8], func=AF.Exp, scale=-1.0), reads=[GTM], writes=[GTM])
            S.op("act", lambda e: e.activation(out=lf3, in_=ef3, func=AF.Ln, bias=1.0), reads=[GTM], writes=[GTM])
            S.op("pe", lambda e: e.matmul(P4[:, 64:80], lhsT=tri, rhs=lfn[:, :], start=True, stop=True),
                 reads=[GTM, CONST], writes=[BK[4]])
            S.op("pe", lambda e: e.matmul(P4[:, 80:96], lhsT=ones, rhs=lfn[:, :], start=True, stop=True),
                 reads=[GTM, CONST], writes=[BK[4]])
            S.op("dve", lambda e: e.tensor_copy(out=nbg[:, :], in_=P4[:, 64:96]), reads=[BK[4]], writes=[GTM])
            S.op("dve", lambda e: e.tensor_tensor(out=t13, in0=g3[:, :, 0:4], in1=nb3, op=ALU.add), reads=[GTM], writes=[GTM])
            S.op("dve", lambda e: e.tensor_tensor(out=t23, in0=t13, in1=ngg3, op=ALU.subtract), reads=[GTM], writes=[GTM])
            bs = slice(4 * b, 4 * b + 4)
            S.op("act", lambda e: e.activation(out=Ax[:, bs, :], in_=t13, func=AF.Exp), reads=[GTM], writes=[GATE[b]])
            S.op("act", lambda e: e.activation(out=A2x[:, bs, :], in_=t23, func=AF.Exp), reads=[GTM], writes=[GATE[b]])
            S.op("act", lambda e: e.activation(out=EG[:, bs, :], in_=ngg3, func=AF.Exp, scale=-1.0), reads=[GTM], writes=[GATE[b]])
            S.op("act", lambda e: e.activation(out=ENB[:, bs, :], in_=nb3, func=AF.Exp, scale=2.0), reads=[GTM], writes=[GATE[b]])
            S.op("dve", lambda e: e.tensor_scalar(out=Ax[:, bs, :], in0=Ax[:, bs, :], scalar1=0.0625, scalar2=None, op0=ALU.mult),
                 reads=[GATE[b]], writes=[GATE[b]])
            S.op("dve", lambda e: e.tensor_scalar(out=A2x[:, bs, :], in0=A2x[:, bs, :], scalar1=0.0625, scalar2=None, op0=ALU.mult),
                 reads=[GATE[b]], writes=[GATE[b]])
            for j in range(4):
                S.op("dve", lambda e, j=j: e.tensor_tensor(out=ngt[:, :], in0=ngt[:, :], in1=ngg3[:, j, :], op=ALU.add),
                     reads=[GTM, NGT], writes=[NGT])

        def A_scan_k(b, j):
            c = 4 * b + j
            ks = c % 2
            tc0 = b * 512 + j * 128
            for pair in range(2):
                pk, bk = (P5, 5) if pair == 0 else (P4, 4)
                for q4 in range(4):
                    ct = 4 * pair + q4
                    S.op("pe", lambda e, ct=ct, q4=q4, pk=pk: e.matmul(
                        pk[:, q4 * 128:(q4 + 1) * 128], lhsT=xcT[:, ct, tc0:tc0 + 128], rhs=wbd[:, 1, ct, :],
                        start=True, stop=True), reads=[XC[b][ct], WBD], writes=[BK[bk]])
                for hh in range(2):
                    h = 2 * pair + hh
                    S.op("act", lambda e, h=h, hh=hh, pk=pk: e.activation(
                        out=k2[ks][:, h * 256:(h + 1) * 256], in_=pk[:, hh * 256:(hh + 1) * 256], func=AF.Identity,
                        scale=A2x[:, c, h:h + 1]), reads=[BK[bk], GATE[b]], writes=[K2[ks][pair]])

        def A_scan_u(b, j, h):
            c = 4 * b + j
            ks = c % 2
            pair = h // 2
            for dt in range(2):
                pu = P6 if dt == 0 else P7
                S.op("pe", lambda e, dt=dt, pu=pu: e.matmul(
                    pu[:, 0:257], lhsT=k2[ks][:, h * 256 + dt * 128:h * 256 + dt * 128 + 128],
                    rhs=vext[:, c, h, 0:257], start=True, stop=True), reads=[K2[ks][pair], VX[c]], writes=[BK[6 + dt]])
            S.op("dve", lambda e: e.scalar_tensor_tensor(
                out=Cst[:, h, :, 0:257], in0=Cst[:, h, :, 0:257], scalar=EG[:, c, h:h + 1],
                in1=P67[:, :].rearrange("p (a b) -> p a b", a=2)[:, :, 0:257],
                op0=ALU.mult, op1=ALU.add), reads=[CS[h], GATE[b], BK[6], BK[7]], writes=[CS[h]])

        def A_scan(b, j):
            A_scan_k(b, j)
            for h in range(4):
                A_scan_u(b, j, h)

        for nt in range(8):
            A_proj(-1, nt)
        for i in range(4):
            A_proj(0, 2 * i, "p")
            A_proj(0, 2 * i + 1, "p")
            A_proj(0, 2 * i, "c")
            A_proj(0, 2 * i + 1, "c")
        A_vg(0)
        deferred = []
        for b in range(4):
            for i in range(4):
                if b < 3:
                    if i == 0:
                        A_proj(b + 1, 0, "p")
                        A_proj(b + 1, 1, "p")
                        A_gmath(b)
                        A_scan_k(b, 0)
                        A_scan_u(b, 0, 0)
                        A_proj(b + 1, 0, "c")
                        A_scan_u(b, 0, 1)
                        A_proj(b + 1, 1, "c")
                        A_scan_u(b, 0, 2)
                        A_scan_u(b, 0, 3)
                    else:
                        A_scan_k(b, i)
                        A_scan_u(b, i, 0)
                        A_proj(b + 1, 2 * i, "p")
                        A_scan_u(b, i, 1)
                        A_proj(b + 1, 2 * i + 1, "p")
                        A_scan_u(b, i, 2)
                        A_proj(b + 1, 2 * i, "c")
                        A_scan_u(b, i, 3)
                        A_proj(b + 1, 2 * i + 1, "c")
                else:
                    if i == 0:
                        A_gmath(b)
                    deferred.append(lambda i=i: A_scan(3, i))
            if b < 3:
                A_vg(b + 1)

        load_w(0, wout_d[0:1024, :])

        def publish():
          for h in range(4):
              for dt in range(2):
                  S.op("dve", lambda e, h=h, dt=dt: e.tensor_copy(out=Cst[:, h, dt, 257:258], in_=ngt[:, h:h + 1]),
                       reads=[NGT, CS[h]], writes=[CS[h]])
              S.dma("sp", "cci%d" % h, lambda e, h=h: e.dma_start(out=cci_t[h].ap(), in_=Cst[:, h, :, :].rearrange("p a b -> p (a b)")),
                    reads=[CS[h]], writes=[CCI[h]])
              S.dma("pool", "cc%d" % h, lambda e, h=h: e.collective_compute(
                  "AllGather", ALU.bypass, replica_groups=[[0, 1, 2, 3], [4, 5, 6, 7]],
                  ins=[cci_t[h].ap().opt()], outs=[cco_t[h].ap().opt()]), reads=[CCI[h]], writes=[CCO[h]], inc=1)

        wk[0] = 0
        pxe = [take(528) for _ in range(2)]
        tA1 = take(528)
        tA = [tA1, tA1]
        tB1 = take(528)
        tB = [tB1, tB1]
        szp = [take(256).bitcast(BF16) for _ in range(2)]
        dTt2 = [take(512).bitcast(BF16).rearrange("p (k t) -> p k t", k=2) for _ in range(2)]
        ypT = take(2048).bitcast(BF16).rearrange("p (k t) -> p k t", k=8)
        xacc = [take(512) for _ in range(4)]
        tmp16 = take(16)
        PXE = [Tk(), Tk()]
        TA1 = Tk()
        TA = [TA1, TA1]
        TB1 = Tk()
        TB = [TB1, TB1]
        SZP = [Tk(), Tk()]
        DT2 = [[Tk(), Tk()], [Tk(), Tk()]]
        YP = [Tk() for _ in range(8)]
        XACC = [Tk() for _ in range(4)]
        STGB = PXE + [TA1, TB1] + SZP + DT2[0] + DT2[1] + YP + XACC
        for t in XACC:
            for o in K2[0] + K2[1]:
                for s_, i_ in list(o.w.items()) + list(o.r.items()):
                    if s_ not in t.w or t.w[s_].gid < i_.gid:
                        t.w[s_] = i_
        for t in STGB:
            for o in STGA:
                for s_, i_ in list(o.w.items()) + list(o.r.items()):
                    if s_ not in t.w or t.w[s_].gid < i_.gid:
                        t.w[s_] = i_

        for nt in range(8):
            proj_fm(1, nt, 0, HALO, nt % 2, 0, PAh[nt % 2])
            S.op("act", lambda e, nt=nt: e.activation(out=pxhalo[:, nt, :], in_=PAh[nt % 2][:, 0:16], func=AF.Copy),
                 reads=[BK[nt % 2]], writes=[PXH[nt]])

        def B_px(b, g, n2):
            dTt, DT = dTt2[g % 2], DT2[g % 2]
            c0 = HALO + 512 * b
            win = 2 ** (g + 1)
            nt = 2 * g + n2
            proj_fm(1, nt, c0, 512, n2, 0, PAh[n2])
            S.op("act", lambda e: e.activation(out=pxe[n2][:, 16:528], in_=PAh[n2][:, 0:512], func=AF.Copy),
                 reads=[BK[n2]], writes=[PXE[n2]])
            S.op("pool", lambda e: e.tensor_copy(out=pxe[n2][:, 0:16], in_=pxhalo[:, nt, :]),
                 reads=[PXH[nt], PXE[n2]], writes=[PXE[n2]])
            S.op("pool", lambda e: e.tensor_copy(out=pxhalo[:, nt, :], in_=pxe[n2][:, 512:528]),
                 reads=[PXE[n2]], writes=[PXH[nt]])
            cur, curT = pxe[n2], PXE[n2]
            v = 0
            for lev in range(g + 1):
                sh = 2 ** lev
                nv = v + sh
                dst, dstT = (tA[n2], TA[n2]) if lev % 2 == 0 else (tB[n2], TB[n2])
                eng = "dve" if lev % 2 == 0 else "pool"
                S.op(eng, lambda e, dst=dst, cur=cur, nv=nv, sh=sh: e.tensor_tensor(
                    out=dst[:, nv:528], in0=cur[:, nv:528], in1=cur[:, nv - sh:528 - sh], op=ALU.add),
                    reads=[curT], writes=[dstT])
                cur, curT, v = dst, dstT, nv
            S.op("dve", lambda e, cur=cur: e.scalar_tensor_tensor(
                out=dTt[:, n2, :], in0=cur[:, 16:528], scalar=1.0 / win, in1=pxe[n2][:, 16:528],
                op0=ALU.mult, op1=ALU.subtract), reads=[curT, PXE[n2]], writes=[DT[n2]])
            if b == 0:
                S.op("dve", lambda e, cur=cur: e.tensor_tensor(out=tmp16[:, :], in0=cur[:, 16:32], in1=invc[:, g, :], op=ALU.mult),
                     reads=[curT, CONST, DT[n2]], writes=[DT[n2]])
                S.op("dve", lambda e: e.tensor_tensor(out=dTt[:, n2, 0:16], in0=tmp16[:, :], in1=pxe[n2][:, 16:32], op=ALU.subtract),
                     reads=[PXE[n2], DT[n2]], writes=[DT[n2]])

        def B_pz(b, g, cot):
            c0 = HALO + 512 * b
            nt = 2 * g + cot
            proj_fm(2, nt, c0, 512, cot, 2, PBh[cot])
            S.op("act", lambda e: e.activation(out=szp[cot][:, :], in_=PBh[cot][:, 0:512], func=AF.Silu),
                 reads=[BK[2 + cot]], writes=[SZP[cot]])

        def B_pw(b, g, cot):
            dTt, DT = dTt2[g % 2], DT2[g % 2]
            nt = 2 * g + cot
            pv = P4 if cot == 0 else P5
            for cit in range(2):
                S.op("pe", lambda e, cit=cit: e.matmul(
                    pv[:, 0:512], lhsT=poolw[:, g, cit, cot * 128:(cot + 1) * 128], rhs=dTt[:, cit, :],
                    start=(cit == 0), stop=(cit == 1)), reads=[DT[0], DT[1], POOLW], writes=[BK[4 + cot]])
            S.op("dve", lambda e: e.scalar_tensor_tensor(
                out=ypT[:, nt, :], in0=pv[:, 0:512], scalar=chv[:, nt, 5:6], in1=szp[cot][:, :],
                op0=ALU.mult, op1=ALU.mult), reads=[BK[4 + cot], SZP[cot], CONST], writes=[YP[nt]])

        XB = xacc + [pxe[0][:, 0:512], pxe[1][:, 0:512], tA1[:, 0:512], tB1[:, 0:512]]
        XBT = XACC + [PXE[0], PXE[1], TA1, TB1]

        def B_ld(b, u, extra=()):
            j, half = divmod(u, 2)
            c = 4 * b + j
            xb = u if b == 3 else u % 4
            S.dma("sp", "xa%d" % xb, lambda e: e.dma_start(
                out=XB[xb][:, :], in_=x_d[HALO + c * 128:HALO + (c + 1) * 128, half * 512:(half + 1) * 512]),
                writes=[XBT[xb]] + list(extra))

        def B_out(b, u):
            j, half = divmod(u, 2)
            c = 4 * b + j
            xb = u if b == 3 else u % 4
            pv = P6 if half == 0 else P7
            for kt in range(8):
                S.op("pe", lambda e, kt=kt: e.matmul(
                    pv[:, 0:512], lhsT=ypT[:, kt, j * 128:(j + 1) * 128],
                    rhs=WS[0][:, kt, half * 512:(half + 1) * 512], start=(kt == 0), stop=(kt == 7)),
                    reads=[YP[kt], WST[0][kt]], writes=[BK[6 + half]])
            S.op("dve", lambda e: e.tensor_tensor(out=XB[xb][:, :], in0=pv[:, 0:512], in1=XB[xb][:, :], op=ALU.add),
                 reads=[BK[6 + half], XBT[xb]], writes=[XBT[xb]])
            S.dma("sp", "acc%d" % xb, lambda e: e.dma_start(
                out=acc_d[c * 128:(c + 1) * 128, half * 512:(half + 1) * 512], in_=XB[xb][:, :]),
                reads=[XBT[xb]], writes=[ACC[c]])
            if b < 3 and u + 4 < 8:
                B_ld(b, u + 4)

        seq = [(b, g) for b in range(4) for g in range(4)]
        B_px(0, 0, 0)
        B_px(0, 0, 1)
        for i, (b, g) in enumerate(seq):
            nxt = seq[i + 1] if i + 1 < len(seq) else None
            if g == 3 and b > 0:
                for u in range(4):
                    B_ld(b, u)
            B_pz(b, g, 0)
            if nxt:
                B_px(nxt[0], nxt[1], 0)
            B_pz(b, g, 1)
            if nxt:
                B_px(nxt[0], nxt[1], 1)
            if i == len(seq) - 1:
                for u in range(4, 8):
                    B_ld(3, u)
                load_w(1, win_d[:, 3072:4096])
                load_w(2, win_d[:, 4096:5120])
            if g == 0 and b > 0:
                for u in range(8):
                    B_out(b - 1, u)
            B_pw(b, g, 0)
            B_pw(b, g, 1)
            if deferred:
                deferred.pop(0)()
                if not deferred:
                    publish()
                    for u in range(4):
                        B_ld(0, u, extra=K2[0] + K2[1])
        for u in range(8):
            B_out(3, u)

        wk[0] = 0
        gathA = take(2064).rearrange("p (r n) -> p r n", r=4)
        gathB = work[:, 2064:4128].rearrange("p (r n) -> p r n", r=4)
        qT = [take(512).bitcast(BF16).rearrange("p (k t) -> p k t", k=8) for _ in range(2)]
        kT = [take(512).bitcast(BF16).rearrange("p (k t) -> p k t", k=8) for _ in range(2)]
        take(16)
        szT = take(1024).bitcast(BF16).rearrange("p (k t) -> p k t", k=8)
        so = [take(512).bitcast(BF16) for _ in range(2)]
        k2c1 = take(512).bitcast(BF16)
        k2c = [k2c1, k2c1]
        Wt = [take(64).bitcast(BF16) for _ in range(2)]
        hg = [take(256) for _ in range(2)]
        hn = [take(128).bitcast(BF16) for _ in range(2)]
        tt = [take(128) for _ in range(2)]
        cf = take(64)
        stt = take(64)
        wk_end = wk[0]
        wk[0] = 0
        ymT = [take(512).bitcast(BF16).rearrange("p (k t) -> p k t", k=8) for _ in range(2)]
        res = [take(512) for _ in range(2)]
        assert wk[0] <= 2064
        wk[0] = wk_end
        GA = Tk()
        GA2 = Tk()
        QT = [[Tk() for _ in range(2)] for _ in range(2)]
        KT = [[Tk() for _ in range(2)] for _ in range(2)]
        SZ = [Tk() for _ in range(4)]
        SO = [[Tk(), Tk()], [Tk(), Tk()]]
        K2C1 = [Tk(), Tk()]
        K2C = [K2C1, K2C1]
        WT = [Tk(), Tk()]
        HG = [Tk(), Tk()]
        HN = [Tk(), Tk()]
        TTt = [Tk(), Tk()]
        YM = [[Tk() for _ in range(4)] for _ in range(2)]
        RES = [Tk(), Tk()]
        CF = Tk()
        STAT = [Tk(), Tk()]
        allc = [GA, GA2, CF] + SZ + WT + HG + HN + TTt + RES + STAT
        for l_ in (QT, KT, SO, [K2C1], YM):
            for x_ in l_:
                allc += x_
        for t in allc:
            for o in STGB + STGA:
                for s_, i_ in list(o.w.items()) + list(o.r.items()):
                    if s_ not in t.w or t.w[s_].gid < i_.gid:
                        t.w[s_] = i_

        PT7 = P4[:, :].bitcast(BF16)[:, 512:768]
        PJ1 = PAh[1].bitcast(BF16)
        xcb = lambda c: XC[c // 4]

        def T_mo(c, half, part=None):
            p = c % 2
            ucol = HALO + c * 128
            if part in (None, "pe"):
                for kt in range(8):
                    S.op("pe", lambda e, kt=kt: e.matmul(
                        PAh[0][:, 0:512], lhsT=uT[:, kt, ucol:ucol + 128], rhs=WS[2][:, kt, half * 512:(half + 1) * 512],
                        start=(kt == 0), stop=(kt == 7)), reads=[WST[2][kt], UT[c + 1]], writes=[BK[0]])
            if part in (None, "ev"):
                S.op("act", lambda e: e.activation(out=so[p][:, half * 512:(half + 1) * 512], in_=PAh[0][:, 0:512], func=AF.Tanh, scale=0.5),
                     reads=[BK[0]], writes=[SO[p][half]])

        def T_k2(c, pair):
            p = c % 2
            tc0 = c * 128
            pv = PBh[pair]
            for q4 in range(4):
                ct = 4 * pair + q4
                S.op("pe", lambda e, ct=ct, q4=q4: e.matmul(
                    pv[:, q4 * 128:(q4 + 1) * 128], lhsT=xcT[:, ct, tc0:tc0 + 128], rhs=wbd[:, 1, ct, :],
                    start=True, stop=True), reads=[xcb(c)[ct], WBD], writes=[BK[2 + pair]])
            for hh in range(2):
                h = 2 * pair + hh
                S.op("act", lambda e, h=h, hh=hh: e.activation(
                    out=k2c[p][:, h * 256:(h + 1) * 256], in_=pv[:, hh * 256:(hh + 1) * 256], func=AF.Identity,
                    scale=A2x[:, c, h:h + 1]), reads=[BK[2 + pair], GATE[c // 4]], writes=[K2C[p][pair]])

        def T_qk(c, pair, which):
            p = c % 2
            tc0 = c * 128
            pv = PBh[which]
            dst, dstT = (qT, QT) if which == 0 else (kT, KT)
            for q4 in range(4):
                nt = 4 * pair + q4
                S.op("pe", lambda e, nt=nt, q4=q4: e.matmul(
                    pv[:, q4 * 128:(q4 + 1) * 128], lhsT=wbd[:, which, nt, :], rhs=xcT[:, nt, tc0:tc0 + 128],
                    start=True, stop=True), reads=[xcb(c)[nt], WBD], writes=[BK[2 + which]])
            S.op("act", lambda e: e.activation(out=dst[p][:, 4 * pair:4 * pair + 4, :],
                                               in_=pv[:, 0:512].rearrange("p (k t) -> p k t", k=4), func=AF.Copy),
                 reads=[BK[2 + which]], writes=[dstT[p][pair]])

        def T_szp(sbi, np_):
            c0 = HALO + sbi * 256
            hb = np_ % 2
            for q2 in range(2):
                nt = 2 * np_ + q2
                uts = list(range((c0 - HALO) // 128 + 1, (c0 + 255 - HALO) // 128 + 2))
                for kt in range(8):
                    S.op("pe", lambda e, kt=kt, nt=nt, q2=q2: e.matmul(
                        PBh[hb][:, q2 * 256:(q2 + 1) * 256], lhsT=WS[1][:, kt, nt * 128:(nt + 1) * 128],
                        rhs=uT[:, kt, c0:c0 + 256], start=(kt == 0), stop=(kt == 7)),
                        reads=[WST[1][kt]] + [UT[i] for i in uts], writes=[BK[2 + hb]])
            S.op("act", lambda e: e.activation(
                out=szT[:, 2 * np_:2 * np_ + 2, :], in_=PBh[hb][:, 0:512].rearrange("p (k t) -> p k t", k=2), func=AF.Silu),
                reads=[BK[2 + hb]], writes=[SZ[np_]])

        def T_sz(sbi):
            for np_ in range(4):
                T_szp(sbi, np_)

        def T_st(c, h):
            p = c % 2
            ws = h % 2
            for dt in range(2):
                S.op("pe", lambda e, dt=dt: e.matmul(
                    P4[:, 0:128], lhsT=kT[p][:, 2 * h + dt, :], rhs=qT[p][:, 2 * h + dt, :],
                    start=(dt == 0), stop=(dt == 1)), reads=[KT[p][h // 2], QT[p][h // 2]], writes=[BK[4]])
            S.op("dve", lambda e: e.scalar_tensor_tensor(
                out=Wt[ws][:, :], in0=P4[:, 0:128], scalar=Ax[:, c, h:h + 1], in1=mask01,
                op0=ALU.mult, op1=ALU.mult), reads=[BK[4], GATE[c // 4], CONST], writes=[WT[ws]])

        def T_nd(c, h):
            p = c % 2
            ws = h % 2
            pv, bk = (P5, 5) if h % 2 == 0 else (PAh[1], 1)
            S.op("pe", lambda e: e.matmul(pv[:, 0:257], lhsT=Wt[ws][:, :], rhs=vext[:, c, h, 0:257],
                                          start=True, stop=False), reads=[WT[ws], VX[c]], writes=[BK[bk]])
            for dt in range(2):
                S.op("pe", lambda e, dt=dt: e.matmul(
                    pv[:, 0:257], lhsT=qT[p][:, 2 * h + dt, :], rhs=Cbf[:, h, dt, 0:257],
                    start=False, stop=(dt == 1)), reads=[QT[p][h // 2], CB[h]], writes=[BK[bk]])

        def T_u(c, h, dt):
            p = c % 2
            pu = P6 if dt == 0 else P7
            S.op("pe", lambda e: e.matmul(
                pu[:, 0:257], lhsT=k2c[p][:, h * 256 + dt * 128:h * 256 + dt * 128 + 128],
                rhs=vext[:, c, h, 0:257], start=True, stop=True), reads=[K2C[p][h // 2], VX[c]], writes=[BK[6 + dt]])
            if dt == 1:
                S.op("dve", lambda e: e.scalar_tensor_tensor(
                    out=Cst[:, h, :, 0:257], in0=Cst[:, h, :, 0:257], scalar=EG[:, c, h:h + 1],
                    in1=P67[:, :].rearrange("p (a b) -> p a b", a=2)[:, :, 0:257],
                    op0=ALU.mult, op1=ALU.add), reads=[CS[h], GATE[c // 4], BK[6], BK[7]], writes=[CS[h]])

        def T_cb(c, h):
            S.op("pool", lambda e: e.tensor_copy(out=Cbf[:, h, :, 0:257], in_=Cst[:, h, :, 0:257]),
                 reads=[CS[h]], writes=[CB[h]])

        def T_stat(c, h):
            p = c % 2
            ws = h % 2
            pr = h // 2
            pv, bk = (P5, 5) if h % 2 == 0 else (PAh[1], 1)
            S.op("dve", lambda e: e.tensor_copy(out=stt[:, h:h + 1], in_=pv[:, 256:257]), reads=[BK[bk], STAT[pr]], writes=[STAT[pr]])
            S.op("dve", lambda e: e.scalar_tensor_tensor(
                out=hg[ws][:, :], in0=so[p][:, h * 256:(h + 1) * 256], scalar=1.0, in1=pv[:, 0:256],
                op0=ALU.add, op1=ALU.mult), reads=[BK[bk], SO[p][h // 2]], writes=[HG[ws]])
            st6 = stt[:, 32 + 8 * ws:38 + 8 * ws]
            S.op("dve", lambda e: e.bn_stats(out=st6, in_=hg[ws][:, :]), reads=[HG[ws], STAT[pr]], writes=[STAT[pr]])
            S.op("dve", lambda e: e.bn_aggr(out=stt[:, 8 + 2 * h:10 + 2 * h], in_=st6), reads=[STAT[pr]], writes=[STAT[pr]])

        def T_rstd(c, pr):
            hs = slice(2 * pr, 2 * pr + 2)
            dcol = stt[:, hs]
            mv = stt[:, 8:16].rearrange("p (h two) -> p h two", two=2)
            q_ = stt[:, 16 + 2 * pr:18 + 2 * pr]
            r_ = stt[:, 20 + 2 * pr:22 + 2 * pr]
            nb_ = stt[:, 24 + 2 * pr:26 + 2 * pr]
            S.op("dve", lambda e: e.tensor_tensor(out=dcol, in0=dcol, in1=dcol, op=ALU.mult), reads=[STAT[pr]], writes=[STAT[pr]])
            S.op("dve", lambda e: e.tensor_tensor(out=dcol, in0=dcol, in1=ENB[:, c, hs], op=ALU.max),
                 reads=[STAT[pr], GATE[c // 4]], writes=[STAT[pr]])
            S.op("dve", lambda e: e.scalar_tensor_tensor(out=q_, in0=dcol, scalar=4.0 * EPS, in1=mv[:, hs, 1], op0=ALU.mult, op1=ALU.add),
                 reads=[STAT[pr]], writes=[STAT[pr]])
            S.op("act", lambda e: e.activation(out=q_, in_=q_, func=AF.Sqrt), reads=[STAT[pr]], writes=[STAT[pr]])
            S.op("dve", lambda e: e.reciprocal(out=r_, in_=q_), reads=[STAT[pr]], writes=[STAT[pr]])
            S.op("dve", lambda e: e.scalar_tensor_tensor(out=nb_, in0=mv[:, hs, 0], scalar=-1.0, in1=r_, op0=ALU.mult, op1=ALU.mult),
                 reads=[STAT[pr]], writes=[STAT[pr]])

        def T_hn(c, h):
            ws = h % 2
            pr = h // 2
            S.op("act", lambda e: e.activation(out=hn[ws][:, :], in_=hg[ws][:, :], func=AF.Identity,
                                               scale=stt[:, 20 + h:21 + h], bias=stt[:, 24 + h:25 + h]),
                 reads=[HG[ws], STAT[pr]], writes=[HN[ws]])

        def T_tr(c, h):
            p = c % 2
            ws = h % 2
            lo = (c % 2) * 128
            tc0 = c * 128
            for dt in range(2):
                S.op("pe", lambda e, dt=dt: e.transpose(out=PT7[:, dt * 128:(dt + 1) * 128],
                                                        in_=hn[ws][:, dt * 128:(dt + 1) * 128], identity=identb[:, :]),
                     reads=[HN[ws], CONST], writes=[BK[4]])
            for dt in range(2):
                nt = 2 * h + dt
                S.op("act", lambda e, dt=dt, nt=nt: e.activation(out=tt[dt][:, :], in_=PT7[:, dt * 128:(dt + 1) * 128],
                                                                 func=AF.Identity, scale=chv[:, nt, 7:8]),
                     reads=[BK[4], CONST], writes=[TTt[dt]])
                S.op("dve", lambda e, dt=dt, nt=nt: e.scalar_tensor_tensor(
                    out=tt[dt][:, :], in0=xcT[:, nt, tc0:tc0 + 128], scalar=chv[:, nt, 6:7], in1=tt[dt][:, :],
                    op0=ALU.mult, op1=ALU.add), reads=[xcb(c)[nt], TTt[dt], CONST], writes=[TTt[dt]])
                S.op("pool", lambda e, dt=dt, nt=nt: e.tensor_tensor(
                    out=ymT[p][:, nt, :], in0=tt[dt][:, :], in1=szT[:, nt, lo:lo + 128], op=ALU.mult),
                    reads=[TTt[dt], SZ[nt // 2]], writes=[YM[p][h]])

        def T_res(c):
            for half in range(2):
                S.dma("sp", "res%d" % half, lambda e, half=half: e.dma_start(
                    out=res[half][:, :], in_=acc_d[c * 128:(c + 1) * 128, half * 512:(half + 1) * 512]),
                    reads=[ACC[c]], writes=[RES[half]])

        def T_out(c, half, part=None):
            p = c % 2
            if part in (None, "pe"):
                for kt in range(8):
                    S.op("pe", lambda e, kt=kt: e.matmul(
                        PAh[0][:, 0:512], lhsT=ymT[p][:, kt, :], rhs=WS[0][:, kt, half * 512:(half + 1) * 512],
                        start=(kt == 0), stop=(kt == 7)), reads=[YM[p][kt // 2], WST[0][kt]], writes=[BK[0]])
            if part in (None, "ev"):
                S.op("dve", lambda e: e.tensor_tensor(out=res[half][:, :], in0=PAh[0][:, 0:512], in1=res[half][:, :], op=ALU.add),
                     reads=[BK[0], RES[half]], writes=[RES[half]])

        def T_fin(c):
            p = c % 2
            fs = cf[:, 40:42]
            fr = cf[:, 42:43]
            for half in range(2):
                S.op("act", lambda e, half=half: e.activation(out=PAh[0][:, 0:512], in_=res[half][:, :],
                                                              func=AF.Square, accum_out=fs[:, half:half + 1]),
                     reads=[RES[half], CF], writes=[BK[0], CF])
            S.op("dve", lambda e: e.tensor_tensor(out=fr, in0=fs[:, 0:1], in1=fs[:, 1:2], op=ALU.add), reads=[CF], writes=[CF])
            S.op("act", lambda e: e.activation(out=fr, in_=fr, func=AF.Sqrt, scale=1.0 / D, bias=EPS), reads=[CF], writes=[CF])
            S.op("dve", lambda e: e.reciprocal(out=fr, in_=fr), reads=[CF], writes=[CF])
            for half in range(2):
                S.op("dve", lambda e, half=half: e.scalar_tensor_tensor(
                    out=res[half][:, :], in0=res[half][:, :], scalar=fr, in1=nwb[:, half * 512:(half + 1) * 512],
                    op0=ALU.mult, op1=ALU.mult), reads=[RES[half], CF, NWB], writes=[RES[half]])
                S.dma("sp", "out%d" % half, lambda e, half=half: e.dma_start(
                    out=out_d[c * 128:(c + 1) * 128, half * 512:(half + 1) * 512], in_=res[half][:, :]),
                    reads=[RES[half]], writes=[OUT])

        def setup(c):
            T_mo(c, 0)
            T_mo(c, 1)
            T_k2(c, 0)
            T_k2(c, 1)
            T_qk(c, 0, 0)
            T_qk(c, 0, 1)
            T_qk(c, 1, 0)
            T_qk(c, 1, 1)

        T_sz(0)
        T_mo(0, 0)
        T_mo(0, 1)
        T_k2(0, 0)
        T_k2(0, 1)
        for h in range(4):
            gath, GAh = (gathA, GA) if h % 2 == 0 else (gathB, GA2)
            S.dma("sp", "ga%d" % (h % 2), lambda e, h=h, gath=gath: e.dma_start(out=gath[:, :, :], in_=cco_t[h].ap().rearrange("(r p) n -> p r n", p=128)),
                  reads=[CCO[h]], writes=[GAh])
            tm = cf[:, 0:16].rearrange("p (a b) -> p a b", a=4)
            for i in range(4):
                S.op("dve", lambda e, i=i, tm=tm, gath=gath: e.tensor_tensor(out=tm[:, i, :], in0=brb[:, i, :], in1=gath[:, :, 257], op=ALU.mult),
                     reads=[GAh, CONST, CF], writes=[CF])
            S.op("dve", lambda e, tm=tm: e.tensor_reduce(out=cf[:, 16:20], in_=tm, axis=AX.X, op=ALU.add), reads=[CF], writes=[CF])
            S.op("act", lambda e: e.activation(out=cf[:, 20:24], in_=cf[:, 16:20], func=AF.Exp, scale=-1.0), reads=[CF], writes=[CF])
            S.op("dve", lambda e: e.tensor_tensor(out=cf[:, 24:28], in0=cf[:, 20:24], in1=cmask[:, :], op=ALU.mult),
                 reads=[CF, CONST], writes=[CF])
            cflat = Cst[:, h, :, :].rearrange("p a b -> p (a b)")
            S.op("dve", lambda e, cflat=cflat, gath=gath: e.tensor_scalar(out=cflat, in0=gath[:, 0, :], scalar1=cf[:, 24:25], scalar2=None, op0=ALU.mult),
                 reads=[GAh, CF, CS[h]], writes=[CS[h]])
            for i in range(1, 4):
                S.op("dve", lambda e, cflat=cflat, i=i, gath=gath: e.scalar_tensor_tensor(
                    out=cflat, in0=gath[:, i, :], scalar=cf[:, 24 + i:25 + i], in1=cflat, op0=ALU.mult, op1=ALU.add),
                    reads=[GAh, CF, CS[h]], writes=[CS[h]])
            S.op("act", lambda e, h=h: e.activation(out=Cbf[:, h, :, :], in_=Cst[:, h, :, :], func=AF.Copy),
                 reads=[CS[h]], writes=[CB[h]])
        for l_ in YM:
            for t in l_ + RES:
                for s_, i_ in list(GA.w.items()) + list(GA.r.items()):
                    if s_ not in t.w or t.w[s_].gid < i_.gid:
                        t.w[s_] = i_
        for l_ in QT + KT:
            for t in l_:
                for s_, i_ in list(GA2.w.items()) + list(GA2.r.items()):
                    if s_ not in t.w or t.w[s_].gid < i_.gid:
                        t.w[s_] = i_
        T_qk(0, 0, 0)
        T_qk(0, 0, 1)
        T_qk(0, 1, 0)
        T_qk(0, 1, 1)

        load_w(0, wout_d[1024:2048, :], after=[CB[3]])

        def front(c, h, piece=None):
            T_nd(c, h)
            T_u(c, h, 0)
            T_u(c, h, 1)
            if h < 3:
                T_st(c, h + 1)
            if piece is not None:
                piece("pe")

        def back(c, h, piece=None):
            T_stat(c, h)
            T_cb(c, h)
            if piece is not None:
                piece("ev")

        for c in range(16):
            nx = c + 1 < 16
            pc = [None] * 4
            if nx:
                pc[0] = lambda part, c=c: T_mo(c + 1, 0, part)
                pc[1] = lambda part, c=c: T_mo(c + 1, 1, part)
            if c > 0:
                pc[2] = lambda part, c=c: T_out(c - 1, 0, part)
                pc[3] = lambda part, c=c: T_out(c - 1, 1, part)
            T_st(c, 0)
            front(c, 0, pc[0])
            if c > 0:
                T_hn(c - 1, 2)
                T_hn(c - 1, 3)
            back(c, 0, pc[0])
            if nx:
                T_qk(c + 1, 0, 0); T_qk(c + 1, 0, 1)
            front(c, 1, pc[1])
            back(c, 1, pc[1])
            if c > 0:
                T_tr(c - 1, 2)
                T_tr(c - 1, 3)
                if (c - 1) % 2 == 1:
                    T_szp(c // 2, 2)
                    T_szp(c // 2, 3)
            if nx:
                T_qk(c + 1, 1, 0); T_qk(c + 1, 1, 1)
            T_rstd(c, 0)
            front(c, 2, pc[2])
            T_hn(c, 0)
            T_hn(c, 1)
            back(c, 2, pc[2])
            if nx:
                T_k2(c + 1, 0)
            front(c, 3, pc[3])
            back(c, 3, pc[3])
            if nx:
                T_k2(c + 1, 1)
            T_tr(c, 0)
            T_tr(c, 1)
            if c > 0:
                T_fin(c - 1)
            T_rstd(c, 1)
            T_res(c)
            if c % 2 == 1 and nx:
                T_szp((c + 1) // 2, 0)
                T_szp((c + 1) // 2, 1)
        T_hn(15, 2)
        T_hn(15, 3)
        T_tr(15, 2)
        T_tr(15, 3)
        T_out(15, 0)
        T_out(15, 1)
        T_fin(15)
        if DEBUG:
            dbg_xc = nc.dram_tensor("dbg_xc", [128, 8 * 2048], BF16, kind="ExternalOutput").ap()
            dbg_g = nc.dram_tensor("dbg_g", [128, 256], F32, kind="ExternalOutput").ap()
            dbg_v = nc.dram_tensor("dbg_v", [128, 16 * 4 * 258], BF16, kind="ExternalOutput").ap()
            allxc = [t for l_ in XC for t in l_]
            S.dma("sp", "dbg1", lambda e: e.dma_start(out=dbg_xc, in_=xcT[:, :, :].rearrange("p k t -> p (k t)")), reads=allxc, writes=[OUT])
            for i_, arr in enumerate([Ax, A2x, EG, ENB]):
                S.dma("sp", "dbg2", lambda e, i_=i_, arr=arr: e.dma_start(out=dbg_g[:, i_ * 64:(i_ + 1) * 64], in_=arr[:, :, :].rearrange("p a b -> p (a b)")),
                      reads=GATE, writes=[OUT])
            S.dma("sp", "dbg3", lambda e: e.dma_start(out=dbg_v, in_=vext[:, :, :, :].rearrange("p a b c -> p (a b c)")), reads=VX, writes=[OUT])
        S.join("sp", [OUT])
        S.emit(nc)
    return nc


def _block_diag(w):
    m = np.zeros((1024, 128), np.float32)
    for n in range(256):
        ct, nn = divmod(n, 32)
        m[ct * 128 + 4 * nn:ct * 128 + 4 * nn + 4, 4 * nn:4 * nn + 4] = w[n]
    return m


_NC_CACHE = {}


def kernel(x, norm_w, w_in, pool_w, pool_scale, conv_w, conv_b, w_q, w_k, w_v, w_if, b_if,
           mh_norm_w, m_skip, w_out, final_norm_w):
    f = lambda a: np.ascontiguousarray(np.asarray(a, dtype=np.float32))
    x = f(x)
    B, SEQ, _ = x.shape
    nseg = SEQ // T
    assert B * nseg == NCORES
    if "nc" not in _NC_CACHE:
        _NC_CACHE["nc"] = build_nc()
    nc = _NC_CACHE["nc"]

    chv = np.zeros((128, 8, 8), np.float32)
    cw = f(conv_w)[0]
    for j in range(4):
        chv[:, :, j] = cw[j].reshape(8, 128).T
    chv[:, :, 4] = f(conv_b)[0].reshape(8, 128).T
    chv[:, :, 5] = f(pool_scale)[0].reshape(8, 128).T
    chv[:, :, 6] = f(m_skip)[0].reshape(8, 128).T
    chv[:, :, 7] = f(mh_norm_w)[0].reshape(8, 128).T
    cst = np.zeros((128, 4, 128), np.float32)
    cst[:, 0, :] = np.eye(128, dtype=np.float32)
    cst[:, 1, :] = np.triu(np.ones((128, 128), np.float32))
    cst[:, 2, :] = 1.0
    cst[:, 3, :] = np.triu(np.ones((128, 128), np.float32))
    shared = {
        "w_in": f(w_in)[0], "w_out": f(w_out)[0], "pool_w": f(pool_w)[0].reshape(1024, 256),
        "wbd": np.ascontiguousarray(np.stack([_block_diag(f(w_q)[0]), _block_diag(f(w_k)[0]), _block_diag(f(w_v)[0])], axis=0)
                                    .reshape(3, 8, 128, 128).transpose(2, 0, 1, 3).reshape(128, 3 * 8 * 128)),
        "wbdT": np.ascontiguousarray(np.stack([_block_diag(f(w_q)[0]), _block_diag(f(w_k)[0]), _block_diag(f(w_v)[0])], axis=0)
                                     .reshape(3, 8, 128, 128).transpose(3, 0, 1, 2).reshape(128, 3 * 8 * 128)),
        "w_if": np.ascontiguousarray(f(w_if)[0].reshape(24, 128, 8).transpose(1, 0, 2).reshape(128, 192)), "bifb": np.ascontiguousarray(np.broadcast_to(np.tile(f(b_if)[0].reshape(1, 8), (1, 4)), (128, 32))), "chv": chv.reshape(128, 64),
        "normw_b": np.ascontiguousarray(np.broadcast_to(f(norm_w)[0][None, :], (128, D))),
        "fnw_b": np.ascontiguousarray(np.broadcast_to(f(final_norm_w)[None, :], (128, D))),
        "cst": cst.reshape(128, 512),
        "identb_in": np.eye(128, dtype=np.float32).astype(ml_dtypes.bfloat16),
    }
    in_maps = []
    for r in range(NCORES):
        b, j = divmod(r, nseg)
        start = j * T
        xs = np.zeros((TT, D), np.float32)
        if j == 0:
            xs[HALO:] = x[b, 0:T]
        else:
            xs[:] = x[b, start - HALO:start + T]
        invc = np.zeros((4, 16), np.float32)
        for g in range(4):
            win = 2 ** (g + 1)
            for i in range(16):
                invc[g, i] = 1.0 / min(start + i + 1, win)
        brb = np.zeros((4, 4), np.float32)
        cm = np.zeros((4,), np.float32)
        for i in range(4):
            cm[i] = 1.0 if i < j else 0.0
            for l in range(4):
                brb[i, l] = 1.0 if (i < l < j) else 0.0
        m = dict(shared)
        m["x"] = xs
        m["invcnt"] = np.ascontiguousarray(np.broadcast_to(invc.reshape(1, 64), (128, 64)))
        m["brb"] = np.ascontiguousarray(np.broadcast_to(brb.reshape(1, 16), (128, 16)))
        m["cmask"] = np.ascontiguousarray(np.broadcast_to(cm.reshape(1, 4), (128, 4)))
        in_maps.append(m)
    res = run_bass_kernel_spmd(nc, in_maps, core_ids=list(range(NCORES)))
    if DEBUG:
        _NC_CACHE["res"] = res
    out = np.zeros((B, SEQ, D), np.float32)
    for r in range(NCORES):
        b, j = divmod(r, nseg)
        out[b, j * T:(j + 1) * T] = res.results[r]["out"]
    return out
```

```python
from contextlib import ExitStack
import numpy as np
import ml_dtypes
import concourse.bass as bass
import concourse.mybir as mybir
from concourse.bass_utils import run_bass_kernel_spmd

F32 = mybir.dt.float32
BF16 = mybir.dt.bfloat16
AF = mybir.ActivationFunctionType
ALU = mybir.AluOpType
AX = mybir.AxisListType

NCORES = 8
T = 2048
HALO = 16
TT = T + HALO
D = 1024
EPS = 1e-6
ENGS = ["pe", "act", "dve", "pool", "sp"]
STRICT_ENGS = ("dve", "act")


class Tk:
    __slots__ = ("w", "r", "excl")

    def __init__(self, excl=False):
        self.w = {}
        self.r = {}
        self.excl = excl


class Ins:
    __slots__ = ("eng", "stream", "fn", "deps", "needed", "seq", "gid", "isdma", "inc")


class Sched:
    def __init__(self):
        self.order = {e: [] for e in ENGS}
        self.dmacnt = {}
        self.gid = 0

    def _new(self, eng, stream, fn, isdma):
        i = Ins()
        i.eng, i.stream, i.fn, i.isdma = eng, stream, fn, isdma
        i.deps = set()
        i.needed = False
        i.seq = 0
        i.inc = 16
        self.gid += 1
        i.gid = self.gid
        return i

    def _link(self, ins, reads, writes):
        xr = [t for t in reads if t.excl and t not in writes]
        if xr:
            writes = list(writes) + xr
        raw, oth = set(), set()
        for t in reads:
            raw.update(t.w.values())
        for t in writes:
            oth.update(t.w.values())
            oth.update(t.r.values())
        deps = set()
        for d in raw:
            if d.stream == ins.stream and ins.stream == "pe":
                continue
            deps.add(d)
        for d in oth:
            if d.stream == ins.stream and not ins.isdma and ins.stream not in STRICT_ENGS:
                continue
            deps.add(d)
        deps.discard(ins)
        ins.deps = deps
        for d in deps:
            d.needed = True
        for t in reads:
            t.r[ins.stream] = ins
        for t in writes:
            t.w = {ins.stream: ins}
            t.r = {}
        self.order[ins.eng].append(ins)
        return ins

    def op(self, eng, fn, reads=(), writes=()):
        return self._link(self._new(eng, eng, fn, False), reads, writes)

    def dma(self, qeng, key, fn, reads=(), writes=(), inc=16):
        ins = self._new(qeng, ("d", key), fn, True)
        ins.inc = inc
        n = self.dmacnt.get(key, 0) + 1
        self.dmacnt[key] = n
        ins.seq = n
        return self._link(ins, reads, writes)

    def join(self, eng, tiles):
        return self._link(self._new(eng, eng, None, False), (), tiles)

    def emit(self, nc):
        for e in ENGS:
            n = 0
            for ins in self.order[e]:
                if ins.isdma:
                    continue
                if ins.needed and ins.fn is not None:
                    n += 1
                ins.seq = n
        with ExitStack() as st:
            sems = {}
            for e in ENGS:
                sems[e] = st.enter_context(nc.semaphore("s_" + e))
            for i, k in enumerate(self.dmacnt):
                sems[("d", k)] = st.enter_context(nc.semaphore("d%d" % i))
            block = st.enter_context(nc.Block())

            def run(eng_name, e):
                known = {}
                for ins in self.order[eng_name]:
                    for d in sorted(ins.deps, key=lambda z: z.gid):
                        val = d.seq * d.inc if d.isdma else d.seq
                        if val <= 0 or known.get(d.stream, 0) >= val:
                            continue
                        e.wait_ge(sems[d.stream], val)
                        known[d.stream] = val
                    if ins.fn is None:
                        continue
                    r = ins.fn(e)
                    if ins.isdma:
                        r.then_inc(sems[ins.stream], ins.inc)
                    elif ins.needed:
                        r.then_inc(sems[ins.stream], 1)

            @block.tensor
            def _(e):
                run("pe", e)

            @block.scalar
            def _(e):
                run("act", e)

            @block.vector
            def _(e):
                run("dve", e)

            @block.gpsimd
            def _(e):
                run("pool", e)

            @block.sync
            def _(e):
                run("sp", e)


DEBUG = False


def build_nc():
    nc = bass.Bass("TRN2", target_bir_lowering=False)
    S = Sched()

    def din(name, shape):
        return nc.dram_tensor(name, shape, F32, kind="ExternalInput").ap()

    x_d = din("x", [TT, D])
    win_d = din("w_in", [D, 5120])
    wout_d = din("w_out", [2048, D])
    poolw_d = din("pool_w", [1024, 256])
    wbd_d = din("wbd", [128, 3 * 8 * 128])
    wbdT_d = din("wbdT", [128, 3 * 8 * 128])
    wif_d = din("w_if", [128, 24 * 8])
    bifb_d = din("bifb", [128, 32])
    chv_d = din("chv", [128, 64])
    nw_d = din("normw_b", [128, D])
    fnw_d = din("fnw_b", [128, D])
    cst_d = din("cst", [128, 512])
    identb_d = nc.dram_tensor("identb_in", [128, 128], BF16, kind="ExternalInput").ap()
    invc_d = din("invcnt", [128, 64])
    brb_d = din("brb", [128, 16])
    cmask_d = din("cmask", [128, 4])
    out_d = nc.dram_tensor("out", [T, D], F32, kind="ExternalOutput").ap()
    acc_d = nc.dram_tensor("acc_scr", [T, D], F32).ap()
    cci_t = [nc.dram_tensor("cci%d" % h, [128, 516], F32) for h in range(4)]
    cco_t = [nc.dram_tensor("cco%d" % h, [512, 516], F32) for h in range(4)]

    with ExitStack() as st:
        def sb(name, shape, dt):
            return st.enter_context(nc.sbuf_tensor(name, shape, dt))

        def ps(name, shape):
            return st.enter_context(nc.psum_tensor(name, shape, F32))

        uT = sb("uT", [128, 8, TT], BF16)
        xcT = sb("xcT", [128, 8, T], BF16)
        vext = sb("vext", [128, 16, 4, 258], BF16)
        WS = [sb("ws%d" % i, [128, 8, 1024], BF16) for i in range(3)]
        poolw = sb("poolw", [128, 4, 2, 256], BF16)
        wbd = sb("wbd_sb", [128, 3, 8, 128], BF16)
        wif = sb("wif_sb", [128, 24, 8], BF16)
        bifb = sb("bifb_sb", [128, 32], F32)
        wg = sb("wg", [128, 16, 8], BF16)
        chv = sb("chv_sb", [128, 8, 8], F32)
        nwb = sb("nwb", [128, D], F32)
        identb = sb("identb", [128, 128], BF16)
        cstf = sb("cstf", [128, 4, 128], F32)
        invc = sb("invc", [128, 4, 16], F32)
        brb = sb("brb_sb", [128, 4, 4], F32)
        cmask = sb("cmask_sb", [128, 4], F32)
        Ax = sb("Ax", [128, 16, 4], F32)
        A2x = sb("A2x", [128, 16, 4], F32)
        EG = sb("EG", [128, 16, 4], F32)
        ENB = sb("ENB", [128, 16, 4], F32)
        Cst = sb("Cst", [128, 4, 2, 258], F32)
        Cbf = sb("Cbf", [128, 4, 2, 258], BF16)
        mxhalo = sb("mxhalo", [128, 8, 4], F32)
        pxhalo = sb("pxhalo", [128, 8, 16], F32)
        ngt = sb("ngt", [128, 4], F32)
        small = sb("small", [128, 16], F32)
        WN = 8170
        work = sb("work", [128, WN], F32)

        identf = cstf[:, 0, :]
        tri = cstf[:, 1, :]
        ones = cstf[:, 2, :]
        mask01 = cstf[:, 3, :]

        PA = ps("PA", [128, 1024])
        PB = ps("PB", [128, 1024])
        P4 = ps("P4", [128, 512])
        P5 = ps("P5", [128, 512])
        P67 = ps("P67", [128, 1024])
        P6 = P67[:, 0:512]
        P7 = P67[:, 512:1024]
        BK = [Tk(True) for _ in range(8)]

        UT = [Tk() for _ in range(17)]
        XC = [[Tk() for _ in range(8)] for _ in range(4)]
        VX = [Tk() for _ in range(16)]
        WST = [[Tk() for _ in range(8)] for _ in range(3)]
        CONST = Tk()
        NWB = Tk()
        GATE = [Tk() for _ in range(4)]
        CS = [Tk() for _ in range(4)]
        CB = [Tk() for _ in range(4)]
        MXH = [Tk() for _ in range(8)]
        PXH = [Tk() for _ in range(8)]
        NGT = Tk()
        OUT = Tk()
        ACC = [Tk() for _ in range(16)]
        CCI = [Tk() for _ in range(4)]
        CCO = [Tk() for _ in range(4)]

        wk = [0]

        def take(n):
            a = work[:, wk[0]:wk[0] + n]
            wk[0] += n
            assert wk[0] <= WN, wk[0]
            return a

        xt = [take(1024) for _ in range(2)]
        sqj = take(512).bitcast(BF16)
        ub = [take(512).bitcast(BF16) for _ in range(2)]
        XT = [Tk(), Tk()]
        SQ = Tk()
        UB = [Tk(), Tk()]
        SS = [Tk(), Tk()]
        def s0_load(k, ti):
            rows = 16 if ti < 0 else 128
            r0 = 0 if ti < 0 else HALO + 128 * ti
            sl = k % 2
            S.dma("sp", "x%d" % sl, lambda e: e.dma_start(out=xt[sl][:rows, :], in_=x_d[r0:r0 + rows, :]), writes=[XT[sl]])

        s0_load(0, -1)
        s0_load(1, 0)
        def ld(key, q, out, in_, w):
            ins = S.dma(q, key, lambda e: e.dma_start(out=out, in_=in_), writes=[] if w and w[0] is CONST else w)
            if w and w[0] is CONST:
                CONST.w[ins.stream] = ins

        ld("c0", "sp", nwb[:, :], nw_d, [NWB])
        ld("c1", "sp", identb[:, :], identb_d, [CONST])
        ld("c2", "sp", cstf[:, :, :], cst_d.rearrange("p (a b) -> p a b", a=4), [CONST])
        ld("c3", "sp", chv[:, :, :], chv_d.rearrange("p (a b) -> p a b", a=8), [CONST])
        ld("c4", "sp", bifb[:, :], bifb_d, [CONST])
        ld("c5", "sp", invc[:, :, :], invc_d.rearrange("p (a b) -> p a b", a=4), [CONST])
        ld("c6", "sp", brb[:, :, :], brb_d.rearrange("p (a b) -> p a b", a=4), [CONST])
        ld("c7", "sp", cmask[:, :], cmask_d, [CONST])
        def load_w(slot, src2d, after=()):
            for kt in range(8):
                S.dma("pool", "w%d_%d" % (slot, kt),
                      lambda e, kt=kt: e.dma_start(out=WS[slot][:, kt, :], in_=src2d[kt * 128:(kt + 1) * 128, :]),
                      reads=list(after), writes=[WST[slot][kt]])

        S.op("pool", lambda e: e.memset(Cst[:, :, :, :], 0.0), writes=CS)
        S.op("pool", lambda e: e.memset(vext[:, :, :, 256:258], 1.0), writes=VX)
        S.op("pool", lambda e: e.memset(ngt[:, :], 0.0), writes=[NGT])

        WBD = Tk()
        WIF = Tk()
        POOLW = Tk()
        WBT = Tk()
        DGT = Tk()
        WG = Tk()
        WSC = [Tk() for _ in range(4)]
        wbdT = work[:, 5808:5808 + 1536].bitcast(BF16).rearrange("p (w c n) -> p w c n", w=3, c=8)
        dg = work[:, 3760:3760 + 2048].bitcast(BF16).rearrange("p (c j n) -> p c j n", c=8, j=4)

        def load_wmx(cb, after):
            S.dma("pool", "wc%d" % cb, lambda e: e.dma_start(
                out=WS[0][:, :, cb * 256:(cb + 1) * 256],
                in_=win_d[:, 2048 + cb * 256:2048 + (cb + 1) * 256].rearrange("(kt p) n -> p kt n", p=128)),
                reads=list(after), writes=[WSC[cb]])

        def ld_after(key, out, in_, w, after):
            S.dma("pool", key, lambda e: e.dma_start(out=out, in_=in_), reads=list(after), writes=w)

        load_wmx(0, [])
        ld("c9", "pool", wif[:, :, :], wif_d.rearrange("p (k n) -> p k n", k=24), [WIF])
        late_loads = {
            3: lambda: load_wmx(1, [UT[3]]),
            6: lambda: ld_after("c11", wbdT, wbdT_d.rearrange("p (w c n) -> p w c n", w=3, c=8), [WBT], [UT[6]]),
            8: lambda: load_wmx(2, [UT[8]]),
            10: lambda: ld_after("c8", wbd[:, :, :, :], wbd_d.rearrange("p (w c n) -> p w c n", w=3, c=8), [WBD], [UT[10]]),
            12: lambda: load_wmx(3, [UT[12]]),
            16: lambda: ld_after("c10", poolw[:, :, :, :], poolw_d.rearrange("(g c p) n -> p g c n", g=4, c=2), [POOLW], [UT[16]]),
        }

        PTbs = [P6.bitcast(BF16), P7.bitcast(BF16)]

        def s0_front(k, ti):
            rows = 16 if ti < 0 else 128
            r0 = 0 if ti < 0 else HALO + 128 * ti
            sl = k % 2
            PTb = PTbs[sl]
            ss = small[:, sl * 2:sl * 2 + 1]
            rs = small[:, sl * 2 + 1:sl * 2 + 2]
            if k >= 2:
                s0_load(k, ti)
            S.op("act", lambda e: e.activation(out=sqj[:rows, :], in_=xt[sl][:rows, :], func=AF.Square, accum_out=ss[:rows, :]),
                 reads=[XT[sl]], writes=[SQ, SS[sl]])
            S.op("act", lambda e: e.activation(out=rs[:rows, :], in_=ss[:rows, :], func=AF.Sqrt, scale=1.0 / D, bias=EPS),
                 reads=[SS[sl]], writes=[SS[sl]])
            S.op("dve", lambda e: e.reciprocal(out=rs[:rows, :], in_=rs[:rows, :]), reads=[SS[sl]], writes=[SS[sl]])
            S.op("dve", lambda e: e.scalar_tensor_tensor(
                out=ub[sl][:rows, :], in0=xt[sl][:rows, :], scalar=rs[:rows, :], in1=nwb[:rows, :],
                op0=ALU.mult, op1=ALU.mult), reads=[XT[sl], SS[sl], NWB], writes=[UB[sl]])
            for kt in range(8):
                S.op("pe", lambda e, kt=kt: e.transpose(
                    out=PTb[:, kt * 128:kt * 128 + rows], in_=ub[sl][:rows, kt * 128:(kt + 1) * 128],
                    identity=identb[:rows, :rows]), reads=[UB[sl], CONST], writes=[BK[6 + sl]])

        def s0_back(k, ti):
            rows = 16 if ti < 0 else 128
            r0 = 0 if ti < 0 else HALO + 128 * ti
            sl = k % 2
            PTb = PTbs[sl]
            S.op("act", lambda e: e.activation(
                out=uT[:, :, r0:r0 + rows], in_=PTb.rearrange("p (k t) -> p k t", k=8)[:, :, 0:rows], func=AF.Copy),
                reads=[BK[6 + sl]], writes=[UT[k]])

        tiles = list(enumerate(range(-1, 16)))
        s0_front(*tiles[0])
        for i in range(len(tiles)):
            if i + 1 < len(tiles):
                s0_front(*tiles[i + 1])
            s0_back(*tiles[i])
            if tiles[i][0] in late_loads:
                late_loads[tiles[i][0]]()

        for nt in range(8):
            S.op("pe", lambda e, nt=nt: e.matmul(P6[:, nt * 8:(nt + 1) * 8], lhsT=wbdT[:, 0, nt, :], rhs=wif[:, nt, :],
                                                 start=True, stop=False), reads=[WBT, WIF], writes=[BK[6]])
            S.op("pe", lambda e, nt=nt: e.matmul(P6[:, nt * 8:(nt + 1) * 8], lhsT=wbdT[:, 1, nt, :], rhs=wif[:, 8 + nt, :],
                                                 start=False, stop=True), reads=[WBT, WIF], writes=[BK[6]])
            S.op("pe", lambda e, nt=nt: e.matmul(P6[:, 64 + nt * 8:64 + (nt + 1) * 8], lhsT=wbdT[:, 2, nt, :], rhs=wif[:, 16 + nt, :],
                                                 start=True, stop=True), reads=[WBT, WIF], writes=[BK[6]])
        S.op("act", lambda e: e.activation(out=wg[:, :, :], in_=P6[:, 0:128].rearrange("p (k n) -> p k n", k=16), func=AF.Copy),
             reads=[BK[6]], writes=[WG])
        for nt in range(8):
            for j in range(4):
                S.op("dve", lambda e, nt=nt, j=j: e.tensor_scalar(out=dg[:, nt, j, :], in0=identb[:, :], scalar1=chv[:, nt, j:j + 1],
                                                                  scalar2=None, op0=ALU.mult), reads=[CONST], writes=[DGT])
        load_w(1, win_d[:, 0:1024], after=[UT[16]])
        load_w(2, win_d[:, 1024:2048], after=[UT[16]])
        ld("c0", "sp", nwb[:, :], fnw_d, [NWB])

        STG0 = XT + [SQ] + UB
        wk[0] = 0
        mxb = take(2064).bitcast(BF16).rearrange("p (k t) -> p k t", k=8)
        k2 = [work[:, 5808 + 512 * i_:5808 + 512 * (i_ + 1)].bitcast(BF16) for i_ in range(2)]
        gtm = take(32)
        ef = take(16)
        lfn = take(16)
        nbg = take(32)
        t1 = take(16)
        t2 = take(16)
        assert wk[0] <= 3760
        MXB = [Tk() for _ in range(8)]
        K2 = [[Tk(), Tk()], [Tk(), Tk()]]
        GTM = Tk()
        STGA = MXB + [GTM]
        for t in STGA:
            for o in STG0:
                for s_, i_ in list(o.w.items()) + list(o.r.items()):
                    if s_ not in t.w or t.w[s_].gid < i_.gid:
                        t.w[s_] = i_

        STGA = STGA + [DGT, WBT]
        for t in K2[0] + K2[1]:
            for s_, i_ in list(WBT.w.items()) + list(WBT.r.items()):
                if s_ not in t.w or t.w[s_].gid < i_.gid:
                    t.w[s_] = i_
        PAh = [PA[:, 0:512], PA[:, 512:1024]]
        PBh = [PB[:, 0:512], PB[:, 512:1024]]

        def proj_fm(slot, nt, c0, n, hb, bank_base, pview):
            uts = [0] if c0 < HALO else list(range((c0 - HALO) // 128 + 1, (c0 + n - 1 - HALO) // 128 + 2))
            utk = [UT[i] for i in uts]
            for kt in range(8):
                S.op("pe", lambda e, kt=kt: e.matmul(
                    pview[:, 0:n], lhsT=WS[slot][:, kt, nt * 128:(nt + 1) * 128], rhs=uT[:, kt, c0:c0 + n],
                    start=(kt == 0), stop=(kt == 7)), reads=[WST[slot][kt]] + ([WSC[nt // 2]] if slot == 0 else []) + utk,
                    writes=[BK[bank_base + hb]])

        def A_proj(b, nt, part=None):
            hb = nt % 2
            if b < 0:
                proj_fm(0, nt, 0, HALO, hb, 0, PAh[hb])
                S.op("act", lambda e: e.activation(out=mxb[:, nt, 1:4], in_=PAh[hb][:, 13:16], func=AF.Copy),
                     reads=[BK[hb]], writes=[MXB[nt]])
                return
            c0 = HALO + 512 * b
            if part in (None, "p"):
                proj_fm(0, nt, c0, 512, hb, 0, PAh[hb])
                if b > 0:
                    S.op("dve", lambda e: e.tensor_copy(out=mxb[:, nt, 1:4], in_=mxb[:, nt, 513:516]),
                         reads=[MXB[nt]], writes=[MXB[nt]])
                S.op("act", lambda e: e.activation(out=mxb[:, nt, 4:516], in_=PAh[hb][:, 0:512], func=AF.Copy),
                     reads=[BK[hb]], writes=[MXB[nt]])
            if part in (None, "c"):
                for j in range(4):
                    S.op("pe", lambda e, j=j: e.matmul(PBh[hb][:, 0:512], lhsT=dg[:, nt, j, :], rhs=mxb[:, nt, 1 + j:1 + j + 512],
                                                       start=(j == 0), stop=(j == 3)), reads=[MXB[nt], DGT], writes=[BK[2 + hb]])
                S.op("act", lambda e: e.activation(out=xcT[:, nt, b * 512:(b + 1) * 512], in_=PBh[hb][:, 0:512], func=AF.Silu,
                                                   bias=chv[:, nt, 4:5]), reads=[BK[2 + hb], CONST], writes=[XC[b][nt]])

        def A_vg(b):
            for j in range(4):
                c = 4 * b + j
                tc0 = b * 512 + j * 128
                pvv, bk0 = (PB, 2) if j % 2 == 0 else (PA, 0)
                for ct in range(8):
                    S.op("pe", lambda e, ct=ct, j=j, pvv=pvv: e.matmul(
                        pvv[:, ct * 128:(ct + 1) * 128], lhsT=mxb[:, ct, 4 + j * 128:4 + (j + 1) * 128], rhs=wbd[:, 2, ct, :],
                        start=True, stop=True), reads=[MXB[ct], WBD], writes=[BK[bk0 + ct // 4]])
                S.op("act", lambda e, c=c, pvv=pvv: e.activation(out=vext[:, c, :, 0:256],
                                                               in_=pvv[:, :].rearrange("p (h e) -> p h e", h=4), func=AF.Copy),
                     reads=[BK[bk0], BK[bk0 + 1]], writes=[VX[c]])
                for k in range(16):
                    if k < 8:
                        lh, rd = xcT[:, k, tc0:tc0 + 128], XC[b][k]
                    else:
                        lh, rd = mxb[:, k - 8, 4 + j * 128:4 + (j + 1) * 128], MXB[k - 8]
                    S.op("pe", lambda e, k=k, lh=lh, j=j: e.matmul(P4[:, j * 8:(j + 1) * 8], lhsT=lh, rhs=wg[:, k, :],
                                                                   start=(k == 0), stop=(k == 15)), reads=[rd, WG], writes=[BK[4]])

        def A_gmath(b):
            S.op("dve", lambda e: e.tensor_tensor(out=gtm[:, :], in0=P4[:, 0:32], in1=bifb[:, :], op=ALU.add),
                 reads=[BK[4], CONST], writes=[GTM])
            g3 = gtm.rearrange("p (a b) -> p a b", a=4)
            ef3 = ef.rearrange("p (a b) -> p a b", a=4)
            lf3 = lfn.rearrange("p (a b) -> p a b", a=4)
            t13 = t1.rearrange("p (a b) -> p a b", a=4)
            t23 = t2.rearrange("p (a b) -> p a b", a=4)
            nb3 = nbg[:, 0:16].rearrange("p (a b) -> p a b", a=4)
            ngg3 = nbg[:, 16:32].rearrange("p (a b) -> p a b", a=4)
            S.op("act", lambda e: e.activation(out=ef3, in_=g3[:, :, 4:8], func=AF.Exp, scale=-1.0), reads=[GTM], writes=[GTM])
            S.op("act", lambda e: e.activation(out=lf3, in_=ef3, func=AF.Ln, bias=1.0), reads=[GTM], writes=[GTM])
            S.op("pe", lambda e: e.matmul(P4[:, 64:80], lhsT=tri, rhs=lfn[:, :], start=True, stop=True),
                 reads=[GTM, CONST], writes=[BK[4]])
            S.op("pe", lambda e: e.matmul(P4[:, 80:96], lhsT=ones, rhs=lfn[:, :], start=True, stop=True),
                 reads=[GTM, CONST], writes=[BK[4]])
            S.op("dve", lambda e: e.tensor_copy(out=nbg[:, :], in_=P4[:, 64:96]), reads=[BK[4]], writes=[GTM])
            S.op("dve", lambda e: e.tensor_tensor(out=t13, in0=g3[:, :, 0:4], in1=nb3, op=ALU.add), reads=[GTM], writes=[GTM])
            S.op("dve", lambda e: e.tensor_tensor(out=t23, in0=t13, in1=ngg3, op=ALU.subtract), reads=[GTM], writes=[GTM])
            bs = slice(4 * b, 4 * b + 4)
            S.op("act", lambda e: e.activation(out=Ax[:, bs, :], in_=t13, func=AF.Exp), reads=[GTM], writes=[GATE[b]])
            S.op("act", lambda e: e.activation(out=A2x[:, bs, :], in_=t23, func=AF.Exp), reads=[GTM], writes=[GATE[b]])
            S.op("act", lambda e: e.activation(out=EG[:, bs, :], in_=ngg3, func=AF.Exp, scale=-1.0), reads=[GTM], writes=[GATE[b]])
            S.op("act", lambda e: e.activation(out=ENB[:, bs, :], in_=nb3, func=AF.Exp, scale=2.0), reads=[GTM], writes=[GATE[b]])
            S.op("dve", lambda e: e.tensor_scalar(out=Ax[:, bs, :], in0=Ax[:, bs, :], scalar1=0.0625, scalar2=None, op0=ALU.mult),
                 reads=[GATE[b]], writes=[GATE[b]])
            S.op("dve", lambda e: e.tensor_scalar(out=A2x[:, bs, :], in0=A2x[:, bs, :], scalar1=0.0625, scalar2=None, op0=ALU.mult),
                 reads=[GATE[b]], writes=[GATE[b]])
            for j in range(4):
                S.op("dve", lambda e, j=j: e.tensor_tensor(out=ngt[:, :], in0=ngt[:, :], in1=ngg3[:, j, :], op=ALU.add),
                     reads=[GTM, NGT], writes=[NGT])

        def A_scan_k(b, j):
            c = 4 * b + j
            ks = c % 2
            tc0 = b * 512 + j * 128
            for pair in range(2):
                pk, bk = (P5, 5) if pair == 0 else (P4, 4)
                for q4 in range(4):
                    ct = 4 * pair + q4
                    S.op("pe", lambda e, ct=ct, q4=q4, pk=pk: e.matmul(
                        pk[:, q4 * 128:(q4 + 1) * 128], lhsT=xcT[:, ct, tc0:tc0 + 128], rhs=wbd[:, 1, ct, :],
                        start=True, stop=True), reads=[XC[b][ct], WBD], writes=[BK[bk]])
                for hh in range(2):
                    h = 2 * pair + hh
                    S.op("act", lambda e, h=h, hh=hh, pk=pk: e.activation(
                        out=k2[ks][:, h * 256:(h + 1) * 256], in_=pk[:, hh * 256:(hh + 1) * 256], func=AF.Identity,
                        scale=A2x[:, c, h:h + 1]), reads=[BK[bk], GATE[b]], writes=[K2[ks][pair]])

        def A_scan_u(b, j, h):
            c = 4 * b + j
            ks = c % 2
            pair = h // 2
            for dt in range(2):
                pu = P6 if dt == 0 else P7
                S.op("pe", lambda e, dt=dt, pu=pu: e.matmul(
                    pu[:, 0:257], lhsT=k2[ks][:, h * 256 + dt * 128:h * 256 + dt * 128 + 128],
                    rhs=vext[:, c, h, 0:257], start=True, stop=True), reads=[K2[ks][pair], VX[c]], writes=[BK[6 + dt]])
            S.op("dve", lambda e: e.scalar_tensor_tensor(
                out=Cst[:, h, :, 0:257], in0=Cst[:, h, :, 0:257], scalar=EG[:, c, h:h + 1],
                in1=P67[:, :].rearrange("p (a b) -> p a b", a=2)[:, :, 0:257],
                op0=ALU.mult, op1=ALU.add), reads=[CS[h], GATE[b], BK[6], BK[7]], writes=[CS[h]])

        def A_scan(b, j):
            A_scan_k(b, j)
            for h in range(4):
                A_scan_u(b, j, h)

        for nt in range(8):
            A_proj(-1, nt)
        for i in range(4):
            A_proj(0, 2 * i, "p")
            A_proj(0, 2 * i + 1, "p")
            A_proj(0, 2 * i, "c")
            A_proj(0, 2 * i + 1, "c")
        A_vg(0)
        deferred = []
        for b in range(4):
            for i in range(4):
                if b < 3:
                    if i == 0:
                        A_proj(b + 1, 0, "p")
                        A_proj(b + 1, 1, "p")
                        A_gmath(b)
                        A_scan_k(b, 0)
                        A_scan_u(b, 0, 0)
                        A_proj(b + 1, 0, "c")
                        A_scan_u(b, 0, 1)
                        A_proj(b + 1, 1, "c")
                        A_scan_u(b, 0, 2)
                        A_scan_u(b, 0, 3)
                    else:
                        A_scan_k(b, i)
                        A_scan_u(b, i, 0)
                        A_proj(b + 1, 2 * i, "p")
                        A_scan_u(b, i, 1)
                        A_proj(b + 1, 2 * i + 1, "p")
                        A_scan_u(b, i, 2)
                        A_proj(b + 1, 2 * i, "c")
                        A_scan_u(b, i, 3)
                        A_proj(b + 1, 2 * i + 1, "c")
                else:
                    if i == 0:
                        A_gmath(b)
                    deferred.append(lambda i=i: A_scan(3, i))
            if b < 3:
                A_vg(b + 1)

        load_w(0, wout_d[0:1024, :])

        def publish():
          for h in range(4):
              for dt in range(2):
                  S.op("dve", lambda e, h=h, dt=dt: e.tensor_copy(out=Cst[:, h, dt, 257:258], in_=ngt[:, h:h + 1]),
                       reads=[NGT, CS[h]], writes=[CS[h]])
              S.dma("sp", "cci%d" % h, lambda e, h=h: e.dma_start(out=cci_t[h].ap(), in_=Cst[:, h, :, :].rearrange("p a b -> p (a b)")),
                    reads=[CS[h]], writes=[CCI[h]])
              S.dma("pool", "cc%d" % h, lambda e, h=h: e.collective_compute(
                  "AllGather", ALU.bypass, replica_groups=[[0, 1, 2, 3], [4, 5, 6, 7]],
                  ins=[cci_t[h].ap().opt()], outs=[cco_t[h].ap().opt()]), reads=[CCI[h]], writes=[CCO[h]], inc=1)

        wk[0] = 0
        pxe = [take(528) for _ in range(2)]
        tA1 = take(528)
        tA = [tA1, tA1]
        tB1 = take(528)
        tB = [tB1, tB1]
        szp = [take(256).bitcast(BF16) for _ in range(2)]
        dTt2 = [take(512).bitcast(BF16).rearrange("p (k t) -> p k t", k=2) for _ in range(2)]
        ypT = take(2048).bitcast(BF16).rearrange("p (k t) -> p k t", k=8)
        xacc = [take(512) for _ in range(4)]
        tmp16 = take(16)
        PXE = [Tk(), Tk()]
        TA1 = Tk()
        TA = [TA1, TA1]
        TB1 = Tk()
        TB = [TB1, TB1]
        SZP = [Tk(), Tk()]
        DT2 = [[Tk(), Tk()], [Tk(), Tk()]]
        YP = [Tk() for _ in range(8)]
        XACC = [Tk() for _ in range(4)]
        STGB = PXE + [TA1, TB1] + SZP + DT2[0] + DT2[1] + YP + XACC
        for t in XACC:
            for o in K2[0] + K2[1]:
                for s_, i_ in list(o.w.items()) + list(o.r.items()):
                    if s_ not in t.w or t.w[s_].gid < i_.gid:
                        t.w[s_] = i_
        for t in STGB:
            for o in STGA:
                for s_, i_ in list(o.w.items()) + list(o.r.items()):
                    if s_ not in t.w or t.w[s_].gid < i_.gid:
                        t.w[s_] = i_

        for nt in range(8):
            proj_fm(1, nt, 0, HALO, nt % 2, 0, PAh[nt % 2])
            S.op("act", lambda e, nt=nt: e.activation(out=pxhalo[:, nt, :], in_=PAh[nt % 2][:, 0:16], func=AF.Copy),
                 reads=[BK[nt % 2]], writes=[PXH[nt]])

        def B_px(b, g, n2):
            dTt, DT = dTt2[g % 2], DT2[g % 2]
            c0 = HALO + 512 * b
            win = 2 ** (g + 1)
            nt = 2 * g + n2
            proj_fm(1, nt, c0, 512, n2, 0, PAh[n2])
            S.op("act", lambda e: e.activation(out=pxe[n2][:, 16:528], in_=PAh[n2][:, 0:512], func=AF.Copy),
                 reads=[BK[n2]], writes=[PXE[n2]])
            S.op("pool", lambda e: e.tensor_copy(out=pxe[n2][:, 0:16], in_=pxhalo[:, nt, :]),
                 reads=[PXH[nt], PXE[n2]], writes=[PXE[n2]])
            S.op("pool", lambda e: e.tensor_copy(out=pxhalo[:, nt, :], in_=pxe[n2][:, 512:528]),
                 reads=[PXE[n2]], writes=[PXH[nt]])
            cur, curT = pxe[n2], PXE[n2]
            v = 0
            for lev in range(g + 1):
                sh = 2 ** lev
                nv = v + sh
                dst, dstT = (tA[n2], TA[n2]) if lev % 2 == 0 else (tB[n2], TB[n2])
                eng = "dve" if lev % 2 == 0 else "pool"
                S.op(eng, lambda e, dst=dst, cur=cur, nv=nv, sh=sh: e.tensor_tensor(
                    out=dst[:, nv:528], in0=cur[:, nv:528], in1=cur[:, nv - sh:528 - sh], op=ALU.add),
                    reads=[curT], writes=[dstT])
                cur, curT, v = dst, dstT, nv
            S.op("dve", lambda e, cur=cur: e.scalar_tensor_tensor(
                out=dTt[:, n2, :], in0=cur[:, 16:528], scalar=1.0 / win, in1=pxe[n2][:, 16:528],
                op0=ALU.mult, op1=ALU.subtract), reads=[curT, PXE[n2]], writes=[DT[n2]])
            if b == 0:
                S.op("dve", lambda e, cur=cur: e.tensor_tensor(out=tmp16[:, :], in0=cur[:, 16:32], in1=invc[:, g, :], op=ALU.mult),
                     reads=[curT, CONST, DT[n2]], writes=[DT[n2]])
                S.op("dve", lambda e: e.tensor_tensor(out=dTt[:, n2, 0:16], in0=tmp16[:, :], in1=pxe[n2][:, 16:32], op=ALU.subtract),
                     reads=[PXE[n2], DT[n2]], writes=[DT[n2]])

        def B_pz(b, g, cot):
            c0 = HALO + 512 * b
            nt = 2 * g + cot
            proj_fm(2, nt, c0, 512, cot, 2, PBh[cot])
            S.op("act", lambda e: e.activation(out=szp[cot][:, :], in_=PBh[cot][:, 0:512], func=AF.Silu),
                 reads=[BK[2 + cot]], writes=[SZP[cot]])

        def B_pw(b, g, cot):
            dTt, DT = dTt2[g % 2], DT2[g % 2]
            nt = 2 * g + cot
            pv = P4 if cot == 0 else P5
            for cit in range(2):
                S.op("pe", lambda e, cit=cit: e.matmul(
                    pv[:, 0:512], lhsT=poolw[:, g, cit, cot * 128:(cot + 1) * 128], rhs=dTt[:, cit, :],
                    start=(cit == 0), stop=(cit == 1)), reads=[DT[0], DT[1], POOLW], writes=[BK[4 + cot]])
            S.op("dve", lambda e: e.scalar_tensor_tensor(
                out=ypT[:, nt, :], in0=pv[:, 0:512], scalar=chv[:, nt, 5:6], in1=szp[cot][:, :],
                op0=ALU.mult, op1=ALU.mult), reads=[BK[4 + cot], SZP[cot], CONST], writes=[YP[nt]])

        XB = xacc + [pxe[0][:, 0:512], pxe[1][:, 0:512], tA1[:, 0:512], tB1[:, 0:512]]
        XBT = XACC + [PXE[0], PXE[1], TA1, TB1]

        def B_ld(b, u, extra=()):
            j, half = divmod(u, 2)
            c = 4 * b + j
            xb = u if b == 3 else u % 4
            S.dma("sp", "xa%d" % xb, lambda e: e.dma_start(
                out=XB[xb][:, :], in_=x_d[HALO + c * 128:HALO + (c + 1) * 128, half * 512:(half + 1) * 512]),
                writes=[XBT[xb]] + list(extra))

        def B_out(b, u):
            j, half = divmod(u, 2)
            c = 4 * b + j
            xb = u if b == 3 else u % 4
            pv = P6 if half == 0 else P7
            for kt in range(8):
                S.op("pe", lambda e, kt=kt: e.matmul(
                    pv[:, 0:512], lhsT=ypT[:, kt, j * 128:(j + 1) * 128],
                    rhs=WS[0][:, kt, half * 512:(half + 1) * 512], start=(kt == 0), stop=(kt == 7)),
                    reads=[YP[kt], WST[0][kt]], writes=[BK[6 + half]])
            S.op("dve", lambda e: e.tensor_tensor(out=XB[xb][:, :], in0=pv[:, 0:512], in1=XB[xb][:, :], op=ALU.add),
                 reads=[BK[6 + half], XBT[xb]], writes=[XBT[xb]])
            S.dma("sp", "acc%d" % xb, lambda e: e.dma_start(
                out=acc_d[c * 128:(c + 1) * 128, half * 512:(half + 1) * 512], in_=XB[xb][:, :]),
                reads=[XBT[xb]], writes=[ACC[c]])
            if b < 3 and u + 4 < 8:
                B_ld(b, u + 4)

        seq = [(b, g) for b in range(4) for g in range(4)]
        B_px(0, 0, 0)
        B_px(0, 0, 1)
        for i, (b, g) in enumerate(seq):
            nxt = seq[i + 1] if i + 1 < len(seq) else None
            if g == 3 and b > 0:
                for u in range(4):
                    B_ld(b, u)
            B_pz(b, g, 0)
            if nxt:
                B_px(nxt[0], nxt[1], 0)
            B_pz(b, g, 1)
            if nxt:
                B_px(nxt[0], nxt[1], 1)
            if i == len(seq) - 1:
                for u in range(4, 8):
                    B_ld(3, u)
                load_w(1, win_d[:, 3072:4096])
                load_w(2, win_d[:, 4096:5120])
            if g == 0 and b > 0:
                for u in range(8):
                    B_out(b - 1, u)
            B_pw(b, g, 0)
            B_pw(b, g, 1)
            if deferred:
                deferred.pop(0)()
                if not deferred:
                    publish()
                    for u in range(4):
                        B_ld(0, u, extra=K2[0] + K2[1])
        for u in range(8):
            B_out(3, u)

        wk[0] = 0
        gathA = take(2064).rearrange("p (r n) -> p r n", r=4)
        gathB = work[:, 2064:4128].rearrange("p (r n) -> p r n", r=4)
        qT = [take(512).bitcast(BF16).rearrange("p (k t) -> p k t", k=8) for _ in range(2)]
        kT = [take(512).bitcast(BF16).rearrange("p (k t) -> p k t", k=8) for _ in range(2)]
        take(16)
        szT = take(1024).bitcast(BF16).rearrange("p (k t) -> p k t", k=8)
        so = [take(512).bitcast(BF16) for _ in range(2)]
        k2c1 = take(512).bitcast(BF16)
        k2c = [k2c1, k2c1]
        Wt = [take(64).bitcast(BF16) for _ in range(2)]
        hg = [take(256) for _ in range(2)]
        hn = [take(128).bitcast(BF16) for _ in range(2)]
        tt = [take(128) for _ in range(2)]
        cf = take(64)
        stt = take(64)
        wk_end = wk[0]
        wk[0] = 0
        ymT = [take(512).bitcast(BF16).rearrange("p (k t) -> p k t", k=8) for _ in range(2)]
        res = [take(512) for _ in range(2)]
        assert wk[0] <= 2064
        wk[0] = wk_end
        GA = Tk()
        GA2 = Tk()
        QT = [[Tk() for _ in range(2)] for _ in range(2)]
        KT = [[Tk() for _ in range(2)] for _ in range(2)]
        SZ = [Tk() for _ in range(4)]
        SO = [[Tk(), Tk()], [Tk(), Tk()]]
        K2C1 = [Tk(), Tk()]
        K2C = [K2C1, K2C1]
        WT = [Tk(), Tk()]
        HG = [Tk(), Tk()]
        HN = [Tk(), Tk()]
        TTt = [Tk(), Tk()]
        YM = [[Tk() for _ in range(4)] for _ in range(2)]
        RES = [Tk(), Tk()]
        CF = Tk()
        STAT = [Tk(), Tk()]
        allc = [GA, GA2, CF] + SZ + WT + HG + HN + TTt + RES + STAT
        for l_ in (QT, KT, SO, [K2C1], YM):
            for x_ in l_:
                allc += x_
        for t in allc:
            for o in STGB + STGA:
                for s_, i_ in list(o.w.items()) + list(o.r.items()):
                    if s_ not in t.w or t.w[s_].gid < i_.gid:
                        t.w[s_] = i_

        PT7 = P4[:, :].bitcast(BF16)[:, 512:768]
        PJ1 = PAh[1].bitcast(BF16)
        xcb = lambda c: XC[c // 4]

        def T_mo(c, half, part=None):
            p = c % 2
            ucol = HALO + c * 128
            if part in (None, "pe"):
                for kt in range(8):
                    S.op("pe", lambda e, kt=kt: e.matmul(
                        PAh[0][:, 0:512], lhsT=uT[:, kt, ucol:ucol + 128], rhs=WS[2][:, kt, half * 512:(half + 1) * 512],
                        start=(kt == 0), stop=(kt == 7)), reads=[WST[2][kt], UT[c + 1]], writes=[BK[0]])
            if part in (None, "ev"):
                S.op("act", lambda e: e.activation(out=so[p][:, half * 512:(half + 1) * 512], in_=PAh[0][:, 0:512], func=AF.Tanh, scale=0.5),
                     reads=[BK[0]], writes=[SO[p][half]])

        def T_k2(c, pair):
            p = c % 2
            tc0 = c * 128
            pv = PBh[pair]
            for q4 in range(4):
                ct = 4 * pair + q4
                S.op("pe", lambda e, ct=ct, q4=q4: e.matmul(
                    pv[:, q4 * 128:(q4 + 1) * 128], lhsT=xcT[:, ct, tc0:tc0 + 128], rhs=wbd[:, 1, ct, :],
                    start=True, stop=True), reads=[xcb(c)[ct], WBD], writes=[BK[2 + pair]])
            for hh in range(2):
                h = 2 * pair + hh
                S.op("act", lambda e, h=h, hh=hh: e.activation(
                    out=k2c[p][:, h * 256:(h + 1) * 256], in_=pv[:, hh * 256:(hh + 1) * 256], func=AF.Identity,
                    scale=A2x[:, c, h:h + 1]), reads=[BK[2 + pair], GATE[c // 4]], writes=[K2C[p][pair]])

        def T_qk(c, pair, which):
            p = c % 2
            tc0 = c * 128
            pv = PBh[which]
            dst, dstT = (qT, QT) if which == 0 else (kT, KT)
            for q4 in range(4):
                nt = 4 * pair + q4
                S.op("pe", lambda e, nt=nt, q4=q4: e.matmul(
                    pv[:, q4 * 128:(q4 + 1) * 128], lhsT=wbd[:, which, nt, :], rhs=xcT[:, nt, tc0:tc0 + 128],
                    start=True, stop=True), reads=[xcb(c)[nt], WBD], writes=[BK[2 + which]])
            S.op("act", lambda e: e.activation(out=dst[p][:, 4 * pair:4 * pair + 4, :],
                                               in_=pv[:, 0:512].rearrange("p (k t) -> p k t", k=4), func=AF.Copy),
                 reads=[BK[2 + which]], writes=[dstT[p][pair]])

        def T_szp(sbi, np_):
            c0 = HALO + sbi * 256
            hb = np_ % 2
            for q2 in range(2):
                nt = 2 * np_ + q2
                uts = list(range((c0 - HALO) // 128 + 1, (c0 + 255 - HALO) // 128 + 2))
                for kt in range(8):
                    S.op("pe", lambda e, kt=kt, nt=nt, q2=q2: e.matmul(
                        PBh[hb][:, q2 * 256:(q2 + 1) * 256], lhsT=WS[1][:, kt, nt * 128:(nt + 1) * 128],
                        rhs=uT[:, kt, c0:c0 + 256], start=(kt == 0), stop=(kt == 7)),
                        reads=[WST[1][kt]] + [UT[i] for i in uts], writes=[BK[2 + hb]])
            S.op("act", lambda e: e.activation(
                out=szT[:, 2 * np_:2 * np_ + 2, :], in_=PBh[hb][:, 0:512].rearrange("p (k t) -> p k t", k=2), func=AF.Silu),
                reads=[BK[2 + hb]], writes=[SZ[np_]])

        def T_sz(sbi):
            for np_ in range(4):
                T_szp(sbi, np_)

        def T_st(c, h):
            p = c % 2
            ws = h % 2
            for dt in range(2):
                S.op("pe", lambda e, dt=dt: e.matmul(
                    P4[:, 0:128], lhsT=kT[p][:, 2 * h + dt, :], rhs=qT[p][:, 2 * h + dt, :],
                    start=(dt == 0), stop=(dt == 1)), reads=[KT[p][h // 2], QT[p][h // 2]], writes=[BK[4]])
            S.op("dve", lambda e: e.scalar_tensor_tensor(
                out=Wt[ws][:, :], in0=P4[:, 0:128], scalar=Ax[:, c, h:h + 1], in1=mask01,
                op0=ALU.mult, op1=ALU.mult), reads=[BK[4], GATE[c // 4], CONST], writes=[WT[ws]])

        def T_nd(c, h):
            p = c % 2
            ws = h % 2
            pv, bk = (P5, 5) if h % 2 == 0 else (PAh[1], 1)
            S.op("pe", lambda e: e.matmul(pv[:, 0:257], lhsT=Wt[ws][:, :], rhs=vext[:, c, h, 0:257],
                                          start=True, stop=False), reads=[WT[ws], VX[c]], writes=[BK[bk]])
            for dt in range(2):
                S.op("pe", lambda e, dt=dt: e.matmul(
                    pv[:, 0:257], lhsT=qT[p][:, 2 * h + dt, :], rhs=Cbf[:, h, dt, 0:257],
                    start=False, stop=(dt == 1)), reads=[QT[p][h // 2], CB[h]], writes=[BK[bk]])

        def T_u(c, h, dt):
            p = c % 2
            pu = P6 if dt == 0 else P7
            S.op("pe", lambda e: e.matmul(
                pu[:, 0:257], lhsT=k2c[p][:, h * 256 + dt * 128:h * 256 + dt * 128 + 128],
                rhs=vext[:, c, h, 0:257], start=True, stop=True), reads=[K2C[p][h // 2], VX[c]], writes=[BK[6 + dt]])
            if dt == 1:
                S.op("dve", lambda e: e.scalar_tensor_tensor(
                    out=Cst[:, h, :, 0:257], in0=Cst[:, h, :, 0:257], scalar=EG[:, c, h:h + 1],
                    in1=P67[:, :].rearrange("p (a b) -> p a b", a=2)[:, :, 0:257],
                    op0=ALU.mult, op1=ALU.add), reads=[CS[h], GATE[c // 4], BK[6], BK[7]], writes=[CS[h]])

        def T_cb(c, h):
            S.op("pool", lambda e: e.tensor_copy(out=Cbf[:, h, :, 0:257], in_=Cst[:, h, :, 0:257]),
                 reads=[CS[h]], writes=[CB[h]])

        def T_stat(c, h):
            p = c % 2
            ws = h % 2
            pr = h // 2
            pv, bk = (P5, 5) if h % 2 == 0 else (PAh[1], 1)
            S.op("dve", lambda e: e.tensor_copy(out=stt[:, h:h + 1], in_=pv[:, 256:257]), reads=[BK[bk], STAT[pr]], writes=[STAT[pr]])
            S.op("dve", lambda e: e.scalar_tensor_tensor(
                out=hg[ws][:, :], in0=so[p][:, h * 256:(h + 1) * 256], scalar=1.0, in1=pv[:, 0:256],
                op0=ALU.add, op1=ALU.mult), reads=[BK[bk], SO[p][h // 2]], writes=[HG[ws]])
            st6 = stt[:, 32 + 8 * ws:38 + 8 * ws]
            S.op("dve", lambda e: e.bn_stats(out=st6, in_=hg[ws][:, :]), reads=[HG[ws], STAT[pr]], writes=[STAT[pr]])
            S.op("dve", lambda e: e.bn_aggr(out=stt[:, 8 + 2 * h:10 + 2 * h], in_=st6), reads=[STAT[pr]], writes=[STAT[pr]])

        def T_rstd(c, pr):
            hs = slice(2 * pr, 2 * pr + 2)
            dcol = stt[:, hs]
            mv = stt[:, 8:16].rearrange("p (h two) -> p h two", two=2)
            q_ = stt[:, 16 + 2 * pr:18 + 2 * pr]
            r_ = stt[:, 20 + 2 * pr:22 + 2 * pr]
            nb_ = stt[:, 24 + 2 * pr:26 + 2 * pr]
            S.op("dve", lambda e: e.tensor_tensor(out=dcol, in0=dcol, in1=dcol, op=ALU.mult), reads=[STAT[pr]], writes=[STAT[pr]])
            S.op("dve", lambda e: e.tensor_tensor(out=dcol, in0=dcol, in1=ENB[:, c, hs], op=ALU.max),
                 reads=[STAT[pr], GATE[c // 4]], writes=[STAT[pr]])
            S.op("dve", lambda e: e.scalar_tensor_tensor(out=q_, in0=dcol, scalar=4.0 * EPS, in1=mv[:, hs, 1], op0=ALU.mult, op1=ALU.add),
                 reads=[STAT[pr]], writes=[STAT[pr]])
            S.op("act", lambda e: e.activation(out=q_, in_=q_, func=AF.Sqrt), reads=[STAT[pr]], writes=[STAT[pr]])
            S.op("dve", lambda e: e.reciprocal(out=r_, in_=q_), reads=[STAT[pr]], writes=[STAT[pr]])
            S.op("dve", lambda e: e.scalar_tensor_tensor(out=nb_, in0=mv[:, hs, 0], scalar=-1.0, in1=r_, op0=ALU.mult, op1=ALU.mult),
                 reads=[STAT[pr]], writes=[STAT[pr]])

        def T_hn(c, h):
            ws = h % 2
            pr = h // 2
            S.op("act", lambda e: e.activation(out=hn[ws][:, :], in_=hg[ws][:, :], func=AF.Identity,
                                               scale=stt[:, 20 + h:21 + h], bias=stt[:, 24 + h:25 + h]),
                 reads=[HG[ws], STAT[pr]], writes=[HN[ws]])

        def T_tr(c, h):
            p = c % 2
            ws = h % 2
            lo = (c % 2) * 128
            tc0 = c * 128
            for dt in range(2):
                S.op("pe", lambda e, dt=dt: e.transpose(out=PT7[:, dt * 128:(dt + 1) * 128],
                                                        in_=hn[ws][:, dt * 128:(dt + 1) * 128], identity=identb[:, :]),
                     reads=[HN[ws], CONST], writes=[BK[4]])
            for dt in range(2):
                nt = 2 * h + dt
                S.op("act", lambda e, dt=dt, nt=nt: e.activation(out=tt[dt][:, :], in_=PT7[:, dt * 128:(dt + 1) * 128],
                                                                 func=AF.Identity, scale=chv[:, nt, 7:8]),
                     reads=[BK[4], CONST], writes=[TTt[dt]])
                S.op("dve", lambda e, dt=dt, nt=nt: e.scalar_tensor_tensor(
                    out=tt[dt][:, :], in0=xcT[:, nt, tc0:tc0 + 128], scalar=chv[:, nt, 6:7], in1=tt[dt][:, :],
                    op0=ALU.mult, op1=ALU.add), reads=[xcb(c)[nt], TTt[dt], CONST], writes=[TTt[dt]])
                S.op("pool", lambda e, dt=dt, nt=nt: e.tensor_tensor(
                    out=ymT[p][:, nt, :], in0=tt[dt][:, :], in1=szT[:, nt, lo:lo + 128], op=ALU.mult),
                    reads=[TTt[dt], SZ[nt // 2]], writes=[YM[p][h]])

        def T_res(c):
            for half in range(2):
                S.dma("sp", "res%d" % half, lambda e, half=half: e.dma_start(
                    out=res[half][:, :], in_=acc_d[c * 128:(c + 1) * 128, half * 512:(half + 1) * 512]),
                    reads=[ACC[c]], writes=[RES[half]])

        def T_out(c, half, part=None):
            p = c % 2
            if part in (None, "pe"):
                for kt in range(8):
                    S.op("pe", lambda e, kt=kt: e.matmul(
                        PAh[0][:, 0:512], lhsT=ymT[p][:, kt, :], rhs=WS[0][:, kt, half * 512:(half + 1) * 512],
                        start=(kt == 0), stop=(kt == 7)), reads=[YM[p][kt // 2], WST[0][kt]], writes=[BK[0]])
            if part in (None, "ev"):
                S.op("dve", lambda e: e.tensor_tensor(out=res[half][:, :], in0=PAh[0][:, 0:512], in1=res[half][:, :], op=ALU.add),
                     reads=[BK[0], RES[half]], writes=[RES[half]])

        def T_fin(c):
            p = c % 2
            fs = cf[:, 40:42]
            fr = cf[:, 42:43]
            for half in range(2):
                S.op("act", lambda e, half=half: e.activation(out=PAh[0][:, 0:512], in_=res[half][:, :],
                                                              func=AF.Square, accum_out=fs[:, half:half + 1]),
                     reads=[RES[half], CF], writes=[BK[0], CF])
            S.op("dve", lambda e: e.tensor_tensor(out=fr, in0=fs[:, 0:1], in1=fs[:, 1:2], op=ALU.add), reads=[CF], writes=[CF])
            S.op("act", lambda e: e.activation(out=fr, in_=fr, func=AF.Sqrt, scale=1.0 / D, bias=EPS), reads=[CF], writes=[CF])
            S.op("dve", lambda e: e.reciprocal(out=fr, in_=fr), reads=[CF], writes=[CF])
            for half in range(2):
                S.op("dve", lambda e, half=half: e.scalar_tensor_tensor(
                    out=res[half][:, :], in0=res[half][:, :], scalar=fr, in1=nwb[:, half * 512:(half + 1) * 512],
                    op0=ALU.mult, op1=ALU.mult), reads=[RES[half], CF, NWB], writes=[RES[half]])
                S.dma("sp", "out%d" % half, lambda e, half=half: e.dma_start(
                    out=out_d[c * 128:(c + 1) * 128, half * 512:(half + 1) * 512], in_=res[half][:, :]),
                    reads=[RES[half]], writes=[OUT])

        def setup(c):
            T_mo(c, 0)
            T_mo(c, 1)
            T_k2(c, 0)
            T_k2(c, 1)
            T_qk(c, 0, 0)
            T_qk(c, 0, 1)
            T_qk(c, 1, 0)
            T_qk(c, 1, 1)

        T_sz(0)
        T_mo(0, 0)
        T_mo(0, 1)
        T_k2(0, 0)
        T_k2(0, 1)
        for h in range(4):
            gath, GAh = (gathA, GA) if h % 2 == 0 else (gathB, GA2)
            S.dma("sp", "ga%d" % (h % 2), lambda e, h=h, gath=gath: e.dma_start(out=gath[:, :, :], in_=cco_t[h].ap().rearrange("(r p) n -> p r n", p=128)),
                  reads=[CCO[h]], writes=[GAh])
            tm = cf[:, 0:16].rearrange("p (a b) -> p a b", a=4)
            for i in range(4):
                S.op("dve", lambda e, i=i, tm=tm, gath=gath: e.tensor_tensor(out=tm[:, i, :], in0=brb[:, i, :], in1=gath[:, :, 257], op=ALU.mult),
                     reads=[GAh, CONST, CF], writes=[CF])
            S.op("dve", lambda e, tm=tm: e.tensor_reduce(out=cf[:, 16:20], in_=tm, axis=AX.X, op=ALU.add), reads=[CF], writes=[CF])
            S.op("act", lambda e: e.activation(out=cf[:, 20:24], in_=cf[:, 16:20], func=AF.Exp, scale=-1.0), reads=[CF], writes=[CF])
            S.op("dve", lambda e: e.tensor_tensor(out=cf[:, 24:28], in0=cf[:, 20:24], in1=cmask[:, :], op=ALU.mult),
                 reads=[CF, CONST], writes=[CF])
            cflat = Cst[:, h, :, :].rearrange("p a b -> p (a b)")
            S.op("dve", lambda e, cflat=cflat, gath=gath: e.tensor_scalar(out=cflat, in0=gath[:, 0, :], scalar1=cf[:, 24:25], scalar2=None, op0=ALU.mult),
                 reads=[GAh, CF, CS[h]], writes=[CS[h]])
            for i in range(1, 4):
                S.op("dve", lambda e, cflat=cflat, i=i, gath=gath: e.scalar_tensor_tensor(
                    out=cflat, in0=gath[:, i, :], scalar=cf[:, 24 + i:25 + i], in1=cflat, op0=ALU.mult, op1=ALU.add),
                    reads=[GAh, CF, CS[h]], writes=[CS[h]])
            S.op("act", lambda e, h=h: e.activation(out=Cbf[:, h, :, :], in_=Cst[:, h, :, :], func=AF.Copy),
                 reads=[CS[h]], writes=[CB[h]])
        for l_ in YM:
            for t in l_ + RES:
                for s_, i_ in list(GA.w.items()) + list(GA.r.items()):
                    if s_ not in t.w or t.w[s_].gid < i_.gid:
                        t.w[s_] = i_
        for l_ in QT + KT:
            for t in l_:
                for s_, i_ in list(GA2.w.items()) + list(GA2.r.items()):
                    if s_ not in t.w or t.w[s_].gid < i_.gid:
                        t.w[s_] = i_
        T_qk(0, 0, 0)
        T_qk(0, 0, 1)
        T_qk(0, 1, 0)
        T_qk(0, 1, 1)

        load_w(0, wout_d[1024:2048, :], after=[CB[3]])

        def front(c, h, piece=None):
            T_nd(c, h)
            T_u(c, h, 0)
            T_u(c, h, 1)
            if h < 3:
                T_st(c, h + 1)
            if piece is not None:
                piece("pe")

        def back(c, h, piece=None):
            T_stat(c, h)
            T_cb(c, h)
            if piece is not None:
                piece("ev")

        for c in range(16):
            nx = c + 1 < 16
            pc = [None] * 4
            if nx:
                pc[0] = lambda part, c=c: T_mo(c + 1, 0, part)
                pc[1] = lambda part, c=c: T_mo(c + 1, 1, part)
            if c > 0:
                pc[2] = lambda part, c=c: T_out(c - 1, 0, part)
                pc[3] = lambda part, c=c: T_out(c - 1, 1, part)
            T_st(c, 0)
            front(c, 0, pc[0])
            if c > 0:
                T_hn(c - 1, 2)
                T_hn(c - 1, 3)
            back(c, 0, pc[0])
            if nx:
                T_qk(c + 1, 0, 0); T_qk(c + 1, 0, 1)
            front(c, 1, pc[1])
            back(c, 1, pc[1])
            if c > 0:
                T_tr(c - 1, 2)
                T_tr(c - 1, 3)
                if (c - 1) % 2 == 1:
                    T_szp(c // 2, 2)
                    T_szp(c // 2, 3)
            if nx:
                T_qk(c + 1, 1, 0); T_qk(c + 1, 1, 1)
            T_rstd(c, 0)
            front(c, 2, pc[2])
            T_hn(c, 0)
            T_hn(c, 1)
            back(c, 2, pc[2])
            if nx:
                T_k2(c + 1, 0)
            front(c, 3, pc[3])
            back(c, 3, pc[3])
            T_tr(c, 0)
            T_tr(c, 1)
            if c > 0:
                T_fin(c - 1)
            if nx:
                T_k2(c + 1, 1)
            T_rstd(c, 1)
            T_res(c)
            if c % 2 == 1 and nx:
                T_szp((c + 1) // 2, 0)
                T_szp((c + 1) // 2, 1)
        T_hn(15, 2)
        T_hn(15, 3)
        T_tr(15, 2)
        T_tr(15, 3)
        T_out(15, 0)
        T_out(15, 1)
        T_fin(15)
        if DEBUG:
            dbg_xc = nc.dram_tensor("dbg_xc", [128, 8 * 2048], BF16, kind="ExternalOutput").ap()
            dbg_g = nc.dram_tensor("dbg_g", [128, 256], F32, kind="ExternalOutput").ap()
            dbg_v = nc.dram_tensor("dbg_v", [128, 16 * 4 * 258], BF16, kind="ExternalOutput").ap()
            allxc = [t for l_ in XC for t in l_]
            S.dma("sp", "dbg1", lambda e: e.dma_start(out=dbg_xc, in_=xcT[:, :, :].rearrange("p k t -> p (k t)")), reads=allxc, writes=[OUT])
            for i_, arr in enumerate([Ax, A2x, EG, ENB]):
                S.dma("sp", "dbg2", lambda e, i_=i_, arr=arr: e.dma_start(out=dbg_g[:, i_ * 64:(i_ + 1) * 64], in_=arr[:, :, :].rearrange("p a b -> p (a b)")),
                      reads=GATE, writes=[OUT])
            S.dma("sp", "dbg3", lambda e: e.dma_start(out=dbg_v, in_=vext[:, :, :, :].rearrange("p a b c -> p (a b c)")), reads=VX, writes=[OUT])
        S.join("sp", [OUT])
        S.emit(nc)
    return nc


def _block_diag(w):
    m = np.zeros((1024, 128), np.float32)
    for n in range(256):
        ct, nn = divmod(n, 32)
        m[ct * 128 + 4 * nn:ct * 128 + 4 * nn + 4, 4 * nn:4 * nn + 4] = w[n]
    return m


_NC_CACHE = {}


def kernel(x, norm_w, w_in, pool_w, pool_scale, conv_w, conv_b, w_q, w_k, w_v, w_if, b_if,
           mh_norm_w, m_skip, w_out, final_norm_w):
    f = lambda a: np.ascontiguousarray(np.asarray(a, dtype=np.float32))
    x = f(x)
    B, SEQ, _ = x.shape
    nseg = SEQ // T
    assert B * nseg == NCORES
    if "nc" not in _NC_CACHE:
        _NC_CACHE["nc"] = build_nc()
    nc = _NC_CACHE["nc"]

    chv = np.zeros((128, 8, 8), np.float32)
    cw = f(conv_w)[0]
    for j in range(4):
        chv[:, :, j] = cw[j].reshape(8, 128).T
    chv[:, :, 4] = f(conv_b)[0].reshape(8, 128).T
    chv[:, :, 5] = f(pool_scale)[0].reshape(8, 128).T
    chv[:, :, 6] = f(m_skip)[0].reshape(8, 128).T
    chv[:, :, 7] = f(mh_norm_w)[0].reshape(8, 128).T
    cst = np.zeros((128, 4, 128), np.float32)
    cst[:, 0, :] = np.eye(128, dtype=np.float32)
    cst[:, 1, :] = np.triu(np.ones((128, 128), np.float32))
    cst[:, 2, :] = 1.0
    cst[:, 3, :] = np.triu(np.ones((128, 128), np.float32))
    shared = {
        "w_in": f(w_in)[0], "w_out": f(w_out)[0], "pool_w": f(pool_w)[0].reshape(1024, 256),
        "wbd": np.ascontiguousarray(np.stack([_block_diag(f(w_q)[0]), _block_diag(f(w_k)[0]), _block_diag(f(w_v)[0])], axis=0)
                                    .reshape(3, 8, 128, 128).transpose(2, 0, 1, 3).reshape(128, 3 * 8 * 128)),
        "wbdT": np.ascontiguousarray(np.stack([_block_diag(f(w_q)[0]), _block_diag(f(w_k)[0]), _block_diag(f(w_v)[0])], axis=0)
                                     .reshape(3, 8, 128, 128).transpose(3, 0, 1, 2).reshape(128, 3 * 8 * 128)),
        "w_if": np.ascontiguousarray(f(w_if)[0].reshape(24, 128, 8).transpose(1, 0, 2).reshape(128, 192)), "bifb": np.ascontiguousarray(np.broadcast_to(np.tile(f(b_if)[0].reshape(1, 8), (1, 4)), (128, 32))), "chv": chv.reshape(128, 64),
        "normw_b": np.ascontiguousarray(np.broadcast_to(f(norm_w)[0][None, :], (128, D))),
        "fnw_b": np.ascontiguousarray(np.broadcast_to(f(final_norm_w)[None, :], (128, D))),
        "cst": cst.reshape(128, 512),
        "identb_in": np.eye(128, dtype=np.float32).astype(ml_dtypes.bfloat16),
    }
    in_maps = []
    for r in range(NCORES):
        b, j = divmod(r, nseg)
        start = j * T
        xs = np.zeros((TT, D), np.float32)
        if j == 0:
            xs[HALO:] = x[b, 0:T]
        else:
            xs[:] = x[b, start - HALO:start + T]
        invc = np.zeros((4, 16), np.float32)
        for g in range(4):
            win = 2 ** (g + 1)
            for i in range(16):
                invc[g, i] = 1.0 / min(start + i + 1, win)
        brb = np.zeros((4, 4), np.float32)
        cm = np.zeros((4,), np.float32)
        for i in range(4):
            cm[i] = 1.0 if i < j else 0.0
            for l in range(4):
                brb[i, l] = 1.0 if (i < l < j) else 0.0
        m = dict(shared)
        m["x"] = xs
        m["invcnt"] = np.ascontiguousarray(np.broadcast_to(invc.reshape(1, 64), (128, 64)))
        m["brb"] = np.ascontiguousarray(np.broadcast_to(brb.reshape(1, 16), (128, 16)))
        m["cmask"] = np.ascontiguousarray(np.broadcast_to(cm.reshape(1, 4), (128, 4)))
        in_maps.append(m)
    res = run_bass_kernel_spmd(nc, in_maps, core_ids=list(range(NCORES)))
    if DEBUG:
        _NC_CACHE["res"] = res
    out = np.zeros((B, SEQ, D), np.float32)
    for r in range(NCORES):
        b, j = divmod(r, nseg)
        out[b, j * T:(j + 1) * T] = res.results[r]["out"]
    return out
```

```python
from contextlib import ExitStack
import numpy as np
import ml_dtypes
import concourse.bass as bass
import concourse.mybir as mybir
from concourse.bass_utils import run_bass_kernel_spmd

F32 = mybir.dt.float32
BF16 = mybir.dt.bfloat16
AF = mybir.ActivationFunctionType
ALU = mybir.AluOpType
AX = mybir.AxisListType

NCORES = 8
T = 2048
HALO = 16
TT = T + HALO
D = 1024
EPS = 1e-6
ENGS = ["pe", "act", "dve", "pool", "sp"]
STRICT_ENGS = ("dve", "act")


class Tk:
    __slots__ = ("w", "r", "excl")

    def __init__(self, excl=False):
        self.w = {}
        self.r = {}
        self.excl = excl


class Ins:
    __slots__ = ("eng", "stream", "fn", "deps", "needed", "seq", "gid", "isdma", "inc")


class Sched:
    def __init__(self):
        self.order = {e: [] for e in ENGS}
        self.dmacnt = {}
        self.gid = 0

    def _new(self, eng, stream, fn, isdma):
        i = Ins()
        i.eng, i.stream, i.fn, i.isdma = eng, stream, fn, isdma
        i.deps = set()
        i.needed = False
        i.seq = 0
        i.inc = 16
        self.gid += 1
        i.gid = self.gid
        return i

    def _link(self, ins, reads, writes):
        xr = [t for t in reads if t.excl and t not in writes]
        if xr:
            writes = list(writes) + xr
        raw, oth = set(), set()
        for t in reads:
            raw.update(t.w.values())
        for t in writes:
            oth.update(t.w.values())
            oth.update(t.r.values())
        deps = set()
        for d in raw:
            if d.stream == ins.stream and ins.stream == "pe":
                continue
            deps.add(d)
        for d in oth:
            if d.stream == ins.stream and not ins.isdma and ins.stream not in STRICT_ENGS:
                continue
            deps.add(d)
        deps.discard(ins)
        ins.deps = deps
        for d in deps:
            d.needed = True
        for t in reads:
            t.r[ins.stream] = ins
        for t in writes:
            t.w = {ins.stream: ins}
            t.r = {}
        self.order[ins.eng].append(ins)
        return ins

    def op(self, eng, fn, reads=(), writes=()):
        return self._link(self._new(eng, eng, fn, False), reads, writes)

    def dma(self, qeng, key, fn, reads=(), writes=(), inc=16):
        ins = self._new(qeng, ("d", key), fn, True)
        ins.inc = inc
        n = self.dmacnt.get(key, 0) + 1
        self.dmacnt[key] = n
        ins.seq = n
        return self._link(ins, reads, writes)

    def join(self, eng, tiles):
        return self._link(self._new(eng, eng, None, False), (), tiles)

    def emit(self, nc):
        for e in ENGS:
            n = 0
            for ins in self.order[e]:
                if ins.isdma:
                    continue
                if ins.needed and ins.fn is not None:
                    n += 1
                ins.seq = n
        with ExitStack() as st:
            sems = {}
            for e in ENGS:
                sems[e] = st.enter_context(nc.semaphore("s_" + e))
            for i, k in enumerate(self.dmacnt):
                sems[("d", k)] = st.enter_context(nc.semaphore("d%d" % i))
            block = st.enter_context(nc.Block())

            def run(eng_name, e):
                known = {}
                for ins in self.order[eng_name]:
                    for d in sorted(ins.deps, key=lambda z: z.gid):
                        val = d.seq * d.inc if d.isdma else d.seq
                        if val <= 0 or known.get(d.stream, 0) >= val:
                            continue
                        e.wait_ge(sems[d.stream], val)
                        known[d.stream] = val
                    if ins.fn is None:
                        continue
                    r = ins.fn(e)
                    if ins.isdma:
                        r.then_inc(sems[ins.stream], ins.inc)
                    elif ins.needed:
                        r.then_inc(sems[ins.stream], 1)

            @block.tensor
            def _(e):
                run("pe", e)

            @block.scalar
            def _(e):
                run("act", e)

            @block.vector
            def _(e):
                run("dve", e)

            @block.gpsimd
            def _(e):
                run("pool", e)

            @block.sync
            def _(e):
                run("sp", e)


DEBUG = False


def build_nc():
    nc = bass.Bass("TRN2", target_bir_lowering=False)
    S = Sched()

    def din(name, shape):
        return nc.dram_tensor(name, shape, F32, kind="ExternalInput").ap()

    x_d = din("x", [TT, D])
    win_d = din("w_in", [D, 5120])
    wout_d = din("w_out", [2048, D])
    poolw_d = din("pool_w", [1024, 256])
    wbd_d = din("wbd", [128, 3 * 8 * 128])
    wbdT_d = din("wbdT", [128, 3 * 8 * 128])
    wif_d = din("w_if", [128, 24 * 8])
    bifb_d = din("bifb", [128, 32])
    chv_d = din("chv", [128, 64])
    nw_d = din("normw_b", [128, D])
    fnw_d = din("fnw_b", [128, D])
    cst_d = din("cst", [128, 512])
    identb_d = nc.dram_tensor("identb_in", [128, 128], BF16, kind="ExternalInput").ap()
    invc_d = din("invcnt", [128, 64])
    brb_d = din("brb", [128, 16])
    cmask_d = din("cmask", [128, 4])
    out_d = nc.dram_tensor("out", [T, D], F32, kind="ExternalOutput").ap()
    acc_d = nc.dram_tensor("acc_scr", [T, D], F32).ap()
    cci_t = [nc.dram_tensor("cci%d" % h, [128, 516], F32) for h in range(4)]
    cco_t = [nc.dram_tensor("cco%d" % h, [512, 516], F32) for h in range(4)]

    with ExitStack() as st:
        def sb(name, shape, dt):
            return st.enter_context(nc.sbuf_tensor(name, shape, dt))

        def ps(name, shape):
            return st.enter_context(nc.psum_tensor(name, shape, F32))

        uT = sb("uT", [128, 8, TT], BF16)
        xcT = sb("xcT", [128, 8, T], BF16)
        vext = sb("vext", [128, 16, 4, 258], BF16)
        WS = [sb("ws%d" % i, [128, 8, 1024], BF16) for i in range(3)]
        poolw = sb("poolw", [128, 4, 2, 256], BF16)
        wbd = sb("wbd_sb", [128, 3, 8, 128], BF16)
        wif = sb("wif_sb", [128, 24, 8], BF16)
        bifb = sb("bifb_sb", [128, 32], F32)
        wg = sb("wg", [128, 16, 8], BF16)
        chv = sb("chv_sb", [128, 8, 8], F32)
        nwb = sb("nwb", [128, D], F32)
        identb = sb("identb", [128, 128], BF16)
        cstf = sb("cstf", [128, 4, 128], F32)
        invc = sb("invc", [128, 4, 16], F32)
        brb = sb("brb_sb", [128, 4, 4], F32)
        cmask = sb("cmask_sb", [128, 4], F32)
        Ax = sb("Ax", [128, 16, 4], F32)
        A2x = sb("A2x", [128, 16, 4], F32)
        EG = sb("EG", [128, 16, 4], F32)
        ENB = sb("ENB", [128, 16, 4], F32)
        Cst = sb("Cst", [128, 4, 2, 258], F32)
        Cbf = sb("Cbf", [128, 4, 2, 258], BF16)
        mxhalo = sb("mxhalo", [128, 8, 4], F32)
        pxhalo = sb("pxhalo", [128, 8, 16], F32)
        ngt = sb("ngt", [128, 4], F32)
        small = sb("small", [128, 16], F32)
        WN = 8170
        work = sb("work", [128, WN], F32)

        identf = cstf[:, 0, :]
        tri = cstf[:, 1, :]
        ones = cstf[:, 2, :]
        mask01 = cstf[:, 3, :]

        PA = ps("PA", [128, 1024])
        PB = ps("PB", [128, 1024])
        P4 = ps("P4", [128, 512])
        P5 = ps("P5", [128, 512])
        P67 = ps("P67", [128, 1024])
        P6 = P67[:, 0:512]
        P7 = P67[:, 512:1024]
        BK = [Tk(True) for _ in range(8)]

        UT = [Tk() for _ in range(17)]
        XC = [[Tk() for _ in range(8)] for _ in range(4)]
        VX = [Tk() for _ in range(16)]
        WST = [[Tk() for _ in range(8)] for _ in range(3)]
        CONST = Tk()
        NWB = Tk()
        GATE = [Tk() for _ in range(4)]
        CS = [Tk() for _ in range(4)]
        CB = [Tk() for _ in range(4)]
        MXH = [Tk() for _ in range(8)]
        PXH = [Tk() for _ in range(8)]
        NGT = Tk()
        OUT = Tk()
        ACC = [Tk() for _ in range(16)]
        CCI = [Tk() for _ in range(4)]
        CCO = [Tk() for _ in range(4)]

        wk = [0]

        def take(n):
            a = work[:, wk[0]:wk[0] + n]
            wk[0] += n
            assert wk[0] <= WN, wk[0]
            return a

        xt = [take(1024) for _ in range(3)]
        sqj = take(512).bitcast(BF16)
        ub = [take(512).bitcast(BF16) for _ in range(2)]
        XT = [Tk(), Tk(), Tk()]
        SQ = Tk()
        UB = [Tk(), Tk()]
        SS = [Tk(), Tk(), Tk()]
        def s0_load(k, ti):
            rows = 16 if ti < 0 else 128
            r0 = 0 if ti < 0 else HALO + 128 * ti
            xs = k % 3
            S.dma("sp", "x%d" % xs, lambda e: e.dma_start(out=xt[xs][:rows, :], in_=x_d[r0:r0 + rows, :]), writes=[XT[xs]])

        s0_load(0, -1)
        s0_load(1, 0)
        s0_load(2, 1)
        def ld(key, q, out, in_, w):
            ins = S.dma(q, key, lambda e: e.dma_start(out=out, in_=in_), writes=[] if w and w[0] is CONST else w)
            if w and w[0] is CONST:
                CONST.w[ins.stream] = ins

        ld("c0", "sp", nwb[:, :], nw_d, [NWB])
        ld("c1", "sp", identb[:, :], identb_d, [CONST])
        ld("c2", "sp", cstf[:, :, :], cst_d.rearrange("p (a b) -> p a b", a=4), [CONST])
        ld("c3", "sp", chv[:, :, :], chv_d.rearrange("p (a b) -> p a b", a=8), [CONST])
        ld("c4", "sp", bifb[:, :], bifb_d, [CONST])
        ld("c5", "sp", invc[:, :, :], invc_d.rearrange("p (a b) -> p a b", a=4), [CONST])
        ld("c6", "sp", brb[:, :, :], brb_d.rearrange("p (a b) -> p a b", a=4), [CONST])
        ld("c7", "sp", cmask[:, :], cmask_d, [CONST])
        def load_w(slot, src2d, after=()):
            for kt in range(8):
                S.dma("pool", "w%d_%d" % (slot, kt),
                      lambda e, kt=kt: e.dma_start(out=WS[slot][:, kt, :], in_=src2d[kt * 128:(kt + 1) * 128, :]),
                      reads=list(after), writes=[WST[slot][kt]])

        S.op("pool", lambda e: e.memset(Cst[:, :, :, :], 0.0), writes=CS)
        S.op("pool", lambda e: e.memset(vext[:, :, :, 256:258], 1.0), writes=VX)
        S.op("pool", lambda e: e.memset(ngt[:, :], 0.0), writes=[NGT])

        WBD = Tk()
        WIF = Tk()
        POOLW = Tk()
        WBT = Tk()
        DGT = Tk()
        WG = Tk()
        WSC = [Tk() for _ in range(4)]
        wbdT = work[:, 5808:5808 + 1536].bitcast(BF16).rearrange("p (w c n) -> p w c n", w=3, c=8)
        dg = work[:, 3760:3760 + 2048].bitcast(BF16).rearrange("p (c j n) -> p c j n", c=8, j=4)

        def load_wmx(cb, after):
            S.dma("pool", "wc%d" % cb, lambda e: e.dma_start(
                out=WS[0][:, :, cb * 256:(cb + 1) * 256],
                in_=win_d[:, 2048 + cb * 256:2048 + (cb + 1) * 256].rearrange("(kt p) n -> p kt n", p=128)),
                reads=list(after), writes=[WSC[cb]])

        def ld_after(key, out, in_, w, after):
            S.dma("pool", key, lambda e: e.dma_start(out=out, in_=in_), reads=list(after), writes=w)

        load_wmx(0, [])
        ld("c9", "pool", wif[:, :, :], wif_d.rearrange("p (k n) -> p k n", k=24), [WIF])
        late_loads = {
            3: lambda: load_wmx(1, [UT[3]]),
            6: lambda: ld_after("c11", wbdT, wbdT_d.rearrange("p (w c n) -> p w c n", w=3, c=8), [WBT], [UT[6]]),
            8: lambda: load_wmx(2, [UT[8]]),
            10: lambda: ld_after("c8", wbd[:, :, :, :], wbd_d.rearrange("p (w c n) -> p w c n", w=3, c=8), [WBD], [UT[10]]),
            12: lambda: load_wmx(3, [UT[12]]),
            16: lambda: ld_after("c10", poolw[:, :, :, :], poolw_d.rearrange("(g c p) n -> p g c n", g=4, c=2), [POOLW], [UT[16]]),
        }

        PTbs = [P6.bitcast(BF16), P7.bitcast(BF16)]

        def s0_front(k, ti):
            rows = 16 if ti < 0 else 128
            r0 = 0 if ti < 0 else HALO + 128 * ti
            sl = k % 2
            xs = k % 3
            PTb = PTbs[sl]
            ss = small[:, xs * 2:xs * 2 + 1]
            rs = small[:, xs * 2 + 1:xs * 2 + 2]
            if k >= 3:
                s0_load(k, ti)
            S.op("act", lambda e: e.activation(out=sqj[:rows, :], in_=xt[xs][:rows, :], func=AF.Square, accum_out=ss[:rows, :]),
                 reads=[XT[xs]], writes=[SQ, SS[xs]])
            S.op("act", lambda e: e.activation(out=rs[:rows, :], in_=ss[:rows, :], func=AF.Sqrt, scale=1.0 / D, bias=EPS),
                 reads=[SS[xs]], writes=[SS[xs]])
            S.op("dve", lambda e: e.reciprocal(out=rs[:rows, :], in_=rs[:rows, :]), reads=[SS[xs]], writes=[SS[xs]])
            S.op("dve", lambda e: e.scalar_tensor_tensor(
                out=ub[sl][:rows, :], in0=xt[xs][:rows, :], scalar=rs[:rows, :], in1=nwb[:rows, :],
                op0=ALU.mult, op1=ALU.mult), reads=[XT[xs], SS[xs], NWB], writes=[UB[sl]])
            for kt in range(8):
                S.op("pe", lambda e, kt=kt: e.transpose(
                    out=PTb[:, kt * 128:kt * 128 + rows], in_=ub[sl][:rows, kt * 128:(kt + 1) * 128],
                    identity=identb[:rows, :rows]), reads=[UB[sl], CONST], writes=[BK[6 + sl]])

        def s0_back(k, ti):
            rows = 16 if ti < 0 else 128
            r0 = 0 if ti < 0 else HALO + 128 * ti
            sl = k % 2
            PTb = PTbs[sl]
            S.op("act", lambda e: e.activation(
                out=uT[:, :, r0:r0 + rows], in_=PTb.rearrange("p (k t) -> p k t", k=8)[:, :, 0:rows], func=AF.Copy),
                reads=[BK[6 + sl]], writes=[UT[k]])

        tiles = list(enumerate(range(-1, 16)))
        s0_front(*tiles[0])
        for i in range(len(tiles)):
            if i + 1 < len(tiles):
                s0_front(*tiles[i + 1])
            s0_back(*tiles[i])
            if tiles[i][0] in late_loads:
                late_loads[tiles[i][0]]()

        for nt in range(8):
            S.op("pe", lambda e, nt=nt: e.matmul(P6[:, nt * 8:(nt + 1) * 8], lhsT=wbdT[:, 0, nt, :], rhs=wif[:, nt, :],
                                                 start=True, stop=False), reads=[WBT, WIF], writes=[BK[6]])
            S.op("pe", lambda e, nt=nt: e.matmul(P6[:, nt * 8:(nt + 1) * 8], lhsT=wbdT[:, 1, nt, :], rhs=wif[:, 8 + nt, :],
                                                 start=False, stop=True), reads=[WBT, WIF], writes=[BK[6]])
            S.op("pe", lambda e, nt=nt: e.matmul(P6[:, 64 + nt * 8:64 + (nt + 1) * 8], lhsT=wbdT[:, 2, nt, :], rhs=wif[:, 16 + nt, :],
                                                 start=True, stop=True), reads=[WBT, WIF], writes=[BK[6]])
        S.op("act", lambda e: e.activation(out=wg[:, :, :], in_=P6[:, 0:128].rearrange("p (k n) -> p k n", k=16), func=AF.Copy),
             reads=[BK[6]], writes=[WG])
        for o in UB:
            for s_, i_ in list(o.w.items()) + list(o.r.items()):
                if s_ not in DGT.w or DGT.w[s_].gid < i_.gid:
                    DGT.w[s_] = i_
        for nt in range(8):
            for j in range(4):
                S.op("dve", lambda e, nt=nt, j=j: e.tensor_scalar(out=dg[:, nt, j, :], in0=identb[:, :], scalar1=chv[:, nt, j:j + 1],
                                                                  scalar2=None, op0=ALU.mult), reads=[CONST], writes=[DGT])
        load_w(1, win_d[:, 0:1024], after=[UT[16]])
        load_w(2, win_d[:, 1024:2048], after=[UT[16]])
        ld("c0", "sp", nwb[:, :], fnw_d, [NWB])

        STG0 = XT + [SQ] + UB
        wk[0] = 0
        mxb = take(2064).bitcast(BF16).rearrange("p (k t) -> p k t", k=8)
        k2 = [work[:, 5808 + 512 * i_:5808 + 512 * (i_ + 1)].bitcast(BF16) for i_ in range(2)]
        gtm = take(32)
        ef = take(16)
        lfn = take(16)
        nbg = take(32)
        t1 = take(16)
        t2 = take(16)
        assert wk[0] <= 3760
        MXB = [Tk() for _ in range(8)]
        K2 = [[Tk(), Tk()], [Tk(), Tk()]]
        GTM = Tk()
        STGA = MXB + [GTM]
        for t in STGA:
            for o in STG0:
                for s_, i_ in list(o.w.items()) + list(o.r.items()):
                    if s_ not in t.w or t.w[s_].gid < i_.gid:
                        t.w[s_] = i_

        STGA = STGA + [DGT, WBT]
        for t in K2[0] + K2[1]:
            for s_, i_ in list(WBT.w.items()) + list(WBT.r.items()):
                if s_ not in t.w or t.w[s_].gid < i_.gid:
                    t.w[s_] = i_
        PAh = [PA[:, 0:512], PA[:, 512:1024]]
        PBh = [PB[:, 0:512], PB[:, 512:1024]]

        def proj_fm(slot, nt, c0, n, hb, bank_base, pview):
            uts = [0] if c0 < HALO else list(range((c0 - HALO) // 128 + 1, (c0 + n - 1 - HALO) // 128 + 2))
            utk = [UT[i] for i in uts]
            for kt in range(8):
                S.op("pe", lambda e, kt=kt: e.matmul(
                    pview[:, 0:n], lhsT=WS[slot][:, kt, nt * 128:(nt + 1) * 128], rhs=uT[:, kt, c0:c0 + n],
                    start=(kt == 0), stop=(kt == 7)), reads=[WST[slot][kt]] + ([WSC[nt // 2]] if slot == 0 else []) + utk,
                    writes=[BK[bank_base + hb]])

        def A_proj(b, nt, part=None):
            hb = nt % 2
            if b < 0:
                proj_fm(0, nt, 0, HALO, hb, 0, PAh[hb])
                S.op("act", lambda e: e.activation(out=mxb[:, nt, 1:4], in_=PAh[hb][:, 13:16], func=AF.Copy),
                     reads=[BK[hb]], writes=[MXB[nt]])
                return
            c0 = HALO + 512 * b
            if part in (None, "p"):
                proj_fm(0, nt, c0, 512, hb, 0, PAh[hb])
                if b > 0:
                    S.op("dve", lambda e: e.tensor_copy(out=mxb[:, nt, 1:4], in_=mxb[:, nt, 513:516]),
                         reads=[MXB[nt]], writes=[MXB[nt]])
                S.op("act", lambda e: e.activation(out=mxb[:, nt, 4:516], in_=PAh[hb][:, 0:512], func=AF.Copy),
                     reads=[BK[hb]], writes=[MXB[nt]])
            if part in (None, "c"):
                for j in range(4):
                    S.op("pe", lambda e, j=j: e.matmul(PBh[hb][:, 0:512], lhsT=dg[:, nt, j, :], rhs=mxb[:, nt, 1 + j:1 + j + 512],
                                                       start=(j == 0), stop=(j == 3)), reads=[MXB[nt], DGT], writes=[BK[2 + hb]])
                S.op("act", lambda e: e.activation(out=xcT[:, nt, b * 512:(b + 1) * 512], in_=PBh[hb][:, 0:512], func=AF.Silu,
                                                   bias=chv[:, nt, 4:5]), reads=[BK[2 + hb], CONST], writes=[XC[b][nt]])

        def A_vg(b):
            for j in range(4):
                c = 4 * b + j
                tc0 = b * 512 + j * 128
                pvv, bk0 = (PB, 2) if j % 2 == 0 else (PA, 0)
                for ct in range(8):
                    S.op("pe", lambda e, ct=ct, j=j, pvv=pvv: e.matmul(
                        pvv[:, ct * 128:(ct + 1) * 128], lhsT=mxb[:, ct, 4 + j * 128:4 + (j + 1) * 128], rhs=wbd[:, 2, ct, :],
                        start=True, stop=True), reads=[MXB[ct], WBD], writes=[BK[bk0 + ct // 4]])
                S.op("act", lambda e, c=c, pvv=pvv: e.activation(out=vext[:, c, :, 0:256],
                                                               in_=pvv[:, :].rearrange("p (h e) -> p h e", h=4), func=AF.Copy),
                     reads=[BK[bk0], BK[bk0 + 1]], writes=[VX[c]])
                for k in range(16):
                    if k < 8:
                        lh, rd = xcT[:, k, tc0:tc0 + 128], XC[b][k]
                    else:
                        lh, rd = mxb[:, k - 8, 4 + j * 128:4 + (j + 1) * 128], MXB[k - 8]
                    S.op("pe", lambda e, k=k, lh=lh, j=j: e.matmul(P4[:, j * 8:(j + 1) * 8], lhsT=lh, rhs=wg[:, k, :],
                                                                   start=(k == 0), stop=(k == 15)), reads=[rd, WG], writes=[BK[4]])

        def A_gmath(b):
            S.op("dve", lambda e: e.tensor_tensor(out=gtm[:, :], in0=P4[:, 0:32], in1=bifb[:, :], op=ALU.add),
                 reads=[BK[4], CONST], writes=[GTM])
            g3 = gtm.rearrange("p (a b) -> p a b", a=4)
            ef3 = ef.rearrange("p (a b) -> p a b", a=4)
            lf3 = lfn.rearrange("p (a b) -> p a b", a=4)
            t13 = t1.rearrange("p (a b) -> p a b", a=4)
            t23 = t2.rearrange("p (a b) -> p a b", a=4)
            nb3 = nbg[:, 0:16].rearrange("p (a b) -> p a b", a=4)
            ngg3 = nbg[:, 16:32].rearrange("p (a b) -> p a b", a=4)
            S.op("act", lambda e: e.activation(out=ef3, in_=g3[:, :, 4:8], func=AF.Exp, scale=-1.0), reads=[GTM], writes=[GTM])
            S.op("act", lambda e: e.activation(out=lf3, in_=ef3, func=AF.Ln, bias=1.0), reads=[GTM], writes=[GTM])
            S.op("pe", lambda e: e.matmul(P4[:, 64:80], lhsT=tri, rhs=lfn[:, :], start=True, stop=True),
                 reads=[GTM, CONST], writes=[BK[4]])
            S.op("pe", lambda e: e.matmul(P4[:, 80:96], lhsT=ones, rhs=lfn[:, :], start=True, stop=True),
                 reads=[GTM, CONST], writes=[BK[4]])
            S.op("dve", lambda e: e.tensor_copy(out=nbg[:, :], in_=P4[:, 64:96]), reads=[BK[4]], writes=[GTM])
            S.op("dve", lambda e: e.tensor_tensor(out=t13, in0=g3[:, :, 0:4], in1=nb3, op=ALU.add), reads=[GTM], writes=[GTM])
            S.op("dve", lambda e: e.tensor_tensor(out=t23, in0=t13, in1=ngg3, op=ALU.subtract), reads=[GTM], writes=[GTM])
            bs = slice(4 * b, 4 * b + 4)
            S.op("act", lambda e: e.activation(out=Ax[:, bs, :], in_=t13, func=AF.Exp), reads=[GTM], writes=[GATE[b]])
            S.op("act", lambda e: e.activation(out=A2x[:, bs, :], in_=t23, func=AF.Exp), reads=[GTM], writes=[GATE[b]])
            S.op("act", lambda e: e.activation(out=EG[:, bs, :], in_=ngg3, func=AF.Exp, scale=-1.0), reads=[GTM], writes=[GATE[b]])
            S.op("act", lambda e: e.activation(out=ENB[:, bs, :], in_=nb3, func=AF.Exp, scale=2.0), reads=[GTM], writes=[GATE[b]])
            S.op("dve", lambda e: e.tensor_scalar(out=Ax[:, bs, :], in0=Ax[:, bs, :], scalar1=0.0625, scalar2=None, op0=ALU.mult),
                 reads=[GATE[b]], writes=[GATE[b]])
            S.op("dve", lambda e: e.tensor_scalar(out=A2x[:, bs, :], in0=A2x[:, bs, :], scalar1=0.0625, scalar2=None, op0=ALU.mult),
                 reads=[GATE[b]], writes=[GATE[b]])
            for j in range(4):
                S.op("dve", lambda e, j=j: e.tensor_tensor(out=ngt[:, :], in0=ngt[:, :], in1=ngg3[:, j, :], op=ALU.add),
                     reads=[GTM, NGT], writes=[NGT])

        def A_scan_k(b, j):
            c = 4 * b + j
            ks = c % 2
            tc0 = b * 512 + j * 128
            for pair in range(2):
                pk, bk = (P5, 5) if pair == 0 else (P4, 4)
                for q4 in range(4):
                    ct = 4 * pair + q4
                    S.op("pe", lambda e, ct=ct, q4=q4, pk=pk: e.matmul(
                        pk[:, q4 * 128:(q4 + 1) * 128], lhsT=xcT[:, ct, tc0:tc0 + 128], rhs=wbd[:, 1, ct, :],
                        start=True, stop=True), reads=[XC[b][ct], WBD], writes=[BK[bk]])
                for hh in range(2):
                    h = 2 * pair + hh
                    S.op("act", lambda e, h=h, hh=hh, pk=pk: e.activation(
                        out=k2[ks][:, h * 256:(h + 1) * 256], in_=pk[:, hh * 256:(hh + 1) * 256], func=AF.Identity,
                        scale=A2x[:, c, h:h + 1]), reads=[BK[bk], GATE[b]], writes=[K2[ks][pair]])

        def A_scan_u(b, j, h):
            c = 4 * b + j
            ks = c % 2
            pair = h // 2
            for dt in range(2):
                pu = P6 if dt == 0 else P7
                S.op("pe", lambda e, dt=dt, pu=pu: e.matmul(
                    pu[:, 0:257], lhsT=k2[ks][:, h * 256 + dt * 128:h * 256 + dt * 128 + 128],
                    rhs=vext[:, c, h, 0:257], start=True, stop=True), reads=[K2[ks][pair], VX[c]], writes=[BK[6 + dt]])
            S.op("dve", lambda e: e.scalar_tensor_tensor(
                out=Cst[:, h, :, 0:257], in0=Cst[:, h, :, 0:257], scalar=EG[:, c, h:h + 1],
                in1=P67[:, :].rearrange("p (a b) -> p a b", a=2)[:, :, 0:257],
                op0=ALU.mult, op1=ALU.add), reads=[CS[h], GATE[b], BK[6], BK[7]], writes=[CS[h]])

        def A_scan(b, j):
            A_scan_k(b, j)
            for h in range(4):
                A_scan_u(b, j, h)

        for nt in range(8):
            A_proj(-1, nt)
        for i in range(4):
            A_proj(0, 2 * i, "p")
            A_proj(0, 2 * i + 1, "p")
            A_proj(0, 2 * i, "c")
            A_proj(0, 2 * i + 1, "c")
        A_vg(0)
        deferred = []
        for b in range(4):
            for i in range(4):
                if b < 3:
                    if i == 0:
                        A_proj(b + 1, 0, "p")
                        A_proj(b + 1, 1, "p")
                        A_gmath(b)
                        A_scan_k(b, 0)
                        A_scan_u(b, 0, 0)
                        A_proj(b + 1, 0, "c")
                        A_scan_u(b, 0, 1)
                        A_proj(b + 1, 1, "c")
                        A_scan_u(b, 0, 2)
                        A_scan_u(b, 0, 3)
                    else:
                        A_scan_k(b, i)
                        A_scan_u(b, i, 0)
                        A_proj(b + 1, 2 * i, "p")
                        A_scan_u(b, i, 1)
                        A_proj(b + 1, 2 * i + 1, "p")
                        A_scan_u(b, i, 2)
                        A_proj(b + 1, 2 * i, "c")
                        A_scan_u(b, i, 3)
                        A_proj(b + 1, 2 * i + 1, "c")
                else:
                    if i == 0:
                        A_gmath(b)
                    deferred.append(lambda i=i: A_scan(3, i))
            if b < 3:
                A_vg(b + 1)

        load_w(0, wout_d[0:1024, :])

        def publish():
          for h in range(4):
              for dt in range(2):
                  S.op("dve", lambda e, h=h, dt=dt: e.tensor_copy(out=Cst[:, h, dt, 257:258], in_=ngt[:, h:h + 1]),
                       reads=[NGT, CS[h]], writes=[CS[h]])
              S.dma("sp", "cci%d" % h, lambda e, h=h: e.dma_start(out=cci_t[h].ap(), in_=Cst[:, h, :, :].rearrange("p a b -> p (a b)")),
                    reads=[CS[h]], writes=[CCI[h]])
              S.dma("pool", "cc%d" % h, lambda e, h=h: e.collective_compute(
                  "AllGather", ALU.bypass, replica_groups=[[0, 1, 2, 3], [4, 5, 6, 7]],
                  ins=[cci_t[h].ap().opt()], outs=[cco_t[h].ap().opt()]), reads=[CCI[h]], writes=[CCO[h]], inc=1)

        wk[0] = 0
        pxe = [take(528) for _ in range(2)]
        tA1 = take(528)
        tA = [tA1, tA1]
        tB1 = take(528)
        tB = [tB1, tB1]
        szp = [take(256).bitcast(BF16) for _ in range(2)]
        dTt2 = [take(512).bitcast(BF16).rearrange("p (k t) -> p k t", k=2) for _ in range(2)]
        ypT = take(2048).bitcast(BF16).rearrange("p (k t) -> p k t", k=8)
        xacc = [take(512) for _ in range(4)]
        tmp16 = take(16)
        PXE = [Tk(), Tk()]
        TA1 = Tk()
        TA = [TA1, TA1]
        TB1 = Tk()
        TB = [TB1, TB1]
        SZP = [Tk(), Tk()]
        DT2 = [[Tk(), Tk()], [Tk(), Tk()]]
        YP = [Tk() for _ in range(8)]
        XACC = [Tk() for _ in range(4)]
        STGB = PXE + [TA1, TB1] + SZP + DT2[0] + DT2[1] + YP + XACC
        for t in XACC:
            for o in K2[0] + K2[1]:
                for s_, i_ in list(o.w.items()) + list(o.r.items()):
                    if s_ not in t.w or t.w[s_].gid < i_.gid:
                        t.w[s_] = i_
        for t in STGB:
            for o in STGA:
                for s_, i_ in list(o.w.items()) + list(o.r.items()):
                    if s_ not in t.w or t.w[s_].gid < i_.gid:
                        t.w[s_] = i_

        for nt in range(8):
            proj_fm(1, nt, 0, HALO, nt % 2, 0, PAh[nt % 2])
            S.op("act", lambda e, nt=nt: e.activation(out=pxhalo[:, nt, :], in_=PAh[nt % 2][:, 0:16], func=AF.Copy),
                 reads=[BK[nt % 2]], writes=[PXH[nt]])

        def B_px(b, g, n2):
            dTt, DT = dTt2[g % 2], DT2[g % 2]
            c0 = HALO + 512 * b
            win = 2 ** (g + 1)
            nt = 2 * g + n2
            proj_fm(1, nt, c0, 512, n2, 0, PAh[n2])
            S.op("act", lambda e: e.activation(out=pxe[n2][:, 16:528], in_=PAh[n2][:, 0:512], func=AF.Copy),
                 reads=[BK[n2]], writes=[PXE[n2]])
            S.op("pool", lambda e: e.tensor_copy(out=pxe[n2][:, 0:16], in_=pxhalo[:, nt, :]),
                 reads=[PXH[nt], PXE[n2]], writes=[PXE[n2]])
            S.op("pool", lambda e: e.tensor_copy(out=pxhalo[:, nt, :], in_=pxe[n2][:, 512:528]),
                 reads=[PXE[n2]], writes=[PXH[nt]])
            cur, curT = pxe[n2], PXE[n2]
            v = 0
            for lev in range(g + 1):
                sh = 2 ** lev
                nv = v + sh
                dst, dstT = (tA[n2], TA[n2]) if lev % 2 == 0 else (tB[n2], TB[n2])
                eng = "dve" if lev % 2 == 0 else "pool"
                S.op(eng, lambda e, dst=dst, cur=cur, nv=nv, sh=sh: e.tensor_tensor(
                    out=dst[:, nv:528], in0=cur[:, nv:528], in1=cur[:, nv - sh:528 - sh], op=ALU.add),
                    reads=[curT], writes=[dstT])
                cur, curT, v = dst, dstT, nv
            S.op("dve", lambda e, cur=cur: e.scalar_tensor_tensor(
                out=dTt[:, n2, :], in0=cur[:, 16:528], scalar=1.0 / win, in1=pxe[n2][:, 16:528],
                op0=ALU.mult, op1=ALU.subtract), reads=[curT, PXE[n2]], writes=[DT[n2]])
            if b == 0:
                S.op("dve", lambda e, cur=cur: e.tensor_tensor(out=tmp16[:, :], in0=cur[:, 16:32], in1=invc[:, g, :], op=ALU.mult),
                     reads=[curT, CONST, DT[n2]], writes=[DT[n2]])
                S.op("dve", lambda e: e.tensor_tensor(out=dTt[:, n2, 0:16], in0=tmp16[:, :], in1=pxe[n2][:, 16:32], op=ALU.subtract),
                     reads=[PXE[n2], DT[n2]], writes=[DT[n2]])

        def B_pz(b, g, cot):
            c0 = HALO + 512 * b
            nt = 2 * g + cot
            proj_fm(2, nt, c0, 512, cot, 2, PBh[cot])
            S.op("act", lambda e: e.activation(out=szp[cot][:, :], in_=PBh[cot][:, 0:512], func=AF.Silu),
                 reads=[BK[2 + cot]], writes=[SZP[cot]])

        def B_pw(b, g, cot):
            dTt, DT = dTt2[g % 2], DT2[g % 2]
            nt = 2 * g + cot
            pv = P4 if cot == 0 else P5
            for cit in range(2):
                S.op("pe", lambda e, cit=cit: e.matmul(
                    pv[:, 0:512], lhsT=poolw[:, g, cit, cot * 128:(cot + 1) * 128], rhs=dTt[:, cit, :],
                    start=(cit == 0), stop=(cit == 1)), reads=[DT[0], DT[1], POOLW], writes=[BK[4 + cot]])
            S.op("dve", lambda e: e.scalar_tensor_tensor(
                out=ypT[:, nt, :], in0=pv[:, 0:512], scalar=chv[:, nt, 5:6], in1=szp[cot][:, :],
                op0=ALU.mult, op1=ALU.mult), reads=[BK[4 + cot], SZP[cot], CONST], writes=[YP[nt]])

        XB = xacc + [pxe[0][:, 0:512], pxe[1][:, 0:512], tA1[:, 0:512], tB1[:, 0:512]]
        XBT = XACC + [PXE[0], PXE[1], TA1, TB1]

        def B_ld(b, u, extra=()):
            j, half = divmod(u, 2)
            c = 4 * b + j
            xb = u if b == 3 else u % 4
            S.dma("sp", "xa%d" % xb, lambda e: e.dma_start(
                out=XB[xb][:, :], in_=x_d[HALO + c * 128:HALO + (c + 1) * 128, half * 512:(half + 1) * 512]),
                writes=[XBT[xb]] + list(extra))

        def B_out(b, u):
            j, half = divmod(u, 2)
            c = 4 * b + j
            xb = u if b == 3 else u % 4
            pv = P6 if half == 0 else P7
            for kt in range(8):
                S.op("pe", lambda e, kt=kt: e.matmul(
                    pv[:, 0:512], lhsT=ypT[:, kt, j * 128:(j + 1) * 128],
                    rhs=WS[0][:, kt, half * 512:(half + 1) * 512], start=(kt == 0), stop=(kt == 7)),
                    reads=[YP[kt], WST[0][kt]], writes=[BK[6 + half]])
            S.op("dve", lambda e: e.tensor_tensor(out=XB[xb][:, :], in0=pv[:, 0:512], in1=XB[xb][:, :], op=ALU.add),
                 reads=[BK[6 + half], XBT[xb]], writes=[XBT[xb]])
            S.dma("sp", "acc%d" % xb, lambda e: e.dma_start(
                out=acc_d[c * 128:(c + 1) * 128, half * 512:(half + 1) * 512], in_=XB[xb][:, :]),
                reads=[XBT[xb]], writes=[ACC[c]])
            if b < 3 and u + 4 < 8:
                B_ld(b, u + 4)

        seq = [(b, g) for b in range(4) for g in range(4)]
        B_px(0, 0, 0)
        B_px(0, 0, 1)
        for i, (b, g) in enumerate(seq):
            nxt = seq[i + 1] if i + 1 < len(seq) else None
            if g == 3 and b > 0:
                for u in range(4):
                    B_ld(b, u)
            B_pz(b, g, 0)
            if nxt:
                B_px(nxt[0], nxt[1], 0)
            B_pz(b, g, 1)
            if nxt:
                B_px(nxt[0], nxt[1], 1)
            if i == len(seq) - 1:
                for u in range(4, 8):
                    B_ld(3, u)
                load_w(1, win_d[:, 3072:4096])
                load_w(2, win_d[:, 4096:5120])
            if g == 0 and b > 0:
                for u in range(8):
                    B_out(b - 1, u)
            B_pw(b, g, 0)
            B_pw(b, g, 1)
            if deferred:
                deferred.pop(0)()
                if not deferred:
                    publish()
                    for u in range(4):
                        B_ld(0, u, extra=K2[0] + K2[1])
        for u in range(8):
            B_out(3, u)

        wk[0] = 0
        gathA = take(2064).rearrange("p (r n) -> p r n", r=4)
        gathB = work[:, 2064:4128].rearrange("p (r n) -> p r n", r=4)
        qT = [take(512).bitcast(BF16).rearrange("p (k t) -> p k t", k=8) for _ in range(2)]
        kT = [take(512).bitcast(BF16).rearrange("p (k t) -> p k t", k=8) for _ in range(2)]
        take(16)
        szT = take(1024).bitcast(BF16).rearrange("p (k t) -> p k t", k=8)
        so = [take(512).bitcast(BF16) for _ in range(2)]
        k2c1 = take(512).bitcast(BF16)
        k2c = [k2c1, k2c1]
        Wt = [take(64).bitcast(BF16) for _ in range(2)]
        hg = [take(256) for _ in range(2)]
        hn = [take(128).bitcast(BF16) for _ in range(2)]
        tt = [take(128) for _ in range(2)]
        cf = take(64)
        stt = take(64)
        wk_end = wk[0]
        wk[0] = 0
        ymT = [take(512).bitcast(BF16).rearrange("p (k t) -> p k t", k=8) for _ in range(2)]
        res = [take(512) for _ in range(2)]
        assert wk[0] <= 2064
        wk[0] = wk_end
        GA = Tk()
        GA2 = Tk()
        QT = [[Tk() for _ in range(2)] for _ in range(2)]
        KT = [[Tk() for _ in range(2)] for _ in range(2)]
        SZ = [Tk() for _ in range(4)]
        SO = [[Tk(), Tk()], [Tk(), Tk()]]
        K2C1 = [Tk(), Tk()]
        K2C = [K2C1, K2C1]
        WT = [Tk(), Tk()]
        HG = [Tk(), Tk()]
        HN = [Tk(), Tk()]
        TTt = [Tk(), Tk()]
        YM = [[Tk() for _ in range(4)] for _ in range(2)]
        RES = [Tk(), Tk()]
        CF = Tk()
        STAT = [Tk(), Tk()]
        allc = [GA, GA2, CF] + SZ + WT + HG + HN + TTt + RES + STAT
        for l_ in (QT, KT, SO, [K2C1], YM):
            for x_ in l_:
                allc += x_
        for t in allc:
            for o in STGB + STGA:
                for s_, i_ in list(o.w.items()) + list(o.r.items()):
                    if s_ not in t.w or t.w[s_].gid < i_.gid:
                        t.w[s_] = i_

        PT7 = P4[:, :].bitcast(BF16)[:, 512:768]
        PJ1 = PAh[1].bitcast(BF16)
        xcb = lambda c: XC[c // 4]

        def T_mo(c, half, part=None):
            p = c % 2
            ucol = HALO + c * 128
            if part in (None, "pe"):
                for kt in range(8):
                    S.op("pe", lambda e, kt=kt: e.matmul(
                        PAh[0][:, 0:512], lhsT=uT[:, kt, ucol:ucol + 128], rhs=WS[2][:, kt, half * 512:(half + 1) * 512],
                        start=(kt == 0), stop=(kt == 7)), reads=[WST[2][kt], UT[c + 1]], writes=[BK[0]])
            if part in (None, "ev"):
                S.op("act", lambda e: e.activation(out=so[p][:, half * 512:(half + 1) * 512], in_=PAh[0][:, 0:512], func=AF.Tanh, scale=0.5),
                     reads=[BK[0]], writes=[SO[p][half]])

        def T_k2(c, pair):
            p = c % 2
            tc0 = c * 128
            pv = PBh[pair]
            for q4 in range(4):
                ct = 4 * pair + q4
                S.op("pe", lambda e, ct=ct, q4=q4: e.matmul(
                    pv[:, q4 * 128:(q4 + 1) * 128], lhsT=xcT[:, ct, tc0:tc0 + 128], rhs=wbd[:, 1, ct, :],
                    start=True, stop=True), reads=[xcb(c)[ct], WBD], writes=[BK[2 + pair]])
            for hh in range(2):
                h = 2 * pair + hh
                S.op("act", lambda e, h=h, hh=hh: e.activation(
                    out=k2c[p][:, h * 256:(h + 1) * 256], in_=pv[:, hh * 256:(hh + 1) * 256], func=AF.Identity,
                    scale=A2x[:, c, h:h + 1]), reads=[BK[2 + pair], GATE[c // 4]], writes=[K2C[p][pair]])

        def T_qk(c, pair, which):
            p = c % 2
            tc0 = c * 128
            pv = PBh[which]
            dst, dstT = (qT, QT) if which == 0 else (kT, KT)
            for q4 in range(4):
                nt = 4 * pair + q4
                S.op("pe", lambda e, nt=nt, q4=q4: e.matmul(
                    pv[:, q4 * 128:(q4 + 1) * 128], lhsT=wbd[:, which, nt, :], rhs=xcT[:, nt, tc0:tc0 + 128],
                    start=True, stop=True), reads=[xcb(c)[nt], WBD], writes=[BK[2 + which]])
            S.op("act", lambda e: e.activation(out=dst[p][:, 4 * pair:4 * pair + 4, :],
                                               in_=pv[:, 0:512].rearrange("p (k t) -> p k t", k=4), func=AF.Copy),
                 reads=[BK[2 + which]], writes=[dstT[p][pair]])

        def T_szp(sbi, np_):
            c0 = HALO + sbi * 256
            hb = np_ % 2
            for q2 in range(2):
                nt = 2 * np_ + q2
                uts = list(range((c0 - HALO) // 128 + 1, (c0 + 255 - HALO) // 128 + 2))
                for kt in range(8):
                    S.op("pe", lambda e, kt=kt, nt=nt, q2=q2: e.matmul(
                        PBh[hb][:, q2 * 256:(q2 + 1) * 256], lhsT=WS[1][:, kt, nt * 128:(nt + 1) * 128],
                        rhs=uT[:, kt, c0:c0 + 256], start=(kt == 0), stop=(kt == 7)),
                        reads=[WST[1][kt]] + [UT[i] for i in uts], writes=[BK[2 + hb]])
            S.op("act", lambda e: e.activation(
                out=szT[:, 2 * np_:2 * np_ + 2, :], in_=PBh[hb][:, 0:512].rearrange("p (k t) -> p k t", k=2), func=AF.Silu),
                reads=[BK[2 + hb]], writes=[SZ[np_]])

        def T_sz(sbi):
            for np_ in range(4):
                T_szp(sbi, np_)

        def T_st(c, h):
            p = c % 2
            ws = h % 2
            for dt in range(2):
                S.op("pe", lambda e, dt=dt: e.matmul(
                    P4[:, 0:128], lhsT=kT[p][:, 2 * h + dt, :], rhs=qT[p][:, 2 * h + dt, :],
                    start=(dt == 0), stop=(dt == 1)), reads=[KT[p][h // 2], QT[p][h // 2]], writes=[BK[4]])
            S.op("dve", lambda e: e.scalar_tensor_tensor(
                out=Wt[ws][:, :], in0=P4[:, 0:128], scalar=Ax[:, c, h:h + 1], in1=mask01,
                op0=ALU.mult, op1=ALU.mult), reads=[BK[4], GATE[c // 4], CONST], writes=[WT[ws]])

        def T_nd(c, h):
            p = c % 2
            ws = h % 2
            pv, bk = (P5, 5) if h % 2 == 0 else (PAh[1], 1)
            S.op("pe", lambda e: e.matmul(pv[:, 0:257], lhsT=Wt[ws][:, :], rhs=vext[:, c, h, 0:257],
                                          start=True, stop=False), reads=[WT[ws], VX[c]], writes=[BK[bk]])
            for dt in range(2):
                S.op("pe", lambda e, dt=dt: e.matmul(
                    pv[:, 0:257], lhsT=qT[p][:, 2 * h + dt, :], rhs=Cbf[:, h, dt, 0:257],
                    start=False, stop=(dt == 1)), reads=[QT[p][h // 2], CB[h]], writes=[BK[bk]])

        def T_u(c, h, dt):
            p = c % 2
            pu = P6 if dt == 0 else P7
            S.op("pe", lambda e: e.matmul(
                pu[:, 0:257], lhsT=k2c[p][:, h * 256 + dt * 128:h * 256 + dt * 128 + 128],
                rhs=vext[:, c, h, 0:257], start=True, stop=True), reads=[K2C[p][h // 2], VX[c]], writes=[BK[6 + dt]])
            if dt == 1:
                S.op("dve", lambda e: e.scalar_tensor_tensor(
                    out=Cst[:, h, :, 0:257], in0=Cst[:, h, :, 0:257], scalar=EG[:, c, h:h + 1],
                    in1=P67[:, :].rearrange("p (a b) -> p a b", a=2)[:, :, 0:257],
                    op0=ALU.mult, op1=ALU.add), reads=[CS[h], GATE[c // 4], BK[6], BK[7]], writes=[CS[h]])

        def T_cb(c, h):
            S.op("pool", lambda e: e.tensor_copy(out=Cbf[:, h, :, 0:257], in_=Cst[:, h, :, 0:257]),
                 reads=[CS[h]], writes=[CB[h]])

        def T_stat(c, h):
            p = c % 2
            ws = h % 2
            pr = h // 2
            pv, bk = (P5, 5) if h % 2 == 0 else (PAh[1], 1)
            S.op("dve", lambda e: e.tensor_copy(out=stt[:, h:h + 1], in_=pv[:, 256:257]), reads=[BK[bk], STAT[pr]], writes=[STAT[pr]])
            S.op("dve", lambda e: e.scalar_tensor_tensor(
                out=hg[ws][:, :], in0=so[p][:, h * 256:(h + 1) * 256], scalar=1.0, in1=pv[:, 0:256],
                op0=ALU.add, op1=ALU.mult), reads=[BK[bk], SO[p][h // 2]], writes=[HG[ws]])
            st6 = stt[:, 32 + 8 * ws:38 + 8 * ws]
            S.op("dve", lambda e: e.bn_stats(out=st6, in_=hg[ws][:, :]), reads=[HG[ws], STAT[pr]], writes=[STAT[pr]])
            S.op("dve", lambda e: e.bn_aggr(out=stt[:, 8 + 2 * h:10 + 2 * h], in_=st6), reads=[STAT[pr]], writes=[STAT[pr]])

        def T_rstd(c, pr):
            hs = slice(2 * pr, 2 * pr + 2)
            dcol = stt[:, hs]
            mv = stt[:, 8:16].rearrange("p (h two) -> p h two", two=2)
            q_ = stt[:, 16 + 2 * pr:18 + 2 * pr]
            r_ = stt[:, 20 + 2 * pr:22 + 2 * pr]
            nb_ = stt[:, 24 + 2 * pr:26 + 2 * pr]
            S.op("dve", lambda e: e.tensor_tensor(out=dcol, in0=dcol, in1=dcol, op=ALU.mult), reads=[STAT[pr]], writes=[STAT[pr]])
            S.op("dve", lambda e: e.tensor_tensor(out=dcol, in0=dcol, in1=ENB[:, c, hs], op=ALU.max),
                 reads=[STAT[pr], GATE[c // 4]], writes=[STAT[pr]])
            S.op("dve", lambda e: e.scalar_tensor_tensor(out=q_, in0=dcol, scalar=4.0 * EPS, in1=mv[:, hs, 1], op0=ALU.mult, op1=ALU.add),
                 reads=[STAT[pr]], writes=[STAT[pr]])
            S.op("act", lambda e: e.activation(out=q_, in_=q_, func=AF.Sqrt), reads=[STAT[pr]], writes=[STAT[pr]])
            S.op("dve", lambda e: e.reciprocal(out=r_, in_=q_), reads=[STAT[pr]], writes=[STAT[pr]])
            S.op("dve", lambda e: e.scalar_tensor_tensor(out=nb_, in0=mv[:, hs, 0], scalar=-1.0, in1=r_, op0=ALU.mult, op1=ALU.mult),
                 reads=[STAT[pr]], writes=[STAT[pr]])

        def T_hn(c, h):
            ws = h % 2
            pr = h // 2
            S.op("act", lambda e: e.activation(out=hn[ws][:, :], in_=hg[ws][:, :], func=AF.Identity,
                                               scale=stt[:, 20 + h:21 + h], bias=stt[:, 24 + h:25 + h]),
                 reads=[HG[ws], STAT[pr]], writes=[HN[ws]])

        def T_tr(c, h):
            p = c % 2
            ws = h % 2
            lo = (c % 2) * 128
            tc0 = c * 128
            for dt in range(2):
                S.op("pe", lambda e, dt=dt: e.transpose(out=PT7[:, dt * 128:(dt + 1) * 128],
                                                        in_=hn[ws][:, dt * 128:(dt + 1) * 128], identity=identb[:, :]),
                     reads=[HN[ws], CONST], writes=[BK[4]])
            for dt in range(2):
                nt = 2 * h + dt
                S.op("act", lambda e, dt=dt, nt=nt: e.activation(out=tt[dt][:, :], in_=PT7[:, dt * 128:(dt + 1) * 128],
                                                                 func=AF.Identity, scale=chv[:, nt, 7:8]),
                     reads=[BK[4], CONST], writes=[TTt[dt]])
                S.op("dve", lambda e, dt=dt, nt=nt: e.scalar_tensor_tensor(
                    out=tt[dt][:, :], in0=xcT[:, nt, tc0:tc0 + 128], scalar=chv[:, nt, 6:7], in1=tt[dt][:, :],
                    op0=ALU.mult, op1=ALU.add), reads=[xcb(c)[nt], TTt[dt], CONST], writes=[TTt[dt]])
                S.op("pool", lambda e, dt=dt, nt=nt: e.tensor_tensor(
                    out=ymT[p][:, nt, :], in0=tt[dt][:, :], in1=szT[:, nt, lo:lo + 128], op=ALU.mult),
                    reads=[TTt[dt], SZ[nt // 2]], writes=[YM[p][h]])

        def T_res(c):
            for half in range(2):
                S.dma("sp", "res%d" % half, lambda e, half=half: e.dma_start(
                    out=res[half][:, :], in_=acc_d[c * 128:(c + 1) * 128, half * 512:(half + 1) * 512]),
                    reads=[ACC[c]], writes=[RES[half]])

        def T_out(c, half, part=None):
            p = c % 2
            if part in (None, "pe"):
                for kt in range(8):
                    S.op("pe", lambda e, kt=kt: e.matmul(
                        PAh[0][:, 0:512], lhsT=ymT[p][:, kt, :], rhs=WS[0][:, kt, half * 512:(half + 1) * 512],
                        start=(kt == 0), stop=(kt == 7)), reads=[YM[p][kt // 2], WST[0][kt]], writes=[BK[0]])
            if part in (None, "ev"):
                S.op("dve", lambda e: e.tensor_tensor(out=res[half][:, :], in0=PAh[0][:, 0:512], in1=res[half][:, :], op=ALU.add),
                     reads=[BK[0], RES[half]], writes=[RES[half]])

        def T_fin(c):
            p = c % 2
            fs = cf[:, 40:42]
            fr = cf[:, 42:43]
            for half in range(2):
                S.op("act", lambda e, half=half: e.activation(out=PAh[0][:, 0:512], in_=res[half][:, :],
                                                              func=AF.Square, accum_out=fs[:, half:half + 1]),
                     reads=[RES[half], CF], writes=[BK[0], CF])
            S.op("dve", lambda e: e.tensor_tensor(out=fr, in0=fs[:, 0:1], in1=fs[:, 1:2], op=ALU.add), reads=[CF], writes=[CF])
            S.op("act", lambda e: e.activation(out=fr, in_=fr, func=AF.Sqrt, scale=1.0 / D, bias=EPS), reads=[CF], writes=[CF])
            S.op("dve", lambda e: e.reciprocal(out=fr, in_=fr), reads=[CF], writes=[CF])
            for half in range(2):
                S.op("dve", lambda e, half=half: e.scalar_tensor_tensor(
                    out=res[half][:, :], in0=res[half][:, :], scalar=fr, in1=nwb[:, half * 512:(half + 1) * 512],
                    op0=ALU.mult, op1=ALU.mult), reads=[RES[half], CF, NWB], writes=[RES[half]])
                S.dma("sp", "out%d" % half, lambda e, half=half: e.dma_start(
                    out=out_d[c * 128:(c + 1) * 128, half * 512:(half + 1) * 512], in_=res[half][:, :]),
                    reads=[RES[half]], writes=[OUT])

        def setup(c):
            T_mo(c, 0)
            T_mo(c, 1)
            T_k2(c, 0)
            T_k2(c, 1)
            T_qk(c, 0, 0)
            T_qk(c, 0, 1)
            T_qk(c, 1, 0)
            T_qk(c, 1, 1)

        T_sz(0)
        T_mo(0, 0)
        T_mo(0, 1)
        T_k2(0, 0)
        T_k2(0, 1)
        for h in range(4):
            gath, GAh = (gathA, GA) if h % 2 == 0 else (gathB, GA2)
            S.dma("sp", "ga%d" % (h % 2), lambda e, h=h, gath=gath: e.dma_start(out=gath[:, :, :], in_=cco_t[h].ap().rearrange("(r p) n -> p r n", p=128)),
                  reads=[CCO[h]], writes=[GAh])
            tm = cf[:, 0:16].rearrange("p (a b) -> p a b", a=4)
            for i in range(4):
                S.op("dve", lambda e, i=i, tm=tm, gath=gath: e.tensor_tensor(out=tm[:, i, :], in0=brb[:, i, :], in1=gath[:, :, 257], op=ALU.mult),
                     reads=[GAh, CONST, CF], writes=[CF])
            S.op("dve", lambda e, tm=tm: e.tensor_reduce(out=cf[:, 16:20], in_=tm, axis=AX.X, op=ALU.add), reads=[CF], writes=[CF])
            S.op("act", lambda e: e.activation(out=cf[:, 20:24], in_=cf[:, 16:20], func=AF.Exp, scale=-1.0), reads=[CF], writes=[CF])
            S.op("dve", lambda e: e.tensor_tensor(out=cf[:, 24:28], in0=cf[:, 20:24], in1=cmask[:, :], op=ALU.mult),
                 reads=[CF, CONST], writes=[CF])
            cflat = Cst[:, h, :, :].rearrange("p a b -> p (a b)")
            S.op("dve", lambda e, cflat=cflat, gath=gath: e.tensor_scalar(out=cflat, in0=gath[:, 0, :], scalar1=cf[:, 24:25], scalar2=None, op0=ALU.mult),
                 reads=[GAh, CF, CS[h]], writes=[CS[h]])
            for i in range(1, 4):
                S.op("dve", lambda e, cflat=cflat, i=i, gath=gath: e.scalar_tensor_tensor(
                    out=cflat, in0=gath[:, i, :], scalar=cf[:, 24 + i:25 + i], in1=cflat, op0=ALU.mult, op1=ALU.add),
                    reads=[GAh, CF, CS[h]], writes=[CS[h]])
            S.op("act", lambda e, h=h: e.activation(out=Cbf[:, h, :, :], in_=Cst[:, h, :, :], func=AF.Copy),
                 reads=[CS[h]], writes=[CB[h]])
        for l_ in YM:
            for t in l_ + RES:
                for s_, i_ in list(GA.w.items()) + list(GA.r.items()):
                    if s_ not in t.w or t.w[s_].gid < i_.gid:
                        t.w[s_] = i_
        for l_ in QT + KT:
            for t in l_:
                for s_, i_ in list(GA2.w.items()) + list(GA2.r.items()):
                    if s_ not in t.w or t.w[s_].gid < i_.gid:
                        t.w[s_] = i_
        T_qk(0, 0, 0)
        T_qk(0, 0, 1)
        T_qk(0, 1, 0)
        T_qk(0, 1, 1)

        load_w(0, wout_d[1024:2048, :], after=[CB[3]])

        def front(c, h, piece=None):
            T_nd(c, h)
            T_u(c, h, 0)
            T_u(c, h, 1)
            if h < 3:
                T_st(c, h + 1)
            if piece is not None:
                piece("pe")

        def back(c, h, piece=None):
            T_stat(c, h)
            T_cb(c, h)
            if piece is not None:
                piece("ev")

        for c in range(16):
            nx = c + 1 < 16
            pc = [None] * 4
            if nx:
                pc[0] = lambda part, c=c: T_mo(c + 1, 0, part)
                pc[1] = lambda part, c=c: T_mo(c + 1, 1, part)
            if c > 0:
                pc[2] = lambda part, c=c: T_out(c - 1, 0, part)
                pc[3] = lambda part, c=c: T_out(c - 1, 1, part)
            T_st(c, 0)
            front(c, 0, pc[0])
            if c > 0:
                T_hn(c - 1, 2)
                T_hn(c - 1, 3)
            back(c, 0, pc[0])
            if nx:
                T_qk(c + 1, 0, 0); T_qk(c + 1, 0, 1)
            front(c, 1, pc[1])
            back(c, 1, pc[1])
            if c > 0:
                T_tr(c - 1, 2)
                T_tr(c - 1, 3)
                if (c - 1) % 2 == 1:
                    T_szp(c // 2, 2)
                    T_szp(c // 2, 3)
            if nx:
                T_qk(c + 1, 1, 0); T_qk(c + 1, 1, 1)
            T_rstd(c, 0)
            front(c, 2, pc[2])
            T_hn(c, 0)
            T_hn(c, 1)
            back(c, 2, pc[2])
            if nx:
                T_k2(c + 1, 0)
            front(c, 3, pc[3])
            back(c, 3, pc[3])
            T_tr(c, 0)
            T_tr(c, 1)
            if c > 0:
                T_fin(c - 1)
            if nx:
                T_k2(c + 1, 1)
            T_rstd(c, 1)
            T_res(c)
            if c % 2 == 1 and nx:
                T_szp((c + 1) // 2, 0)
                T_szp((c + 1) // 2, 1)
        T_hn(15, 2)
        T_hn(15, 3)
        T_tr(15, 2)
        T_tr(15, 3)
        T_out(15, 0)
        T_out(15, 1)
        T_fin(15)
        if DEBUG:
            dbg_xc = nc.dram_tensor("dbg_xc", [128, 8 * 2048], BF16, kind="ExternalOutput").ap()
            dbg_g = nc.dram_tensor("dbg_g", [128, 256], F32, kind="ExternalOutput").ap()
            dbg_v = nc.dram_tensor("dbg_v", [128, 16 * 4 * 258], BF16, kind="ExternalOutput").ap()
            allxc = [t for l_ in XC for t in l_]
            S.dma("sp", "dbg1", lambda e: e.dma_start(out=dbg_xc, in_=xcT[:, :, :].rearrange("p k t -> p (k t)")), reads=allxc, writes=[OUT])
            for i_, arr in enumerate([Ax, A2x, EG, ENB]):
                S.dma("sp", "dbg2", lambda e, i_=i_, arr=arr: e.dma_start(out=dbg_g[:, i_ * 64:(i_ + 1) * 64], in_=arr[:, :, :].rearrange("p a b -> p (a b)")),
                      reads=GATE, writes=[OUT])
            S.dma("sp", "dbg3", lambda e: e.dma_start(out=dbg_v, in_=vext[:, :, :, :].rearrange("p a b c -> p (a b c)")), reads=VX, writes=[OUT])
        S.join("sp", [OUT])
        S.emit(nc)
    return nc


def _block_diag(w):
    m = np.zeros((1024, 128), np.float32)
    for n in range(256):
        ct, nn = divmod(n, 32)
        m[ct * 128 + 4 * nn:ct * 128 + 4 * nn + 4, 4 * nn:4 * nn + 4] = w[n]
    return m


_NC_CACHE = {}


def kernel(x, norm_w, w_in, pool_w, pool_scale, conv_w, conv_b, w_q, w_k, w_v, w_if, b_if,
           mh_norm_w, m_skip, w_out, final_norm_w):
    f = lambda a: np.ascontiguousarray(np.asarray(a, dtype=np.float32))
    x = f(x)
    B, SEQ, _ = x.shape
    nseg = SEQ // T
    assert B * nseg == NCORES
    if "nc" not in _NC_CACHE:
        _NC_CACHE["nc"] = build_nc()
    nc = _NC_CACHE["nc"]

    chv = np.zeros((128, 8, 8), np.float32)
    cw = f(conv_w)[0]
    for j in range(4):
        chv[:, :, j] = cw[j].reshape(8, 128).T
    chv[:, :, 4] = f(conv_b)[0].reshape(8, 128).T
    chv[:, :, 5] = f(pool_scale)[0].reshape(8, 128).T
    chv[:, :, 6] = f(m_skip)[0].reshape(8, 128).T
    chv[:, :, 7] = f(mh_norm_w)[0].reshape(8, 128).T
    cst = np.zeros((128, 4, 128), np.float32)
    cst[:, 0, :] = np.eye(128, dtype=np.float32)
    cst[:, 1, :] = np.triu(np.ones((128, 128), np.float32))
    cst[:, 2, :] = 1.0
    cst[:, 3, :] = np.triu(np.ones((128, 128), np.float32))
    shared = {
        "w_in": f(w_in)[0], "w_out": f(w_out)[0], "pool_w": f(pool_w)[0].reshape(1024, 256),
        "wbd": np.ascontiguousarray(np.stack([_block_diag(f(w_q)[0]), _block_diag(f(w_k)[0]), _block_diag(f(w_v)[0])], axis=0)
                                    .reshape(3, 8, 128, 128).transpose(2, 0, 1, 3).reshape(128, 3 * 8 * 128)),
        "wbdT": np.ascontiguousarray(np.stack([_block_diag(f(w_q)[0]), _block_diag(f(w_k)[0]), _block_diag(f(w_v)[0])], axis=0)
                                     .reshape(3, 8, 128, 128).transpose(3, 0, 1, 2).reshape(128, 3 * 8 * 128)),
        "w_if": np.ascontiguousarray(f(w_if)[0].reshape(24, 128, 8).transpose(1, 0, 2).reshape(128, 192)), "bifb": np.ascontiguousarray(np.broadcast_to(np.tile(f(b_if)[0].reshape(1, 8), (1, 4)), (128, 32))), "chv": chv.reshape(128, 64),
        "normw_b": np.ascontiguousarray(np.broadcast_to(f(norm_w)[0][None, :], (128, D))),
        "fnw_b": np.ascontiguousarray(np.broadcast_to(f(final_norm_w)[None, :], (128, D))),
        "cst": cst.reshape(128, 512),
        "identb_in": np.eye(128, dtype=np.float32).astype(ml_dtypes.bfloat16),
    }
    in_maps = []
    for r in range(NCORES):
        b, j = divmod(r, nseg)
        start = j * T
        xs = np.zeros((TT, D), np.float32)
        if j == 0:
            xs[HALO:] = x[b, 0:T]
        else:
            xs[:] = x[b, start - HALO:start + T]
        invc = np.zeros((4, 16), np.float32)
        for g in range(4):
            win = 2 ** (g + 1)
            for i in range(16):
                invc[g, i] = 1.0 / min(start + i + 1, win)
        brb = np.zeros((4, 4), np.float32)
        cm = np.zeros((4,), np.float32)
        for i in range(4):
            cm[i] = 1.0 if i < j else 0.0
            for l in range(4):
                brb[i, l] = 1.0 if (i < l < j) else 0.0
        m = dict(shared)
        m["x"] = xs
        m["invcnt"] = np.ascontiguousarray(np.broadcast_to(invc.reshape(1, 64), (128, 64)))
        m["brb"] = np.ascontiguousarray(np.broadcast_to(brb.reshape(1, 16), (128, 16)))
        m["cmask"] = np.ascontiguousarray(np.broadcast_to(cm.reshape(1, 4), (128, 4)))
        in_maps.append(m)
    res = run_bass_kernel_spmd(nc, in_maps, core_ids=list(range(NCORES)))
    if DEBUG:
        _NC_CACHE["res"] = res
    out = np.zeros((B, SEQ, D), np.float32)
    for r in range(NCORES):
        b, j = divmod(r, nseg)
        out[b, j * T:(j + 1) * T] = res.results[r]["out"]
    return out
```

```python
from contextlib import ExitStack
import numpy as np
import ml_dtypes
import concourse.bass as bass
import concourse.mybir as mybir
from concourse.bass_utils import run_bass_kernel_spmd

F32 = mybir.dt.float32
BF16 = mybir.dt.bfloat16
AF = mybir.ActivationFunctionType
ALU = mybir.AluOpType
AX = mybir.AxisListType

NCORES = 8
T = 2048
HALO = 16
TT = T + HALO
D = 1024
EPS = 1e-6
ENGS = ["pe", "act", "dve", "pool", "sp"]
STRICT_ENGS = ("dve", "act")


class Tk:
    __slots__ = ("w", "r", "excl")

    def __init__(self, excl=False):
        self.w = {}
        self.r = {}
        self.excl = excl


class Ins:
    __slots__ = ("eng", "stream", "fn", "deps", "needed", "seq", "gid", "isdma", "inc")


class Sched:
    def __init__(self):
        self.order = {e: [] for e in ENGS}
        self.dmacnt = {}
        self.gid = 0

    def _new(self, eng, stream, fn, isdma):
        i = Ins()
        i.eng, i.stream, i.fn, i.isdma = eng, stream, fn, isdma
        i.deps = set()
        i.needed = False
        i.seq = 0
        i.inc = 16
        self.gid += 1
        i.gid = self.gid
        return i

    def _link(self, ins, reads, writes):
        xr = [t for t in reads if t.excl and t not in writes]
        if xr:
            writes = list(writes) + xr
        raw, oth = set(), set()
        for t in reads:
            raw.update(t.w.values())
        for t in writes:
            oth.update(t.w.values())
            oth.update(t.r.values())
        deps = set()
        for d in raw:
            if d.stream == ins.stream and ins.stream == "pe":
                continue
            deps.add(d)
        for d in oth:
            if d.stream == ins.stream and not ins.isdma and ins.stream not in STRICT_ENGS:
                continue
            deps.add(d)
        deps.discard(ins)
        ins.deps = deps
        for d in deps:
            d.needed = True
        for t in reads:
            t.r[ins.stream] = ins
        for t in writes:
            t.w = {ins.stream: ins}
            t.r = {}
        self.order[ins.eng].append(ins)
        return ins

    def op(self, eng, fn, reads=(), writes=()):
        return self._link(self._new(eng, eng, fn, False), reads, writes)

    def dma(self, qeng, key, fn, reads=(), writes=(), inc=16):
        ins = self._new(qeng, ("d", key), fn, True)
        ins.inc = inc
        n = self.dmacnt.get(key, 0) + 1
        self.dmacnt[key] = n
        ins.seq = n
        return self._link(ins, reads, writes)

    def join(self, eng, tiles):
        return self._link(self._new(eng, eng, None, False), (), tiles)

    def emit(self, nc):
        for e in ENGS:
            n = 0
            for ins in self.order[e]:
                if ins.isdma:
                    continue
                if ins.needed and ins.fn is not None:
                    n += 1
                ins.seq = n
        with ExitStack() as st:
            sems = {}
            for e in ENGS:
                sems[e] = st.enter_context(nc.semaphore("s_" + e))
            for i, k in enumerate(self.dmacnt):
                sems[("d", k)] = st.enter_context(nc.semaphore("d%d" % i))
            block = st.enter_context(nc.Block())

            def run(eng_name, e):
                known = {}
                for ins in self.order[eng_name]:
                    for d in sorted(ins.deps, key=lambda z: z.gid):
                        val = d.seq * d.inc if d.isdma else d.seq
                        if val <= 0 or known.get(d.stream, 0) >= val:
                            continue
                        e.wait_ge(sems[d.stream], val)
                        known[d.stream] = val
                    if ins.fn is None:
                        continue
                    r = ins.fn(e)
                    if ins.isdma:
                        r.then_inc(sems[ins.stream], ins.inc)
                    elif ins.needed:
                        r.then_inc(sems[ins.stream], 1)

            @block.tensor
            def _(e):
                run("pe", e)

            @block.scalar
            def _(e):
                run("act", e)

            @block.vector
            def _(e):
                run("dve", e)

            @block.gpsimd
            def _(e):
                run("pool", e)

            @block.sync
            def _(e):
                run("sp", e)


DEBUG = False


def build_nc():
    nc = bass.Bass("TRN2", target_bir_lowering=False)
    S = Sched()

    def din(name, shape):
        return nc.dram_tensor(name, shape, F32, kind="ExternalInput").ap()

    x_d = din("x", [TT, D])
    win_d = din("w_in", [D, 5120])
    wout_d = din("w_out", [2048, D])
    poolw_d = din("pool_w", [1024, 256])
    wbd_d = din("wbd", [128, 3 * 8 * 128])
    wbdT_d = din("wbdT", [128, 3 * 8 * 128])
    wif_d = din("w_if", [128, 24 * 8])
    bifb_d = din("bifb", [128, 32])
    chv_d = din("chv", [128, 64])
    nw_d = din("normw_b", [128, D])
    fnw_d = din("fnw_b", [128, D])
    cst_d = din("cst", [128, 512])
    identb_d = nc.dram_tensor("identb_in", [128, 128], BF16, kind="ExternalInput").ap()
    invc_d = din("invcnt", [128, 64])
    brb_d = din("brb", [128, 16])
    cmask_d = din("cmask", [128, 4])
    out_d = nc.dram_tensor("out", [T, D], F32, kind="ExternalOutput").ap()
    acc_d = nc.dram_tensor("acc_scr", [T, D], F32).ap()
    cci_t = [nc.dram_tensor("cci%d" % h, [128, 516], F32) for h in range(4)]
    cco_t = [nc.dram_tensor("cco%d" % h, [512, 516], F32) for h in range(4)]

    with ExitStack() as st:
        def sb(name, shape, dt):
            return st.enter_context(nc.sbuf_tensor(name, shape, dt))

        def ps(name, shape):
            return st.enter_context(nc.psum_tensor(name, shape, F32))

        uT = sb("uT", [128, 8, TT], BF16)
        xcT = sb("xcT", [128, 8, T], BF16)
        vext = sb("vext", [128, 16, 4, 258], BF16)
        WS = [sb("ws%d" % i, [128, 8, 1024], BF16) for i in range(3)]
        poolw = sb("poolw", [128, 4, 2, 256], BF16)
        wbd = sb("wbd_sb", [128, 3, 8, 128], BF16)
        wif = sb("wif_sb", [128, 24, 8], BF16)
        bifb = sb("bifb_sb", [128, 32], F32)
        wg = sb("wg", [128, 16, 8], BF16)
        chv = sb("chv_sb", [128, 8, 8], F32)
        nwb = sb("nwb", [128, D], F32)
        identb = sb("identb", [128, 128], BF16)
        cstf = sb("cstf", [128, 4, 128], F32)
        invc = sb("invc", [128, 4, 16], F32)
        brb = sb("brb_sb", [128, 4, 4], F32)
        cmask = sb("cmask_sb", [128, 4], F32)
        Ax = sb("Ax", [128, 16, 4], F32)
        A2x = sb("A2x", [128, 16, 4], F32)
        EG = sb("EG", [128, 16, 4], F32)
        ENB = sb("ENB", [128, 16, 4], F32)
        Cst = sb("Cst", [128, 4, 2, 258], F32)
        Cbf = sb("Cbf", [128, 4, 2, 258], BF16)
        mxhalo = sb("mxhalo", [128, 8, 4], F32)
        pxhalo = sb("pxhalo", [128, 8, 16], F32)
        ngt = sb("ngt", [128, 4], F32)
        small = sb("small", [128, 16], F32)
        WN = 8170
        work = sb("work", [128, WN], F32)

        identf = cstf[:, 0, :]
        tri = cstf[:, 1, :]
        ones = cstf[:, 2, :]
        mask01 = cstf[:, 3, :]

        PA = ps("PA", [128, 1024])
        PB = ps("PB", [128, 1024])
        P4 = ps("P4", [128, 512])
        P5 = ps("P5", [128, 512])
        P67 = ps("P67", [128, 1024])
        P6 = P67[:, 0:512]
        P7 = P67[:, 512:1024]
        BK = [Tk(True) for _ in range(8)]

        UT = [Tk() for _ in range(17)]
        XC = [[Tk() for _ in range(8)] for _ in range(4)]
        VX = [Tk() for _ in range(16)]
        WST = [[Tk() for _ in range(8)] for _ in range(3)]
        CONST = Tk()
        NWB = Tk()
        GATE = [Tk() for _ in range(4)]
        CS = [Tk() for _ in range(4)]
        CB = [Tk() for _ in range(4)]
        MXH = [Tk() for _ in range(8)]
        PXH = [Tk() for _ in range(8)]
        NGT = Tk()
        OUT = Tk()
        ACC = [Tk() for _ in range(16)]
        CCI = [Tk() for _ in range(4)]
        CCO = [Tk() for _ in range(4)]

        wk = [0]

        def take(n):
            a = work[:, wk[0]:wk[0] + n]
            wk[0] += n
            assert wk[0] <= WN, wk[0]
            return a

        xt = [take(1024) for _ in range(3)]
        sqj = take(512).bitcast(BF16)
        ub = [take(512).bitcast(BF16) for _ in range(2)]
        XT = [Tk(), Tk(), Tk()]
        SQ = Tk()
        UB = [Tk(), Tk()]
        SS = [Tk(), Tk(), Tk()]
        def s0_load(k, ti):
            rows = 16 if ti < 0 else 128
            r0 = 0 if ti < 0 else HALO + 128 * ti
            xs = k % 3
            S.dma("sp", "x%d" % xs, lambda e: e.dma_start(out=xt[xs][:rows, :], in_=x_d[r0:r0 + rows, :]), writes=[XT[xs]])

        s0_load(0, -1)
        s0_load(1, 0)
        s0_load(2, 1)
        def ld(key, q, out, in_, w):
            ins = S.dma(q, key, lambda e: e.dma_start(out=out, in_=in_), writes=[] if w and w[0] is CONST else w)
            if w and w[0] is CONST:
                CONST.w[ins.stream] = ins

        ld("c0", "sp", nwb[:, :], nw_d, [NWB])
        ld("c1", "sp", identb[:, :], identb_d, [CONST])
        ld("c2", "sp", cstf[:, :, :], cst_d.rearrange("p (a b) -> p a b", a=4), [CONST])
        ld("c3", "sp", chv[:, :, :], chv_d.rearrange("p (a b) -> p a b", a=8), [CONST])
        ld("c4", "sp", bifb[:, :], bifb_d, [CONST])
        ld("c5", "sp", invc[:, :, :], invc_d.rearrange("p (a b) -> p a b", a=4), [CONST])
        ld("c6", "sp", brb[:, :, :], brb_d.rearrange("p (a b) -> p a b", a=4), [CONST])
        ld("c7", "sp", cmask[:, :], cmask_d, [CONST])
        def load_w(slot, src2d, after=()):
            for kt in range(8):
                S.dma("pool", "w%d_%d" % (slot, kt),
                      lambda e, kt=kt: e.dma_start(out=WS[slot][:, kt, :], in_=src2d[kt * 128:(kt + 1) * 128, :]),
                      reads=list(after), writes=[WST[slot][kt]])

        S.op("pool", lambda e: e.memset(Cst[:, :, :, :], 0.0), writes=CS)
        S.op("pool", lambda e: e.memset(vext[:, :, :, 256:258], 1.0), writes=VX)
        S.op("pool", lambda e: e.memset(ngt[:, :], 0.0), writes=[NGT])

        WBD = Tk()
        WIF = Tk()
        POOLW = Tk()
        WBT = Tk()
        DGT = Tk()
        WG = Tk()
        WSC = [Tk() for _ in range(4)]
        wbdT = work[:, 5808:5808 + 1536].bitcast(BF16).rearrange("p (w c n) -> p w c n", w=3, c=8)
        dg = work[:, 3760:3760 + 2048].bitcast(BF16).rearrange("p (c j n) -> p c j n", c=8, j=4)

        def load_wmx(cb, after):
            S.dma("pool", "wc%d" % cb, lambda e: e.dma_start(
                out=WS[0][:, :, cb * 256:(cb + 1) * 256],
                in_=win_d[:, 2048 + cb * 256:2048 + (cb + 1) * 256].rearrange("(kt p) n -> p kt n", p=128)),
                reads=list(after), writes=[WSC[cb]])

        def ld_after(key, out, in_, w, after):
            S.dma("pool", key, lambda e: e.dma_start(out=out, in_=in_), reads=list(after), writes=w)

        load_wmx(0, [])
        ld("c9", "pool", wif[:, :, :], wif_d.rearrange("p (k n) -> p k n", k=24), [WIF])
        late_loads = {
            3: lambda: load_wmx(1, [UT[3]]),
            6: lambda: ld_after("c11", wbdT, wbdT_d.rearrange("p (w c n) -> p w c n", w=3, c=8), [WBT], [UT[6]]),
            8: lambda: load_wmx(2, [UT[8]]),
            10: lambda: ld_after("c8", wbd[:, :, :, :], wbd_d.rearrange("p (w c n) -> p w c n", w=3, c=8), [WBD], [UT[10]]),
            12: lambda: load_wmx(3, [UT[12]]),
            16: lambda: ld_after("c10", poolw[:, :, :, :], poolw_d.rearrange("(g c p) n -> p g c n", g=4, c=2), [POOLW], [UT[16]]),
        }

        PTbs = [P6.bitcast(BF16), P7.bitcast(BF16)]

        def s0_front(k, ti):
            rows = 16 if ti < 0 else 128
            r0 = 0 if ti < 0 else HALO + 128 * ti
            sl = k % 2
            xs = k % 3
            PTb = PTbs[sl]
            ss = small[:, xs * 2:xs * 2 + 1]
            rs = small[:, xs * 2 + 1:xs * 2 + 2]
            if k >= 3:
                s0_load(k, ti)
            S.op("act", lambda e: e.activation(out=sqj[:rows, :], in_=xt[xs][:rows, :], func=AF.Square, accum_out=ss[:rows, :]),
                 reads=[XT[xs]], writes=[SQ, SS[xs]])
            S.op("act", lambda e: e.activation(out=rs[:rows, :], in_=ss[:rows, :], func=AF.Sqrt, scale=1.0 / D, bias=EPS),
                 reads=[SS[xs]], writes=[SS[xs]])
            S.op("dve", lambda e: e.reciprocal(out=rs[:rows, :], in_=rs[:rows, :]), reads=[SS[xs]], writes=[SS[xs]])
            S.op("dve", lambda e: e.scalar_tensor_tensor(
                out=ub[sl][:rows, :], in0=xt[xs][:rows, :], scalar=rs[:rows, :], in1=nwb[:rows, :],
                op0=ALU.mult, op1=ALU.mult), reads=[XT[xs], SS[xs], NWB], writes=[UB[sl]])
            for kt in range(8):
                S.op("pe", lambda e, kt=kt: e.transpose(
                    out=PTb[:, kt * 128:kt * 128 + rows], in_=ub[sl][:rows, kt * 128:(kt + 1) * 128],
                    identity=identb[:rows, :rows]), reads=[UB[sl], CONST], writes=[BK[6 + sl]])

        def s0_back(k, ti):
            rows = 16 if ti < 0 else 128
            r0 = 0 if ti < 0 else HALO + 128 * ti
            sl = k % 2
            PTb = PTbs[sl]
            S.op("act", lambda e: e.activation(
                out=uT[:, :, r0:r0 + rows], in_=PTb.rearrange("p (k t) -> p k t", k=8)[:, :, 0:rows], func=AF.Copy),
                reads=[BK[6 + sl]], writes=[UT[k]])

        tiles = list(enumerate(range(-1, 16)))
        s0_front(*tiles[0])
        for i in range(len(tiles)):
            if i + 1 < len(tiles):
                s0_front(*tiles[i + 1])
            s0_back(*tiles[i])
            if tiles[i][0] in late_loads:
                late_loads[tiles[i][0]]()

        for nt in range(8):
            S.op("pe", lambda e, nt=nt: e.matmul(P6[:, nt * 8:(nt + 1) * 8], lhsT=wbdT[:, 0, nt, :], rhs=wif[:, nt, :],
                                                 start=True, stop=False), reads=[WBT, WIF], writes=[BK[6]])
            S.op("pe", lambda e, nt=nt: e.matmul(P6[:, nt * 8:(nt + 1) * 8], lhsT=wbdT[:, 1, nt, :], rhs=wif[:, 8 + nt, :],
                                                 start=False, stop=True), reads=[WBT, WIF], writes=[BK[6]])
            S.op("pe", lambda e, nt=nt: e.matmul(P6[:, 64 + nt * 8:64 + (nt + 1) * 8], lhsT=wbdT[:, 2, nt, :], rhs=wif[:, 16 + nt, :],
                                                 start=True, stop=True), reads=[WBT, WIF], writes=[BK[6]])
        S.op("act", lambda e: e.activation(out=wg[:, :, :], in_=P6[:, 0:128].rearrange("p (k n) -> p k n", k=16), func=AF.Copy),
             reads=[BK[6]], writes=[WG])
        for o in UB:
            for s_, i_ in list(o.w.items()) + list(o.r.items()):
                if s_ not in DGT.w or DGT.w[s_].gid < i_.gid:
                    DGT.w[s_] = i_
        for nt in range(8):
            for j in range(4):
                S.op("dve", lambda e, nt=nt, j=j: e.tensor_scalar(out=dg[:, nt, j, :], in0=identb[:, :], scalar1=chv[:, nt, j:j + 1],
                                                                  scalar2=None, op0=ALU.mult), reads=[CONST], writes=[DGT])
        load_w(1, win_d[:, 0:1024], after=[UT[16]])
        load_w(2, win_d[:, 1024:2048], after=[UT[16]])
        ld("c0", "sp", nwb[:, :], fnw_d, [NWB])

        STG0 = XT + [SQ] + UB
        wk[0] = 0
        mxb = take(2064).bitcast(BF16).rearrange("p (k t) -> p k t", k=8)
        k2 = [work[:, 5808 + 512 * i_:5808 + 512 * (i_ + 1)].bitcast(BF16) for i_ in range(2)]
        gtm = take(32)
        ef = take(16)
        lfn = take(16)
        nbg = take(32)
        t1 = take(16)
        t2 = take(16)
        assert wk[0] <= 3760
        MXB = [Tk() for _ in range(8)]
        K2 = [[Tk(), Tk()], [Tk(), Tk()]]
        GTM = Tk()
        STGA = MXB + [GTM]
        for t in STGA:
            for o in STG0:
                for s_, i_ in list(o.w.items()) + list(o.r.items()):
                    if s_ not in t.w or t.w[s_].gid < i_.gid:
                        t.w[s_] = i_

        STGA = STGA + [DGT, WBT]
        for t in K2[0] + K2[1]:
            for s_, i_ in list(WBT.w.items()) + list(WBT.r.items()):
                if s_ not in t.w or t.w[s_].gid < i_.gid:
                    t.w[s_] = i_
        PAh = [PA[:, 0:512], PA[:, 512:1024]]
        PBh = [PB[:, 0:512], PB[:, 512:1024]]

        def proj_fm(slot, nt, c0, n, hb, bank_base, pview):
            uts = [0] if c0 < HALO else list(range((c0 - HALO) // 128 + 1, (c0 + n - 1 - HALO) // 128 + 2))
            utk = [UT[i] for i in uts]
            for kt in range(8):
                S.op("pe", lambda e, kt=kt: e.matmul(
                    pview[:, 0:n], lhsT=WS[slot][:, kt, nt * 128:(nt + 1) * 128], rhs=uT[:, kt, c0:c0 + n],
                    start=(kt == 0), stop=(kt == 7)), reads=[WST[slot][kt]] + ([WSC[nt // 2]] if slot == 0 else []) + utk,
                    writes=[BK[bank_base + hb]])

        def A_proj(b, nt, part=None):
            hb = nt % 2
            if b < 0:
                proj_fm(0, nt, 0, HALO, hb, 0, PAh[hb])
                S.op("act", lambda e: e.activation(out=mxb[:, nt, 1:4], in_=PAh[hb][:, 13:16], func=AF.Copy),
                     reads=[BK[hb]], writes=[MXB[nt]])
                return
            c0 = HALO + 512 * b
            if part in (None, "p"):
                proj_fm(0, nt, c0, 512, hb, 0, PAh[hb])
                if b > 0:
                    S.op("dve", lambda e: e.tensor_copy(out=mxb[:, nt, 1:4], in_=mxb[:, nt, 513:516]),
                         reads=[MXB[nt]], writes=[MXB[nt]])
                S.op("act", lambda e: e.activation(out=mxb[:, nt, 4:516], in_=PAh[hb][:, 0:512], func=AF.Copy),
                     reads=[BK[hb]], writes=[MXB[nt]])
            if part in (None, "c"):
                for j in range(4):
                    S.op("pe", lambda e, j=j: e.matmul(PBh[hb][:, 0:512], lhsT=dg[:, nt, j, :], rhs=mxb[:, nt, 1 + j:1 + j + 512],
                                                       start=(j == 0), stop=(j == 3)), reads=[MXB[nt], DGT], writes=[BK[2 + hb]])
                S.op("act", lambda e: e.activation(out=xcT[:, nt, b * 512:(b + 1) * 512], in_=PBh[hb][:, 0:512], func=AF.Silu,
                                                   bias=chv[:, nt, 4:5]), reads=[BK[2 + hb], CONST], writes=[XC[b][nt]])

        def A_vg(b):
            for j in range(4):
                c = 4 * b + j
                tc0 = b * 512 + j * 128
                pvv, bk0 = (PB, 2) if j % 2 == 0 else (PA, 0)
                for ct in range(8):
                    S.op("pe", lambda e, ct=ct, j=j, pvv=pvv: e.matmul(
                        pvv[:, ct * 128:(ct + 1) * 128], lhsT=mxb[:, ct, 4 + j * 128:4 + (j + 1) * 128], rhs=wbd[:, 2, ct, :],
                        start=True, stop=True), reads=[MXB[ct], WBD], writes=[BK[bk0 + ct // 4]])
                S.op("act", lambda e, c=c, pvv=pvv: e.activation(out=vext[:, c, :, 0:256],
                                                               in_=pvv[:, :].rearrange("p (h e) -> p h e", h=4), func=AF.Copy),
                     reads=[BK[bk0], BK[bk0 + 1]], writes=[VX[c]])
                for k in range(16):
                    if k < 8:
                        lh, rd = xcT[:, k, tc0:tc0 + 128], XC[b][k]
                    else:
                        lh, rd = mxb[:, k - 8, 4 + j * 128:4 + (j + 1) * 128], MXB[k - 8]
                    S.op("pe", lambda e, k=k, lh=lh, j=j: e.matmul(P4[:, j * 8:(j + 1) * 8], lhsT=lh, rhs=wg[:, k, :],
                                                                   start=(k == 0), stop=(k == 15)), reads=[rd, WG], writes=[BK[4]])

        def A_gmath(b, part=None):
            if part in (None, "g1"):
                S.op("dve", lambda e: e.tensor_tensor(out=gtm[:, :], in0=P4[:, 0:32], in1=bifb[:, :], op=ALU.add),
                     reads=[BK[4], CONST], writes=[GTM])
            g3 = gtm.rearrange("p (a b) -> p a b", a=4)
            ef3 = ef.rearrange("p (a b) -> p a b", a=4)
            lf3 = lfn.rearrange("p (a b) -> p a b", a=4)
            t13 = t1.rearrange("p (a b) -> p a b", a=4)
            t23 = t2.rearrange("p (a b) -> p a b", a=4)
            nb3 = nbg[:, 0:16].rearrange("p (a b) -> p a b", a=4)
            ngg3 = nbg[:, 16:32].rearrange("p (a b) -> p a b", a=4)
            if part in (None, "g1"):
                S.op("act", lambda e: e.activation(out=ef3, in_=g3[:, :, 4:8], func=AF.Exp, scale=-1.0), reads=[GTM], writes=[GTM])
            if part in (None, "g1"):
                S.op("act", lambda e: e.activation(out=lf3, in_=ef3, func=AF.Ln, bias=1.0), reads=[GTM], writes=[GTM])
            if part == "g1":
                return
            S.op("pe", lambda e: e.matmul(P4[:, 64:80], lhsT=tri, rhs=lfn[:, :], start=True, stop=True),
                 reads=[GTM, CONST], writes=[BK[4]])
            S.op("pe", lambda e: e.matmul(P4[:, 80:96], lhsT=ones, rhs=lfn[:, :], start=True, stop=True),
                 reads=[GTM, CONST], writes=[BK[4]])
            S.op("dve", lambda e: e.tensor_copy(out=nbg[:, :], in_=P4[:, 64:96]), reads=[BK[4]], writes=[GTM])
            S.op("dve", lambda e: e.tensor_tensor(out=t13, in0=g3[:, :, 0:4], in1=nb3, op=ALU.add), reads=[GTM], writes=[GTM])
            S.op("dve", lambda e: e.tensor_tensor(out=t23, in0=t13, in1=ngg3, op=ALU.subtract), reads=[GTM], writes=[GTM])
            bs = slice(4 * b, 4 * b + 4)
            S.op("act", lambda e: e.activation(out=Ax[:, bs, :], in_=t13, func=AF.Exp), reads=[GTM], writes=[GATE[b]])
            S.op("act", lambda e: e.activation(out=A2x[:, bs, :], in_=t23, func=AF.Exp), reads=[GTM], writes=[GATE[b]])
            S.op("act", lambda e: e.activation(out=EG[:, bs, :], in_=ngg3, func=AF.Exp, scale=-1.0), reads=[GTM], writes=[GATE[b]])
            S.op("act", lambda e: e.activation(out=ENB[:, bs, :], in_=nb3, func=AF.Exp, scale=2.0), reads=[GTM], writes=[GATE[b]])
            S.op("dve", lambda e: e.tensor_scalar(out=Ax[:, bs, :], in0=Ax[:, bs, :], scalar1=0.0625, scalar2=None, op0=ALU.mult),
                 reads=[GATE[b]], writes=[GATE[b]])
            S.op("dve", lambda e: e.tensor_scalar(out=A2x[:, bs, :], in0=A2x[:, bs, :], scalar1=0.0625, scalar2=None, op0=ALU.mult),
                 reads=[GATE[b]], writes=[GATE[b]])
            for j in range(4):
                S.op("dve", lambda e, j=j: e.tensor_tensor(out=ngt[:, :], in0=ngt[:, :], in1=ngg3[:, j, :], op=ALU.add),
                     reads=[GTM, NGT], writes=[NGT])

        def A_scan_k(b, j):
            c = 4 * b + j
            ks = c % 2
            tc0 = b * 512 + j * 128
            for pair in range(2):
                pk, bk = (P5, 5) if pair == 0 else (P4, 4)
                for q4 in range(4):
                    ct = 4 * pair + q4
                    S.op("pe", lambda e, ct=ct, q4=q4, pk=pk: e.matmul(
                        pk[:, q4 * 128:(q4 + 1) * 128], lhsT=xcT[:, ct, tc0:tc0 + 128], rhs=wbd[:, 1, ct, :],
                        start=True, stop=True), reads=[XC[b][ct], WBD], writes=[BK[bk]])
                for hh in range(2):
                    h = 2 * pair + hh
                    S.op("act", lambda e, h=h, hh=hh, pk=pk: e.activation(
                        out=k2[ks][:, h * 256:(h + 1) * 256], in_=pk[:, hh * 256:(hh + 1) * 256], func=AF.Identity,
                        scale=A2x[:, c, h:h + 1]), reads=[BK[bk], GATE[b]], writes=[K2[ks][pair]])

        def A_scan_u(b, j, h):
            c = 4 * b + j
            ks = c % 2
            pair = h // 2
            for dt in range(2):
                pu = P6 if dt == 0 else P7
                S.op("pe", lambda e, dt=dt, pu=pu: e.matmul(
                    pu[:, 0:257], lhsT=k2[ks][:, h * 256 + dt * 128:h * 256 + dt * 128 + 128],
                    rhs=vext[:, c, h, 0:257], start=True, stop=True), reads=[K2[ks][pair], VX[c]], writes=[BK[6 + dt]])
            S.op("dve", lambda e: e.scalar_tensor_tensor(
                out=Cst[:, h, :, 0:257], in0=Cst[:, h, :, 0:257], scalar=EG[:, c, h:h + 1],
                in1=P67[:, :].rearrange("p (a b) -> p a b", a=2)[:, :, 0:257],
                op0=ALU.mult, op1=ALU.add), reads=[CS[h], GATE[b], BK[6], BK[7]], writes=[CS[h]])

        def A_scan(b, j):
            A_scan_k(b, j)
            for h in range(4):
                A_scan_u(b, j, h)

        for nt in range(8):
            A_proj(-1, nt)
        for i in range(4):
            A_proj(0, 2 * i, "p")
            A_proj(0, 2 * i + 1, "p")
            A_proj(0, 2 * i, "c")
            A_proj(0, 2 * i + 1, "c")
        A_vg(0)
        deferred = []
        for b in range(4):
            for i in range(4):
                if b < 3:
                    if i == 0:
                        A_proj(b + 1, 0, "p")
                        A_proj(b + 1, 1, "p")
                        A_gmath(b, "g1")
                        A_proj(b + 1, 0, "c")
                        A_proj(b + 1, 1, "c")
                        A_gmath(b, "g2")
                        A_scan_k(b, 0)
                        A_scan_u(b, 0, 0)
                        A_scan_u(b, 0, 1)
                        A_scan_u(b, 0, 2)
                        A_scan_u(b, 0, 3)
                    else:
                        A_scan_k(b, i)
                        A_scan_u(b, i, 0)
                        A_proj(b + 1, 2 * i, "p")
                        A_scan_u(b, i, 1)
                        A_proj(b + 1, 2 * i + 1, "p")
                        A_scan_u(b, i, 2)
                        A_proj(b + 1, 2 * i, "c")
                        A_scan_u(b, i, 3)
                        A_proj(b + 1, 2 * i + 1, "c")
                else:
                    if i == 0:
                        A_gmath(b)
                    deferred.append(lambda i=i: A_scan(3, i))
            if b < 3:
                A_vg(b + 1)

        load_w(0, wout_d[0:1024, :])

        def publish():
          for h in range(4):
              for dt in range(2):
                  S.op("dve", lambda e, h=h, dt=dt: e.tensor_copy(out=Cst[:, h, dt, 257:258], in_=ngt[:, h:h + 1]),
                       reads=[NGT, CS[h]], writes=[CS[h]])
              S.dma("sp", "cci%d" % h, lambda e, h=h: e.dma_start(out=cci_t[h].ap(), in_=Cst[:, h, :, :].rearrange("p a b -> p (a b)")),
                    reads=[CS[h]], writes=[CCI[h]])
              S.dma("pool", "cc%d" % h, lambda e, h=h: e.collective_compute(
                  "AllGather", ALU.bypass, replica_groups=[[0, 1, 2, 3], [4, 5, 6, 7]],
                  ins=[cci_t[h].ap().opt()], outs=[cco_t[h].ap().opt()]), reads=[CCI[h]], writes=[CCO[h]], inc=1)

        wk[0] = 0
        pxe = [take(528) for _ in range(2)]
        tA1 = take(528)
        tA = [tA1, tA1]
        tB1 = take(528)
        tB = [tB1, tB1]
        szp = [take(256).bitcast(BF16) for _ in range(2)]
        dTt2 = [take(512).bitcast(BF16).rearrange("p (k t) -> p k t", k=2) for _ in range(2)]
        ypT = take(2048).bitcast(BF16).rearrange("p (k t) -> p k t", k=8)
        xacc = [take(512) for _ in range(4)]
        tmp16 = take(16)
        PXE = [Tk(), Tk()]
        TA1 = Tk()
        TA = [TA1, TA1]
        TB1 = Tk()
        TB = [TB1, TB1]
        SZP = [Tk(), Tk()]
        DT2 = [[Tk(), Tk()], [Tk(), Tk()]]
        YP = [Tk() for _ in range(8)]
        XACC = [Tk() for _ in range(4)]
        STGB = PXE + [TA1, TB1] + SZP + DT2[0] + DT2[1] + YP + XACC
        for t in XACC:
            for o in K2[0] + K2[1]:
                for s_, i_ in list(o.w.items()) + list(o.r.items()):
                    if s_ not in t.w or t.w[s_].gid < i_.gid:
                        t.w[s_] = i_
        for t in STGB:
            for o in STGA:
                for s_, i_ in list(o.w.items()) + list(o.r.items()):
                    if s_ not in t.w or t.w[s_].gid < i_.gid:
                        t.w[s_] = i_

        for nt in range(8):
            proj_fm(1, nt, 0, HALO, nt % 2, 0, PAh[nt % 2])
            S.op("act", lambda e, nt=nt: e.activation(out=pxhalo[:, nt, :], in_=PAh[nt % 2][:, 0:16], func=AF.Copy),
                 reads=[BK[nt % 2]], writes=[PXH[nt]])

        def B_px(b, g, n2):
            dTt, DT = dTt2[g % 2], DT2[g % 2]
            c0 = HALO + 512 * b
            win = 2 ** (g + 1)
            nt = 2 * g + n2
            proj_fm(1, nt, c0, 512, n2, 0, PAh[n2])
            S.op("act", lambda e: e.activation(out=pxe[n2][:, 16:528], in_=PAh[n2][:, 0:512], func=AF.Copy),
                 reads=[BK[n2]], writes=[PXE[n2]])
            S.op("pool", lambda e: e.tensor_copy(out=pxe[n2][:, 0:16], in_=pxhalo[:, nt, :]),
                 reads=[PXH[nt], PXE[n2]], writes=[PXE[n2]])
            S.op("pool", lambda e: e.tensor_copy(out=pxhalo[:, nt, :], in_=pxe[n2][:, 512:528]),
                 reads=[PXE[n2]], writes=[PXH[nt]])
            cur, curT = pxe[n2], PXE[n2]
            v = 0
            for lev in range(g + 1):
                sh = 2 ** lev
                nv = v + sh
                dst, dstT = (tA[n2], TA[n2]) if lev % 2 == 0 else (tB[n2], TB[n2])
                eng = "dve" if lev % 2 == 0 else "pool"
                S.op(eng, lambda e, dst=dst, cur=cur, nv=nv, sh=sh: e.tensor_tensor(
                    out=dst[:, nv:528], in0=cur[:, nv:528], in1=cur[:, nv - sh:528 - sh], op=ALU.add),
                    reads=[curT], writes=[dstT])
                cur, curT, v = dst, dstT, nv
            S.op("dve", lambda e, cur=cur: e.scalar_tensor_tensor(
                out=dTt[:, n2, :], in0=cur[:, 16:528], scalar=1.0 / win, in1=pxe[n2][:, 16:528],
                op0=ALU.mult, op1=ALU.subtract), reads=[curT, PXE[n2]], writes=[DT[n2]])
            if b == 0:
                S.op("dve", lambda e, cur=cur: e.tensor_tensor(out=tmp16[:, :], in0=cur[:, 16:32], in1=invc[:, g, :], op=ALU.mult),
                     reads=[curT, CONST, DT[n2]], writes=[DT[n2]])
                S.op("dve", lambda e: e.tensor_tensor(out=dTt[:, n2, 0:16], in0=tmp16[:, :], in1=pxe[n2][:, 16:32], op=ALU.subtract),
                     reads=[PXE[n2], DT[n2]], writes=[DT[n2]])

        def B_pz(b, g, cot):
            c0 = HALO + 512 * b
            nt = 2 * g + cot
            proj_fm(2, nt, c0, 512, cot, 2, PBh[cot])
            S.op("act", lambda e: e.activation(out=szp[cot][:, :], in_=PBh[cot][:, 0:512], func=AF.Silu),
                 reads=[BK[2 + cot]], writes=[SZP[cot]])

        def B_pw(b, g, cot):
            dTt, DT = dTt2[g % 2], DT2[g % 2]
            nt = 2 * g + cot
            pv = P4 if cot == 0 else P5
            for cit in range(2):
                S.op("pe", lambda e, cit=cit: e.matmul(
                    pv[:, 0:512], lhsT=poolw[:, g, cit, cot * 128:(cot + 1) * 128], rhs=dTt[:, cit, :],
                    start=(cit == 0), stop=(cit == 1)), reads=[DT[0], DT[1], POOLW], writes=[BK[4 + cot]])
            S.op("dve", lambda e: e.scalar_tensor_tensor(
                out=ypT[:, nt, :], in0=pv[:, 0:512], scalar=chv[:, nt, 5:6], in1=szp[cot][:, :],
                op0=ALU.mult, op1=ALU.mult), reads=[BK[4 + cot], SZP[cot], CONST], writes=[YP[nt]])

        XB = xacc + [pxe[0][:, 0:512], pxe[1][:, 0:512], tA1[:, 0:512], tB1[:, 0:512]]
        XBT = XACC + [PXE[0], PXE[1], TA1, TB1]

        def B_ld(b, u, extra=()):
            j, half = divmod(u, 2)
            c = 4 * b + j
            xb = u if b == 3 else u % 4
            S.dma("sp", "xa%d" % xb, lambda e: e.dma_start(
                out=XB[xb][:, :], in_=x_d[HALO + c * 128:HALO + (c + 1) * 128, half * 512:(half + 1) * 512]),
                writes=[XBT[xb]] + list(extra))

        def B_out(b, u):
            j, half = divmod(u, 2)
            c = 4 * b + j
            xb = u if b == 3 else u % 4
            pv = P6 if half == 0 else P7
            for kt in range(8):
                S.op("pe", lambda e, kt=kt: e.matmul(
                    pv[:, 0:512], lhsT=ypT[:, kt, j * 128:(j + 1) * 128],
                    rhs=WS[0][:, kt, half * 512:(half + 1) * 512], start=(kt == 0), stop=(kt == 7)),
                    reads=[YP[kt], WST[0][kt]], writes=[BK[6 + half]])
            S.op("dve", lambda e: e.tensor_tensor(out=XB[xb][:, :], in0=pv[:, 0:512], in1=XB[xb][:, :], op=ALU.add),
                 reads=[BK[6 + half], XBT[xb]], writes=[XBT[xb]])
            S.dma("sp", "acc%d" % xb, lambda e: e.dma_start(
                out=acc_d[c * 128:(c + 1) * 128, half * 512:(half + 1) * 512], in_=XB[xb][:, :]),
                reads=[XBT[xb]], writes=[ACC[c]])
            if b < 3 and u + 4 < 8:
                B_ld(b, u + 4)

        seq = [(b, g) for b in range(4) for g in range(4)]
        B_px(0, 0, 0)
        B_px(0, 0, 1)
        for i, (b, g) in enumerate(seq):
            nxt = seq[i + 1] if i + 1 < len(seq) else None
            if g == 3 and b > 0:
                for u in range(4):
                    B_ld(b, u)
            B_pz(b, g, 0)
            if nxt:
                B_px(nxt[0], nxt[1], 0)
            B_pz(b, g, 1)
            if nxt:
                B_px(nxt[0], nxt[1], 1)
            if i == len(seq) - 1:
                for u in range(4, 8):
                    B_ld(3, u)
                load_w(1, win_d[:, 3072:4096])
                load_w(2, win_d[:, 4096:5120])
            if g == 0 and b > 0:
                for u in range(8):
                    B_out(b - 1, u)
            B_pw(b, g, 0)
            B_pw(b, g, 1)
            if deferred:
                deferred.pop(0)()
                if not deferred:
                    publish()
                    for u in range(4):
                        B_ld(0, u, extra=K2[0] + K2[1])
        for u in range(8):
            B_out(3, u)

        wk[0] = 0
        gathA = take(2064).rearrange("p (r n) -> p r n", r=4)
        gathB = work[:, 2064:4128].rearrange("p (r n) -> p r n", r=4)
        qT = [take(512).bitcast(BF16).rearrange("p (k t) -> p k t", k=8) for _ in range(2)]
        kT = [take(512).bitcast(BF16).rearrange("p (k t) -> p k t", k=8) for _ in range(2)]
        take(16)
        szT = take(1024).bitcast(BF16).rearrange("p (k t) -> p k t", k=8)
        so = [take(512).bitcast(BF16) for _ in range(2)]
        k2c1 = take(512).bitcast(BF16)
        k2c = [k2c1, k2c1]
        Wt = [take(64).bitcast(BF16) for _ in range(2)]
        hg = [take(256) for _ in range(2)]
        hn = [take(128).bitcast(BF16) for _ in range(2)]
        tt = [take(128) for _ in range(2)]
        cf = take(64)
        stt = take(64)
        wk_end = wk[0]
        wk[0] = 0
        ymT = [take(512).bitcast(BF16).rearrange("p (k t) -> p k t", k=8) for _ in range(2)]
        res = [take(512) for _ in range(2)]
        assert wk[0] <= 2064
        wk[0] = wk_end
        GA = Tk()
        GA2 = Tk()
        QT = [[Tk() for _ in range(2)] for _ in range(2)]
        KT = [[Tk() for _ in range(2)] for _ in range(2)]
        SZ = [Tk() for _ in range(4)]
        SO = [[Tk(), Tk()], [Tk(), Tk()]]
        K2C1 = [Tk(), Tk()]
        K2C = [K2C1, K2C1]
        WT = [Tk(), Tk()]
        HG = [Tk(), Tk()]
        HN = [Tk(), Tk()]
        TTt = [Tk(), Tk()]
        YM = [[Tk() for _ in range(4)] for _ in range(2)]
        RES = [Tk(), Tk()]
        CF = Tk()
        STAT = [Tk(), Tk()]
        allc = [GA, GA2, CF] + SZ + WT + HG + HN + TTt + RES + STAT
        for l_ in (QT, KT, SO, [K2C1], YM):
            for x_ in l_:
                allc += x_
        for t in allc:
            for o in STGB + STGA:
                for s_, i_ in list(o.w.items()) + list(o.r.items()):
                    if s_ not in t.w or t.w[s_].gid < i_.gid:
                        t.w[s_] = i_

        PT7 = P4[:, :].bitcast(BF16)[:, 512:768]
        PJ1 = PAh[1].bitcast(BF16)
        xcb = lambda c: XC[c // 4]

        def T_mo(c, half, part=None):
            p = c % 2
            ucol = HALO + c * 128
            if part in (None, "pe"):
                for kt in range(8):
                    S.op("pe", lambda e, kt=kt: e.matmul(
                        PAh[0][:, 0:512], lhsT=uT[:, kt, ucol:ucol + 128], rhs=WS[2][:, kt, half * 512:(half + 1) * 512],
                        start=(kt == 0), stop=(kt == 7)), reads=[WST[2][kt], UT[c + 1]], writes=[BK[0]])
            if part in (None, "ev"):
                S.op("act", lambda e: e.activation(out=so[p][:, half * 512:(half + 1) * 512], in_=PAh[0][:, 0:512], func=AF.Tanh, scale=0.5),
                     reads=[BK[0]], writes=[SO[p][half]])

        def T_k2(c, pair):
            p = c % 2
            tc0 = c * 128
            pv = PBh[pair]
            for q4 in range(4):
                ct = 4 * pair + q4
                S.op("pe", lambda e, ct=ct, q4=q4: e.matmul(
                    pv[:, q4 * 128:(q4 + 1) * 128], lhsT=xcT[:, ct, tc0:tc0 + 128], rhs=wbd[:, 1, ct, :],
                    start=True, stop=True), reads=[xcb(c)[ct], WBD], writes=[BK[2 + pair]])
            for hh in range(2):
                h = 2 * pair + hh
                S.op("act", lambda e, h=h, hh=hh: e.activation(
                    out=k2c[p][:, h * 256:(h + 1) * 256], in_=pv[:, hh * 256:(hh + 1) * 256], func=AF.Identity,
                    scale=A2x[:, c, h:h + 1]), reads=[BK[2 + pair], GATE[c // 4]], writes=[K2C[p][pair]])

        def T_qk(c, pair, which):
            p = c % 2
            tc0 = c * 128
            pv = PBh[which]
            dst, dstT = (qT, QT) if which == 0 else (kT, KT)
            for q4 in range(4):
                nt = 4 * pair + q4
                S.op("pe", lambda e, nt=nt, q4=q4: e.matmul(
                    pv[:, q4 * 128:(q4 + 1) * 128], lhsT=wbd[:, which, nt, :], rhs=xcT[:, nt, tc0:tc0 + 128],
                    start=True, stop=True), reads=[xcb(c)[nt], WBD], writes=[BK[2 + which]])
            S.op("act", lambda e: e.activation(out=dst[p][:, 4 * pair:4 * pair + 4, :],
                                               in_=pv[:, 0:512].rearrange("p (k t) -> p k t", k=4), func=AF.Copy),
                 reads=[BK[2 + which]], writes=[dstT[p][pair]])

        def T_szp(sbi, np_):
            c0 = HALO + sbi * 256
            hb = np_ % 2
            for q2 in range(2):
                nt = 2 * np_ + q2
                uts = list(range((c0 - HALO) // 128 + 1, (c0 + 255 - HALO) // 128 + 2))
                for kt in range(8):
                    S.op("pe", lambda e, kt=kt, nt=nt, q2=q2: e.matmul(
                        PBh[hb][:, q2 * 256:(q2 + 1) * 256], lhsT=WS[1][:, kt, nt * 128:(nt + 1) * 128],
                        rhs=uT[:, kt, c0:c0 + 256], start=(kt == 0), stop=(kt == 7)),
                        reads=[WST[1][kt]] + [UT[i] for i in uts], writes=[BK[2 + hb]])
            S.op("act", lambda e: e.activation(
                out=szT[:, 2 * np_:2 * np_ + 2, :], in_=PBh[hb][:, 0:512].rearrange("p (k t) -> p k t", k=2), func=AF.Silu),
                reads=[BK[2 + hb]], writes=[SZ[np_]])

        def T_sz(sbi):
            for np_ in range(4):
                T_szp(sbi, np_)

        def T_st(c, h):
            p = c % 2
            ws = h % 2
            for dt in range(2):
                S.op("pe", lambda e, dt=dt: e.matmul(
                    P4[:, 0:128], lhsT=kT[p][:, 2 * h + dt, :], rhs=qT[p][:, 2 * h + dt, :],
                    start=(dt == 0), stop=(dt == 1)), reads=[KT[p][h // 2], QT[p][h // 2]], writes=[BK[4]])
            S.op("dve", lambda e: e.scalar_tensor_tensor(
                out=Wt[ws][:, :], in0=P4[:, 0:128], scalar=Ax[:, c, h:h + 1], in1=mask01,
                op0=ALU.mult, op1=ALU.mult), reads=[BK[4], GATE[c // 4], CONST], writes=[WT[ws]])

        def T_nd(c, h):
            p = c % 2
            ws = h % 2
            pv, bk = (P5, 5) if h % 2 == 0 else (PAh[1], 1)
            S.op("pe", lambda e: e.matmul(pv[:, 0:257], lhsT=Wt[ws][:, :], rhs=vext[:, c, h, 0:257],
                                          start=True, stop=False), reads=[WT[ws], VX[c]], writes=[BK[bk]])
            for dt in range(2):
                S.op("pe", lambda e, dt=dt: e.matmul(
                    pv[:, 0:257], lhsT=qT[p][:, 2 * h + dt, :], rhs=Cbf[:, h, dt, 0:257],
                    start=False, stop=(dt == 1)), reads=[QT[p][h // 2], CB[h]], writes=[BK[bk]])

        def T_u(c, h, dt):
            p = c % 2
            pu = P6 if dt == 0 else P7
            S.op("pe", lambda e: e.matmul(
                pu[:, 0:257], lhsT=k2c[p][:, h * 256 + dt * 128:h * 256 + dt * 128 + 128],
                rhs=vext[:, c, h, 0:257], start=True, stop=True), reads=[K2C[p][h // 2], VX[c]], writes=[BK[6 + dt]])
            if dt == 1:
                S.op("dve", lambda e: e.scalar_tensor_tensor(
                    out=Cst[:, h, :, 0:257], in0=Cst[:, h, :, 0:257], scalar=EG[:, c, h:h + 1],
                    in1=P67[:, :].rearrange("p (a b) -> p a b", a=2)[:, :, 0:257],
                    op0=ALU.mult, op1=ALU.add), reads=[CS[h], GATE[c // 4], BK[6], BK[7]], writes=[CS[h]])

        def T_cb(c, h):
            S.op("pool", lambda e: e.tensor_copy(out=Cbf[:, h, :, 0:257], in_=Cst[:, h, :, 0:257]),
                 reads=[CS[h]], writes=[CB[h]])

        def T_stat(c, h):
            p = c % 2
            ws = h % 2
            pr = h // 2
            pv, bk = (P5, 5) if h % 2 == 0 else (PAh[1], 1)
            S.op("dve", lambda e: e.tensor_copy(out=stt[:, h:h + 1], in_=pv[:, 256:257]), reads=[BK[bk], STAT[pr]], writes=[STAT[pr]])
            S.op("dve", lambda e: e.scalar_tensor_tensor(
                out=hg[ws][:, :], in0=so[p][:, h * 256:(h + 1) * 256], scalar=1.0, in1=pv[:, 0:256],
                op0=ALU.add, op1=ALU.mult), reads=[BK[bk], SO[p][h // 2]], writes=[HG[ws]])
            st6 = stt[:, 32 + 8 * ws:38 + 8 * ws]
            S.op("dve", lambda e: e.bn_stats(out=st6, in_=hg[ws][:, :]), reads=[HG[ws], STAT[pr]], writes=[STAT[pr]])
            S.op("dve", lambda e: e.bn_aggr(out=stt[:, 8 + 2 * h:10 + 2 * h], in_=st6), reads=[STAT[pr]], writes=[STAT[pr]])

        def T_rstd(c, pr):
            hs = slice(2 * pr, 2 * pr + 2)
            dcol = stt[:, hs]
            mv = stt[:, 8:16].rearrange("p (h two) -> p h two", two=2)
            q_ = stt[:, 16 + 2 * pr:18 + 2 * pr]
            r_ = stt[:, 20 + 2 * pr:22 + 2 * pr]
            nb_ = stt[:, 24 + 2 * pr:26 + 2 * pr]
            S.op("dve", lambda e: e.tensor_tensor(out=dcol, in0=dcol, in1=dcol, op=ALU.mult), reads=[STAT[pr]], writes=[STAT[pr]])
            S.op("dve", lambda e: e.tensor_tensor(out=dcol, in0=dcol, in1=ENB[:, c, hs], op=ALU.max),
                 reads=[STAT[pr], GATE[c // 4]], writes=[STAT[pr]])
            S.op("dve", lambda e: e.scalar_tensor_tensor(out=q_, in0=dcol, scalar=4.0 * EPS, in1=mv[:, hs, 1], op0=ALU.mult, op1=ALU.add),
                 reads=[STAT[pr]], writes=[STAT[pr]])
            S.op("act", lambda e: e.activation(out=q_, in_=q_, func=AF.Sqrt), reads=[STAT[pr]], writes=[STAT[pr]])
            S.op("dve", lambda e: e.reciprocal(out=r_, in_=q_), reads=[STAT[pr]], writes=[STAT[pr]])
            S.op("dve", lambda e: e.scalar_tensor_tensor(out=nb_, in0=mv[:, hs, 0], scalar=-1.0, in1=r_, op0=ALU.mult, op1=ALU.mult),
                 reads=[STAT[pr]], writes=[STAT[pr]])

        def T_hn(c, h):
            ws = h % 2
            pr = h // 2
            S.op("act", lambda e: e.activation(out=hn[ws][:, :], in_=hg[ws][:, :], func=AF.Identity,
                                               scale=stt[:, 20 + h:21 + h], bias=stt[:, 24 + h:25 + h]),
                 reads=[HG[ws], STAT[pr]], writes=[HN[ws]])

        def T_tr(c, h):
            p = c % 2
            ws = h % 2
            lo = (c % 2) * 128
            tc0 = c * 128
            for dt in range(2):
                S.op("pe", lambda e, dt=dt: e.transpose(out=PT7[:, dt * 128:(dt + 1) * 128],
                                                        in_=hn[ws][:, dt * 128:(dt + 1) * 128], identity=identb[:, :]),
                     reads=[HN[ws], CONST], writes=[BK[4]])
            for dt in range(2):
                nt = 2 * h + dt
                S.op("act", lambda e, dt=dt, nt=nt: e.activation(out=tt[dt][:, :], in_=PT7[:, dt * 128:(dt + 1) * 128],
                                                                 func=AF.Identity, scale=chv[:, nt, 7:8]),
                     reads=[BK[4], CONST], writes=[TTt[dt]])
                S.op("dve", lambda e, dt=dt, nt=nt: e.scalar_tensor_tensor(
                    out=tt[dt][:, :], in0=xcT[:, nt, tc0:tc0 + 128], scalar=chv[:, nt, 6:7], in1=tt[dt][:, :],
                    op0=ALU.mult, op1=ALU.add), reads=[xcb(c)[nt], TTt[dt], CONST], writes=[TTt[dt]])
                S.op("pool", lambda e, dt=dt, nt=nt: e.tensor_tensor(
                    out=ymT[p][:, nt, :], in0=tt[dt][:, :], in1=szT[:, nt, lo:lo + 128], op=ALU.mult),
                    reads=[TTt[dt], SZ[nt // 2]], writes=[YM[p][h]])

        def T_res(c):
            for half in range(2):
                S.dma("sp", "res%d" % half, lambda e, half=half: e.dma_start(
                    out=res[half][:, :], in_=acc_d[c * 128:(c + 1) * 128, half * 512:(half + 1) * 512]),
                    reads=[ACC[c]], writes=[RES[half]])

        def T_out(c, half, part=None):
            p = c % 2
            if part in (None, "pe"):
                for kt in range(8):
                    S.op("pe", lambda e, kt=kt: e.matmul(
                        PAh[0][:, 0:512], lhsT=ymT[p][:, kt, :], rhs=WS[0][:, kt, half * 512:(half + 1) * 512],
                        start=(kt == 0), stop=(kt == 7)), reads=[YM[p][kt // 2], WST[0][kt]], writes=[BK[0]])
            if part in (None, "ev"):
                S.op("dve", lambda e: e.tensor_tensor(out=res[half][:, :], in0=PAh[0][:, 0:512], in1=res[half][:, :], op=ALU.add),
                     reads=[BK[0], RES[half]], writes=[RES[half]])

        def T_fin(c):
            p = c % 2
            fs = cf[:, 40:42]
            fr = cf[:, 42:43]
            for half in range(2):
                S.op("act", lambda e, half=half: e.activation(out=PAh[0][:, 0:512], in_=res[half][:, :],
                                                              func=AF.Square, accum_out=fs[:, half:half + 1]),
                     reads=[RES[half], CF], writes=[BK[0], CF])
            S.op("dve", lambda e: e.tensor_tensor(out=fr, in0=fs[:, 0:1], in1=fs[:, 1:2], op=ALU.add), reads=[CF], writes=[CF])
            S.op("act", lambda e: e.activation(out=fr, in_=fr, func=AF.Sqrt, scale=1.0 / D, bias=EPS), reads=[CF], writes=[CF])
            S.op("dve", lambda e: e.reciprocal(out=fr, in_=fr), reads=[CF], writes=[CF])
            for half in range(2):
                S.op("dve", lambda e, half=half: e.scalar_tensor_tensor(
                    out=res[half][:, :], in0=res[half][:, :], scalar=fr, in1=nwb[:, half * 512:(half + 1) * 512],
                    op0=ALU.mult, op1=ALU.mult), reads=[RES[half], CF, NWB], writes=[RES[half]])
                S.dma("sp", "out%d" % half, lambda e, half=half: e.dma_start(
                    out=out_d[c * 128:(c + 1) * 128, half * 512:(half + 1) * 512], in_=res[half][:, :]),
                    reads=[RES[half]], writes=[OUT])

        def setup(c):
            T_mo(c, 0)
            T_mo(c, 1)
            T_k2(c, 0)
            T_k2(c, 1)
            T_qk(c, 0, 0)
            T_qk(c, 0, 1)
            T_qk(c, 1, 0)
            T_qk(c, 1, 1)

        T_sz(0)
        T_mo(0, 0)
        T_mo(0, 1)
        T_k2(0, 0)
        T_k2(0, 1)
        for h in range(4):
            gath, GAh = (gathA, GA) if h % 2 == 0 else (gathB, GA2)
            S.dma("sp", "ga%d" % (h % 2), lambda e, h=h, gath=gath: e.dma_start(out=gath[:, :, :], in_=cco_t[h].ap().rearrange("(r p) n -> p r n", p=128)),
                  reads=[CCO[h]], writes=[GAh])
            tm = cf[:, 0:16].rearrange("p (a b) -> p a b", a=4)
            for i in range(4):
                S.op("dve", lambda e, i=i, tm=tm, gath=gath: e.tensor_tensor(out=tm[:, i, :], in0=brb[:, i, :], in1=gath[:, :, 257], op=ALU.mult),
                     reads=[GAh, CONST, CF], writes=[CF])
            S.op("dve", lambda e, tm=tm: e.tensor_reduce(out=cf[:, 16:20], in_=tm, axis=AX.X, op=ALU.add), reads=[CF], writes=[CF])
            S.op("act", lambda e: e.activation(out=cf[:, 20:24], in_=cf[:, 16:20], func=AF.Exp, scale=-1.0), reads=[CF], writes=[CF])
            S.op("dve", lambda e: e.tensor_tensor(out=cf[:, 24:28], in0=cf[:, 20:24], in1=cmask[:, :], op=ALU.mult),
                 reads=[CF, CONST], writes=[CF])
            cflat = Cst[:, h, :, :].rearrange("p a b -> p (a b)")
            S.op("dve", lambda e, cflat=cflat, gath=gath: e.tensor_scalar(out=cflat, in0=gath[:, 0, :], scalar1=cf[:, 24:25], scalar2=None, op0=ALU.mult),
                 reads=[GAh, CF, CS[h]], writes=[CS[h]])
            for i in range(1, 4):
                S.op("dve", lambda e, cflat=cflat, i=i, gath=gath: e.scalar_tensor_tensor(
                    out=cflat, in0=gath[:, i, :], scalar=cf[:, 24 + i:25 + i], in1=cflat, op0=ALU.mult, op1=ALU.add),
                    reads=[GAh, CF, CS[h]], writes=[CS[h]])
            S.op("act", lambda e, h=h: e.activation(out=Cbf[:, h, :, :], in_=Cst[:, h, :, :], func=AF.Copy),
                 reads=[CS[h]], writes=[CB[h]])
        for l_ in YM:
            for t in l_ + RES:
                for s_, i_ in list(GA.w.items()) + list(GA.r.items()):
                    if s_ not in t.w or t.w[s_].gid < i_.gid:
                        t.w[s_] = i_
        for l_ in QT + KT:
            for t in l_:
                for s_, i_ in list(GA2.w.items()) + list(GA2.r.items()):
                    if s_ not in t.w or t.w[s_].gid < i_.gid:
                        t.w[s_] = i_
        T_qk(0, 0, 0)
        T_qk(0, 0, 1)
        T_qk(0, 1, 0)
        T_qk(0, 1, 1)

        load_w(0, wout_d[1024:2048, :], after=[CB[3]])

        def front(c, h, piece=None):
            T_nd(c, h)
            T_u(c, h, 0)
            T_u(c, h, 1)
            if h < 3:
                T_st(c, h + 1)
            if piece is not None:
                piece("pe")

        def back(c, h, piece=None):
            T_stat(c, h)
            T_cb(c, h)
            if piece is not None:
                piece("ev")

        for c in range(16):
            nx = c + 1 < 16
            pc = [None] * 4
            if nx:
                pc[0] = lambda part, c=c: T_mo(c + 1, 0, part)
                pc[1] = lambda part, c=c: T_mo(c + 1, 1, part)
            if c > 0:
                pc[2] = lambda part, c=c: T_out(c - 1, 0, part)
                pc[3] = lambda part, c=c: T_out(c - 1, 1, part)
            T_st(c, 0)
            front(c, 0, pc[0])
            if c > 0:
                T_hn(c - 1, 2)
                T_hn(c - 1, 3)
            back(c, 0, pc[0])
            if nx:
                T_qk(c + 1, 0, 0); T_qk(c + 1, 0, 1)
            front(c, 1, pc[1])
            back(c, 1, pc[1])
            if c > 0:
                T_tr(c - 1, 2)
                T_tr(c - 1, 3)
                if (c - 1) % 2 == 1:
                    T_szp(c // 2, 2)
                    T_szp(c // 2, 3)
            if nx:
                T_qk(c + 1, 1, 0); T_qk(c + 1, 1, 1)
            T_rstd(c, 0)
            front(c, 2, pc[2])
            T_hn(c, 0)
            T_hn(c, 1)
            back(c, 2, pc[2])
            if nx:
                T_k2(c + 1, 0)
            front(c, 3, pc[3])
            back(c, 3, pc[3])
            T_tr(c, 0)
            T_tr(c, 1)
            if c > 0:
                T_fin(c - 1)
            if nx:
                T_k2(c + 1, 1)
            T_rstd(c, 1)
            T_res(c)
            if c % 2 == 1 and nx:
                T_szp((c + 1) // 2, 0)
                T_szp((c + 1) // 2, 1)
        T_hn(15, 2)
        T_hn(15, 3)
        T_tr(15, 2)
        T_tr(15, 3)
        T_out(15, 0)
        T_out(15, 1)
        T_fin(15)
        if DEBUG:
            dbg_xc = nc.dram_tensor("dbg_xc", [128, 8 * 2048], BF16, kind="ExternalOutput").ap()
            dbg_g = nc.dram_tensor("dbg_g", [128, 256], F32, kind="ExternalOutput").ap()
            dbg_v = nc.dram_tensor("dbg_v", [128, 16 * 4 * 258], BF16, kind="ExternalOutput").ap()
            allxc = [t for l_ in XC for t in l_]
            S.dma("sp", "dbg1", lambda e: e.dma_start(out=dbg_xc, in_=xcT[:, :, :].rearrange("p k t -> p (k t)")), reads=allxc, writes=[OUT])
            for i_, arr in enumerate([Ax, A2x, EG, ENB]):
                S.dma("sp", "dbg2", lambda e, i_=i_, arr=arr: e.dma_start(out=dbg_g[:, i_ * 64:(i_ + 1) * 64], in_=arr[:, :, :].rearrange("p a b -> p (a b)")),
                      reads=GATE, writes=[OUT])
            S.dma("sp", "dbg3", lambda e: e.dma_start(out=dbg_v, in_=vext[:, :, :, :].rearrange("p a b c -> p (a b c)")), reads=VX, writes=[OUT])
        S.join("sp", [OUT])
        S.emit(nc)
    return nc


def _block_diag(w):
    m = np.zeros((1024, 128), np.float32)
    for n in range(256):
        ct, nn = divmod(n, 32)
        m[ct * 128 + 4 * nn:ct * 128 + 4 * nn + 4, 4 * nn:4 * nn + 4] = w[n]
    return m


_NC_CACHE = {}


def kernel(x, norm_w, w_in, pool_w, pool_scale, conv_w, conv_b, w_q, w_k, w_v, w_if, b_if,
           mh_norm_w, m_skip, w_out, final_norm_w):
    f = lambda a: np.ascontiguousarray(np.asarray(a, dtype=np.float32))
    x = f(x)
    B, SEQ, _ = x.shape
    nseg = SEQ // T
    assert B * nseg == NCORES
    if "nc" not in _NC_CACHE:
        _NC_CACHE["nc"] = build_nc()
    nc = _NC_CACHE["nc"]

    chv = np.zeros((128, 8, 8), np.float32)
    cw = f(conv_w)[0]
    for j in range(4):
        chv[:, :, j] = cw[j].reshape(8, 128).T
    chv[:, :, 4] = f(conv_b)[0].reshape(8, 128).T
    chv[:, :, 5] = f(pool_scale)[0].reshape(8, 128).T
    chv[:, :, 6] = f(m_skip)[0].reshape(8, 128).T
    chv[:, :, 7] = f(mh_norm_w)[0].reshape(8, 128).T
    cst = np.zeros((128, 4, 128), np.float32)
    cst[:, 0, :] = np.eye(128, dtype=np.float32)
    cst[:, 1, :] = np.triu(np.ones((128, 128), np.float32))
    cst[:, 2, :] = 1.0
    cst[:, 3, :] = np.triu(np.ones((128, 128), np.float32))
    shared = {
        "w_in": f(w_in)[0], "w_out": f(w_out)[0], "pool_w": f(pool_w)[0].reshape(1024, 256),
        "wbd": np.ascontiguousarray(np.stack([_block_diag(f(w_q)[0]), _block_diag(f(w_k)[0]), _block_diag(f(w_v)[0])], axis=0)
                                    .reshape(3, 8, 128, 128).transpose(2, 0, 1, 3).reshape(128, 3 * 8 * 128)),
        "wbdT": np.ascontiguousarray(np.stack([_block_diag(f(w_q)[0]), _block_diag(f(w_k)[0]), _block_diag(f(w_v)[0])], axis=0)
                                     .reshape(3, 8, 128, 128).transpose(3, 0, 1, 2).reshape(128, 3 * 8 * 128)),
        "w_if": np.ascontiguousarray(f(w_if)[0].reshape(24, 128, 8).transpose(1, 0, 2).reshape(128, 192)), "bifb": np.ascontiguousarray(np.broadcast_to(np.tile(f(b_if)[0].reshape(1, 8), (1, 4)), (128, 32))), "chv": chv.reshape(128, 64),
        "normw_b": np.ascontiguousarray(np.broadcast_to(f(norm_w)[0][None, :], (128, D))),
        "fnw_b": np.ascontiguousarray(np.broadcast_to(f(final_norm_w)[None, :], (128, D))),
        "cst": cst.reshape(128, 512),
        "identb_in": np.eye(128, dtype=np.float32).astype(ml_dtypes.bfloat16),
    }
    in_maps = []
    for r in range(NCORES):
        b, j = divmod(r, nseg)
        start = j * T
        xs = np.zeros((TT, D), np.float32)
        if j == 0:
            xs[HALO:] = x[b, 0:T]
        else:
            xs[:] = x[b, start - HALO:start + T]
        invc = np.zeros((4, 16), np.float32)
        for g in range(4):
            win = 2 ** (g + 1)
            for i in range(16):
                invc[g, i] = 1.0 / min(start + i + 1, win)
        brb = np.zeros((4, 4), np.float32)
        cm = np.zeros((4,), np.float32)
        for i in range(4):
            cm[i] = 1.0 if i < j else 0.0
            for l in range(4):
                brb[i, l] = 1.0 if (i < l < j) else 0.0
        m = dict(shared)
        m["x"] = xs
        m["invcnt"] = np.ascontiguousarray(np.broadcast_to(invc.reshape(1, 64), (128, 64)))
        m["brb"] = np.ascontiguousarray(np.broadcast_to(brb.reshape(1, 16), (128, 16)))
        m["cmask"] = np.ascontiguousarray(np.broadcast_to(cm.reshape(1, 4), (128, 4)))
        in_maps.append(m)
    res = run_bass_kernel_spmd(nc, in_maps, core_ids=list(range(NCORES)))
    if DEBUG:
        _NC_CACHE["res"] = res
    out = np.zeros((B, SEQ, D), np.float32)
    for r in range(NCORES):
        b, j = divmod(r, nseg)
        out[b, j * T:(j + 1) * T] = res.results[r]["out"]
    return out
```
